# Optimizing a Trainium2 kernel written in Bass

```python
import jax, jax.numpy as jnp
from jax import lax
import numpy as np

D_MODEL = 1024
BATCH = 16
SEQ = 2048
DEPTH = 1

CHUNK = 64
SC_W = D_MODEL
SC_K = 3
RG_W = 1280
RG_HEADS = 16
RG_HEAD_DIM = RG_W // RG_HEADS
RG_K = 4
LRU_C = 8.0
EPS = 1e-6
P_IN = 4 * SC_W + 2 * RG_W + 2 * D_MODEL

kernel_name = "hybrid_shortconv_rglru_gated_merge_block"


def _rmsnorm(x, g):
    xf = x.astype(jnp.float32)
    y = xf * lax.rsqrt(jnp.mean(xf * xf, axis=-1, keepdims=True) + EPS)
    return (y * g.astype(jnp.float32)).astype(x.dtype)


def _causal_dwconv(u, w, b):
    k_taps = w.shape[0]
    s = u.shape[1]
    up = jnp.pad(u, ((0, 0), (k_taps - 1, 0), (0, 0)))
    y = b + up[:, 0:s] * w[0]
    for k in range(1, k_taps):
        y = y + up[:, k:k + s] * w[k]
    return y


def _lin_combine(e1, e2):
    a1, b1 = e1
    a2, b2 = e2
    return a1 * a2, a2 * b1 + b2


def _rg_lru(v, w_a, b_a, w_x, b_x, lam):
    bn, s, w = v.shape
    vh = v.reshape(bn, s, RG_HEADS, RG_HEAD_DIM)
    r = jax.nn.sigmoid(jnp.einsum('bshd,hde->bshe', vh, w_a).reshape(bn, s, w) + b_a)
    i = jax.nn.sigmoid(jnp.einsum('bshd,hde->bshe', vh, w_x).reshape(bn, s, w) + b_x)
    log_a = -LRU_C * r.astype(jnp.float32) * jax.nn.softplus(-lam.astype(jnp.float32))
    a = jnp.exp(log_a)
    bterm = jnp.sqrt(-jnp.expm1(2.0 * log_a)) * (i * v).astype(jnp.float32)
    nc = s // CHUNK
    a = a.reshape(bn, nc, CHUNK, w)
    bterm = bterm.reshape(bn, nc, CHUNK, w)
    cum_a, h_loc = lax.associative_scan(_lin_combine, (a, bterm), axis=2)

    def step(h, inp):
        ca, hl = inp
        hc = hl + ca * h[:, None, :]
        return hc[:, -1], hc

    h0 = jnp.zeros((bn, w), jnp.float32)
    _, hs = lax.scan(step, h0, (cum_a.transpose(1, 0, 2, 3), h_loc.transpose(1, 0, 2, 3)))
    return hs.transpose(1, 0, 2, 3).reshape(bn, s, w).astype(v.dtype)


def setup_inputs(seed: int = 0) -> dict:
    key = jax.random.key(seed)
    ks = jax.random.split(key, 24)
    f32 = jnp.float32
    L, D = DEPTH, D_MODEL
    nrm = lambda k, shp, sc: jax.random.normal(k, shp, f32) * sc
    u = jax.random.uniform(ks[15], (L, RG_W), f32, 0.9, 0.999)
    s_l = u ** (1.0 / LRU_C)
    rg_lambda = jnp.log(s_l) - jnp.log1p(-s_l)
    return {
        "x": nrm(ks[0], (BATCH, SEQ, D), 1.0),
        "c": nrm(ks[1], (BATCH, D), 1.0),
        "w_ada": nrm(ks[2], (L, D, 3 * D), 0.5 * D ** -0.5),
        "b_ada": nrm(ks[3], (L, 3 * D), 0.02),
        "g_norm": 1.0 + nrm(ks[4], (L, D), 0.02),
        "w_in": nrm(ks[5], (L, D, P_IN), D ** -0.5),
        "sc_conv_w": nrm(ks[6], (L, SC_K, SC_W), SC_K ** -0.5),
        "sc_conv_b": nrm(ks[7], (L, SC_W), 0.02),
        "sc_w_out": nrm(ks[8], (L, SC_W, D), SC_W ** -0.5),
        "rg_conv_w": nrm(ks[9], (L, RG_K, RG_W), RG_K ** -0.5),
        "rg_conv_b": nrm(ks[10], (L, RG_W), 0.02),
        "rg_w_a": nrm(ks[11], (L, RG_HEADS, RG_HEAD_DIM, RG_HEAD_DIM), RG_HEAD_DIM ** -0.5),
        "rg_b_a": nrm(ks[12], (L, RG_W), 0.02),
        "rg_w_x": nrm(ks[13], (L, RG_HEADS, RG_HEAD_DIM, RG_HEAD_DIM), RG_HEAD_DIM ** -0.5),
        "rg_b_x": nrm(ks[14], (L, RG_W), 0.02),
        "rg_lambda": rg_lambda,
        "rg_w_out": nrm(ks[16], (L, RG_W, D), RG_W ** -0.5),
        "b_merge": nrm(ks[17], (L, 2, D), 0.02),
        "w_out": nrm(ks[18], (L, D, D), D ** -0.5),
        "g_final": 1.0 + nrm(ks[19], (D,), 0.02),
    }


def reference(x, c, w_ada, b_ada, g_norm, w_in, sc_conv_w, sc_conv_b, sc_w_out,
              rg_conv_w, rg_conv_b, rg_w_a, rg_b_a, rg_w_x, rg_b_x, rg_lambda,
              rg_w_out, b_merge, w_out, g_final):
    sizes = [SC_W] * 4 + [RG_W] * 2 + [D_MODEL] * 2
    split_idx = [int(v) for v in np.cumsum(sizes)[:-1]]
    c_act = jax.nn.silu(c)
    for l in range(DEPTH):
        mod = c_act @ w_ada[l] + b_ada[l]
        shift, scale, gate = jnp.split(mod, 3, axis=-1)
        h = _rmsnorm(x, g_norm[l]) * (1.0 + scale[:, None, :]) + shift[:, None, :]
        z = h @ w_in[l]
        sc_b, sc_c, sc_v, sc_g, rg_v, rg_g, m_a, m_b = jnp.split(z, split_idx, axis=-1)
        u = _causal_dwconv(sc_c * sc_v, sc_conv_w[l], sc_conv_b[l])
        y_a = (sc_b * u * jax.nn.silu(sc_g)) @ sc_w_out[l]
        v = _causal_dwconv(rg_v, rg_conv_w[l], rg_conv_b[l])
        y_rg = _rg_lru(v, rg_w_a[l], rg_b_a[l], rg_w_x[l], rg_b_x[l], rg_lambda[l])
        y_b = (y_rg * jax.nn.silu(rg_g)) @ rg_w_out[l]
        g_a = jax.nn.sigmoid(m_a + b_merge[l, 0])
        g_b = jax.nn.sigmoid(m_b + b_merge[l, 1])
        merged = g_a * y_a + g_b * y_b
        x = x + gate[:, None, :] * (merged @ w_out[l])
    return _rmsnorm(x, g_final)
```

```python
import numpy as np
import concourse.bass as bass
import concourse.mybir as mybir
from concourse.bass_utils import run_bass_kernel_spmd

F32 = mybir.dt.float32
BF16 = mybir.dt.bfloat16
AF = mybir.ActivationFunctionType
ALU = mybir.AluOpType

ENGS = ("pe", "act", "dve", "pool", "sp")
NCORES = 8
D = 1024
SEQ = 2048
TT = 512
NT = 8
RGW = 1280
NRC = 10
HD = 80
EPS = 1e-6
NSLOT = 4
PAD = 16


class Buf:
    __slots__ = ("name", "w", "r", "dsem", "dcnt")

    def __init__(self, name):
        self.name = name
        self.w = None
        self.r = []
        self.dsem = None
        self.dcnt = 0


class Sched:
    def __init__(self, nc):
        self.nc = nc
        self.streams = {e: [] for e in ENGS}
        self.semobj = {}
        for e in ENGS:
            self.semobj["s_" + e] = nc.alloc_semaphore("s_" + e)
        self.tick = {e: 0 for e in ENGS}
        self.seen = {e: {} for e in ENGS}
        self.nbuf = 0

    def buf(self, name=None):
        self.nbuf += 1
        return Buf(f"{name or 'b'}_{self.nbuf}")

    def _need(self, e, tok, waits):
        if tok is None:
            return
        semkey, val, _ = tok
        if self.seen[e].get(semkey, 0) >= val:
            return
        if val > waits.get(semkey, 0):
            waits[semkey] = val

    def _deps(self, e, reads, writes):
        waits = {}
        for b in reads:
            if b.w is not None:
                self._need(e, b.w, waits)
        for b in writes:
            if b.w is not None and b.w[2] != e:
                self._need(e, b.w, waits)
            for t in b.r:
                if t[2] != e:
                    self._need(e, t, waits)
        for k, v in waits.items():
            self.seen[e][k] = v
            self.streams[e].append(("wait", k, v))

    def op(self, e, fn, reads=(), writes=()):
        for b in writes:
            if b.name.startswith("bank") and e == "pe" and b.w is not None and b.w[2] == "pe" and not b.r and b not in reads:
                raise AssertionError(f"PSUM {b.name} re-allocated before its consumers were emitted")
        self._deps(e, reads, writes)
        self.tick[e] += 1
        tok = ("s_" + e, self.tick[e], e)
        self.streams[e].append(("op", fn, "s_" + e, 1))
        for b in reads:
            b.r.append(tok)
        for b in writes:
            b.w = tok
            b.r = []
        return tok

    def dma(self, e, fn, reads=(), writes=(), track=None):
        self._deps(e, reads, writes)
        tb = track or (writes[0] if writes else reads[0])
        if tb.dsem is None:
            tb.dsem, tb.dcnt = {}, {}
        if e not in tb.dsem:
            tb.dsem[e] = f"d_{tb.name}_{e}"
            tb.dcnt[e] = 0
            self.semobj[tb.dsem[e]] = self.nc.alloc_semaphore(tb.dsem[e])
        tb.dcnt[e] += 16
        tok = (tb.dsem[e], tb.dcnt[e], "dma")
        self.streams[e].append(("op", fn, tb.dsem[e], 16))
        for b in reads:
            b.r.append(tok)
        for b in writes:
            b.w = tok
            b.r = []
        return tok

    def wait_all(self, e, toks):
        waits = {}
        for t in toks:
            self._need(e, t, waits)
        for k, v in waits.items():
            self.seen[e][k] = v
            self.streams[e].append(("wait", k, v))

    def emit(self):
        sched = self

        def run(e):
            def body(engine):
                for item in sched.streams[e]:
                    if item[0] == "wait":
                        engine.wait_ge(sched.semobj[item[1]], item[2])
                    else:
                        _, fn, semkey, inc = item
                        fn(engine).then_inc(sched.semobj[semkey], inc)
            return body

        with self.nc.Block() as block:
            block.tensor(run("pe"))
            block.scalar(run("act"))
            block.vector(run("dve"))
            block.gpsimd(run("pool"))
            block.sync(run("sp"))


OFF_B, OFF_C, OFF_V, OFF_G, OFF_RV, OFF_RG, OFF_MA, OFF_MB = 0, 1024, 2048, 3072, 4096, 5376, 6656, 7680


def piece_columns():
    pcs = []
    for j in range(8):
        pcs.append([OFF_C + 128 * j, OFF_V + 128 * j, OFF_G + 128 * j, OFF_B + 128 * j])
    for q in range(5):
        pcs.append([OFF_RV + 128 * (2 * q), OFF_RG + 128 * (2 * q),
                    OFF_RV + 128 * (2 * q + 1), OFF_RG + 128 * (2 * q + 1)])
    for q in range(4):
        pcs.append([OFF_MA + 128 * (2 * q), OFF_MB + 128 * (2 * q),
                    OFF_MA + 128 * (2 * q + 1), OFF_MB + 128 * (2 * q + 1)])
    return pcs


P_A0, P_R0, P_M0, P_SC0, P_RG0, NPIECE = 0, 8, 13, 17, 19, 21
P_GT0, NSCR = 21, 23


def gate_pairs():
    pairs = []
    for j in range(NRC):
        heads = set(range((128 * j) // HD, (128 * j + 127) // HD + 1))
        ins = set()
        for h in heads:
            for d in range(HD * h, HD * h + HD):
                ins.add(d // 128)
        for i in sorted(ins):
            pairs.append((j, i))
    return pairs


GPAIRS = gate_pairs()
NGP = len(GPAIRS)

C_SCW, C_SCB, C_RGW, C_RGB, C_RBA, C_RBX, C_LAM, C_BM, C_GN, C_BADA = 0, 24, 32, 72, 82, 92, 102, 112, 128, 136
NCST = 168


def tile_seq():
    s = []
    for j in range(8):
        s.append(P_A0 + j)
        if j % 2 == 0:
            s.append(P_R0 + j // 2)
    s.append(P_R0 + 4)
    s += [P_M0, P_SC0, P_M0 + 1, P_M0 + 2, P_SC0 + 1, P_RG0, P_M0 + 3, P_RG0 + 1]
    return s


def build_nc():
    nc = bass.Bass("TRN2", target_bir_lowering=False)
    x_d = nc.dram_tensor("x", [NT * TT, D], F32, kind="ExternalInput").ap()
    wall_d = nc.dram_tensor("wall", [NPIECE, 128, 4, 1024 + PAD], F32, kind="ExternalInput").ap()
    wada_d = nc.dram_tensor("wada", [6, 128, 4, 1024 + PAD], F32, kind="ExternalInput").ap()
    rgc_d = nc.dram_tensor("rgc", [128, 2, 1024 + PAD], F32, kind="ExternalInput").ap()
    wout_d = nc.dram_tensor("wout", [128, 8, 1024 + PAD], F32, kind="ExternalInput").ap()
    wg_d = nc.dram_tensor("wg", [128, NGP // 2, 512 + PAD], F32, kind="ExternalInput").ap()
    cst_d = nc.dram_tensor("cst", [128, NCST], F32, kind="ExternalInput").ap()
    rows_d = nc.dram_tensor("rows", [128, 2048], F32, kind="ExternalInput").ap()
    ct_d = nc.dram_tensor("ct", [128, 16], F32, kind="ExternalInput").ap()
    out_d = nc.dram_tensor("out", [NT * TT, D], F32, kind="ExternalOutput").ap()
    wsc_d = nc.dram_tensor("wsc", [NSCR, 128, 4096], BF16, kind="Internal").ap()

    S = Sched(nc)
    cnt = [0]

    def sb(shape, dtype, name):
        cnt[0] += 1
        return nc.alloc_sbuf_tensor(f"{name}_{cnt[0]}", list(shape), dtype)

    ring = sb([128, NSLOT, 4096], BF16, "ring")
    ringB = [S.buf(f"ring{k}") for k in range(NSLOT)]
    rgc = sb([128, 2, 1024], BF16, "rgc"); rgcB = S.buf("rgc")
    wout = sb([128, 8, 1024], BF16, "wout"); woutB = S.buf("wout")
    wg = sb([128, 2, NGP, 128], BF16, "wg"); wgB = S.buf("wg")
    hT = [sb([128, 8, TT], BF16, f"hT{k}") for k in range(2)]
    hTB = [S.buf(f"hT{k}") for k in range(2)]
    xn = sb([128, 4, D], BF16, "xn"); xnB = [S.buf(f"xn{s}") for s in range(4)]
    NXB = 3
    xbuf = [sb([128, D], F32, f"xb{k}") for k in range(NXB)]; xbufB = [S.buf(f"xb{k}") for k in range(NXB)]
    t1 = [sb([128, D], F32, f"t1_{k}") for k in range(2)]; t1B = [S.buf(f"t1_{k}") for k in range(2)]
    gatebc = sb([128, 2, D], F32, "gatebc"); gatebcB = S.buf("gatebc")
    gfin = sb([128, D], F32, "gfin"); gfinB = S.buf("gfin")
    NA = 2
    pbuf = [sb([128, 2 + TT], F32, f"pbuf{k}") for k in range(NA)]; pbufB = [S.buf(f"pbuf{k}") for k in range(NA)]
    ubuf = [sb([128, TT], F32, f"ubuf{k}") for k in range(NA)]; ubufB = [S.buf(f"ubuf{k}") for k in range(NA)]
    gbuf = [sb([128, TT], F32, f"gbuf{k}") for k in range(NA)]; gbufB = [S.buf(f"gbuf{k}") for k in range(NA)]
    pa = sb([128, 8, TT], BF16, "pa"); paB = [S.buf(f"pa{j}") for j in range(8)]
    prg = sb([128, NRC, TT], BF16, "prg"); prgB = [S.buf(f"prg{j}") for j in range(NRC)]
    rbuf = [sb([128, 3 + TT], F32, f"rbuf{k}") for k in range(2)]; rbufB = [S.buf(f"rbuf{k}") for k in range(2)]
    vbuf = [sb([128, TT], F32, f"vbuf{k}") for k in range(4)]; vbufB = [S.buf(f"vbuf{k}") for k in range(4)]
    vb = [sb([128, TT], BF16, f"vb{k}") for k in range(4)]; vbB = [S.buf(f"vb{k}") for k in range(4)]
    sgb = [sb([128, TT], F32, f"sgb{k}") for k in range(4)]; sgbB = [S.buf(f"sgb{k}") for k in range(4)]
    trb = [sb([128, TT], F32, f"trb{k}") for k in range(2)]; trbB = [S.buf(f"trb{k}") for k in range(2)]
    tib = [sb([128, TT], F32, f"tib{k}") for k in range(2)]; tibB = [S.buf(f"tib{k}") for k in range(2)]
    ab = [sb([128, TT], F32, f"ab{k}") for k in range(2)]; abB = [S.buf(f"ab{k}") for k in range(2)]
    gab = [sb([128, TT], F32, f"gab{k}") for k in range(2)]; gabB = [S.buf(f"gab{k}") for k in range(2)]
    gbb = [sb([128, TT], F32, f"gbb{k}") for k in range(2)]; gbbB = [S.buf(f"gbb{k}") for k in range(2)]
    merged = sb([128, 8, TT], BF16, "merged"); mergedB = [S.buf(f"mg{j}") for j in range(8)]
    ident = sb([128, 128], BF16, "ident"); identf = sb([128, 128], F32, "identf"); identB = S.buf("ident")
    ones = sb([128, 128], F32, "ones")
    mhalf = sb([128, 1], F32, "mhalf"); constB = S.buf("const")
    cst = sb([128, NCST], F32, "cst"); cstB = S.buf("cst")
    ctt = sb([128, 16], F32, "ctt"); cttB = S.buf("ctt")
    prm = sb([128, 128], F32, "prm"); prmB = S.buf("prm")
    Q_SCW, Q_SCB, Q_RBA, Q_RBX, Q_CL, Q_HCL, Q_BM, Q_G, Q_SH, Q_C16 = 0, 24, 32, 42, 52, 62, 72, 88, 104, 120
    slb = sb([128, 8, 2], BF16, "slb"); slB = S.buf("sl")

    def slrep(b, kc):
        return merged[:, b * 2 + kc // 4, (kc % 4) * 128:(kc % 4 + 1) * 128]
    ss = sb([128, 32], F32, "ss")
    haloA = sb([128, 8, 2], F32, "haloA"); haloAB = [S.buf(f"hA{j}") for j in range(8)]
    haloR = sb([128, NRC, 3], F32, "haloR"); haloRB = [S.buf(f"hR{j}") for j in range(NRC)]
    hst = sb([128, NRC], F32, "hst"); hstB = [S.buf(f"hs{j}") for j in range(NRC)]
    stat = sb([128, 32], F32, "stat"); statB = [S.buf(f"st{k}") for k in range(16)]

    NBK = 7
    banks = [nc.alloc_psum_tensor(f"bank{k}", [128, TT], F32) for k in range(NBK)]
    bankB = [S.buf(f"bank{k}") for k in range(NBK)]
    pst = nc.alloc_psum_tensor("pst", [128, 2, TT], BF16); pstB = S.buf("pst")
    bctr = [0]

    def nb():
        k = bctr[0] % NBK
        bctr[0] += 1
        return banks[k], bankB[k]

    wscB = [S.buf(f"wsc{p}") for p in range(NSCR)]

    seq = [("ada", k) for k in range(4)]
    def tile0_seq():
        ts = tile_seq()
        k = ts.index(P_M0)
        return ts[:k] + [P_GT0, P_GT0 + 1] + ts[k:]
    for i in range(NT):
        seq += [("w", p) for p in (tile0_seq() if i == 0 else tile_seq())]
    ring_state = {"next": 0, "released": set(), "where": {}}

    def pump():
        while ring_state["next"] < len(seq):
            n = ring_state["next"]
            if n >= NSLOT and (n - NSLOT) not in ring_state["released"]:
                break
            kind, p = seq[n]
            slot = n % NSLOT
            if kind == "ada":
                S.dma("pool", (lambda e, slot=slot, p=p: e.dma_start(out=ring[:, slot, :].rearrange("p (a n) -> p a n", a=4),
                                                                      in_=wada_d[p][:, :, 0:1024])),
                      writes=[ringB[slot]])
            else:
                S.dma("sp", (lambda e, slot=slot, p=p: e.dma_start(out=ring[:, slot, :], in_=wsc_d[p])),
                      reads=[wscB[p]], writes=[ringB[slot]])
            ring_state["where"][(kind, p)] = (slot, n)
            ring_state["next"] += 1

    def get(kind, p):
        key = (kind, p)
        assert key in ring_state["where"], f"piece {key} not loaded (ring order bug)"
        slot, n = ring_state["where"][key]
        return slot, ringB[slot]

    def release(kind, p):
        slot, n = ring_state["where"].pop((kind, p))
        ring_state["released"].add(n)
        pump()

    def prologue_a():
        S.dma("sp", lambda e: e.dma_start(out=cst[:], in_=cst_d), writes=[cstB])
        S.dma("sp", lambda e: e.dma_start(out=ctt[:], in_=ct_d), writes=[cttB])
        S.dma("sp", lambda e: e.dma_start(out=gfin[:], in_=rows_d[:, 0:1024]), writes=[gfinB])
        S.dma("sp", lambda e: e.dma_start(out=t1[0][:], in_=rows_d[:, 1024:2048]), writes=[t1B[0]])
        S.op("pool", lambda e: e.memset(identf[:], 0.0), writes=[identB])
        S.op("pool", lambda e: e.affine_select(out=identf[:], in_=identf[:], pattern=[[-1, 128]], compare_op=ALU.not_equal,
                                               fill=1.0, base=0, channel_multiplier=1), reads=[identB], writes=[identB])
        S.op("pool", lambda e: e.tensor_copy(out=ident[:], in_=identf[:]), reads=[identB], writes=[identB])

        def mk_const(e):
            e.memset(ones[:], 1.0)
            e.memset(mhalf[:], -0.5)
            return e.memset(stat[:], 0.0)
        S.op("pool", mk_const, writes=[constB] + statB)


    def prologue_b():
        order = tile0_seq()

        def cast_piece(p):
            src = wall_d[p] if p < NPIECE else wada_d[4 + p - P_GT0]
            S.dma("pool", (lambda e, p=p, src=src: e.dma_start(out=wsc_d[p].rearrange("p (a n) -> p a n", a=4), in_=src[:, :, 0:1024])),
                  writes=[wscB[p]])
        for k, p in enumerate(order):
            if k == 3:
                S.dma("pool", lambda e: e.dma_start(out=wg[:].rearrange("p a g e -> p (a g e)").rearrange("p (a n) -> p a n", n=512),
                                                    in_=wg_d[:, :, 0:512]), writes=[wgB])
            if p == P_RG0:
                S.dma("pool", lambda e: e.dma_start(out=rgc[:], in_=rgc_d[:, :, 0:1024]), writes=[rgcB])
            cast_piece(p)
        S.dma("pool", lambda e: e.dma_start(out=wout[:], in_=wout_d[:, :, 0:1024]), writes=[woutB])

        def mk_prm1(e):
            e.tensor_scalar(out=prm[:, Q_SCW:Q_SCW + 32], in0=cst[:, C_SCW:C_SCW + 32], scalar1=0.5, scalar2=None, op0=ALU.mult)
            e.tensor_scalar(out=prm[:, Q_RBA:Q_RBA + 20], in0=cst[:, C_RBA:C_RBA + 20], scalar1=0.5, scalar2=None, op0=ALU.mult)
            return e.tensor_scalar(out=prm[:, Q_BM:Q_BM + 16], in0=cst[:, C_BM:C_BM + 16], scalar1=0.5, scalar2=None, op0=ALU.mult)
        S.op("dve", mk_prm1, reads=[cstB], writes=[prmB])
        S.op("dve", lambda e: e.memset(prm[:, Q_C16:Q_C16 + 1], 1.0 / 16), writes=[prmB])
        tmpc = sb([128, 40], F32, "tmpc"); tmpcB = S.buf("tmpc")
        S.op("act", lambda e: e.activation(out=tmpc[:, 0:10], in_=cst[:, C_LAM:C_LAM + 10], func=AF.Abs),
             reads=[cstB], writes=[tmpcB])
        S.op("act", lambda e: e.activation(out=tmpc[:, 10:20], in_=tmpc[:, 0:10], func=AF.Exp, scale=-1.0),
             reads=[tmpcB], writes=[tmpcB])
        S.op("act", lambda e: e.activation(out=tmpc[:, 20:30], in_=tmpc[:, 10:20], func=AF.Ln, bias=1.0),
             reads=[tmpcB], writes=[tmpcB])

        S.op("dve", lambda e: e.tensor_scalar(out=tmpc[:, 30:40], in0=cst[:, C_LAM:C_LAM + 10], scalar1=-1.0, scalar2=0.0,
                                              op0=ALU.mult, op1=ALU.max), reads=[cstB, tmpcB], writes=[tmpcB])
        S.op("dve", lambda e: e.tensor_tensor(out=tmpc[:, 30:40], in0=tmpc[:, 30:40], in1=tmpc[:, 20:30], op=ALU.add),
             reads=[tmpcB], writes=[tmpcB])

        def mk_cl2(e):
            e.tensor_scalar(out=prm[:, Q_CL:Q_CL + 10], in0=tmpc[:, 30:40], scalar1=-8.0, scalar2=None, op0=ALU.mult)
            return e.tensor_scalar(out=prm[:, Q_HCL:Q_HCL + 10], in0=tmpc[:, 30:40], scalar1=-4.0, scalar2=None, op0=ALU.mult)
        S.op("dve", mk_cl2, reads=[tmpcB], writes=[prmB])

        slt = sb([128, 16], F32, "slt")
        S.op("act", lambda e: e.activation(out=slt[:], in_=ctt[:], func=AF.Tanh, scale=0.5), reads=[cttB], writes=[slB])
        S.op("dve", lambda e: e.scalar_tensor_tensor(out=slt[:], in0=slt[:], scalar=1.0, in1=ctt[:], op0=ALU.add, op1=ALU.mult),
             reads=[slB, cttB], writes=[slB])
        S.op("dve", lambda e: e.tensor_scalar(out=slt[:], in0=slt[:], scalar1=0.5, scalar2=None, op0=ALU.mult), reads=[slB], writes=[slB])
        S.op("dve", lambda e: e.tensor_copy(out=slb[:].rearrange("p k b -> p (k b)"), in_=slt[:]), reads=[slB], writes=[slB])

        def mk_slrep(e):
            for b in range(2):
                for kc in range(8):
                    ins = e.tensor_scalar(out=slrep(b, kc), in0=ones[:], scalar1=slt[:, kc * 2 + b:kc * 2 + b + 1],
                                          scalar2=None, op0=ALU.mult)
            return ins
        S.op("dve", mk_slrep, reads=[slB, constB], writes=[slB] + mergedB)

        bk_ss, bk_ssB = nb()

        def mm_ss(e, pc, slot):
            for mm in range(4):
                m = pc * 4 + mm
                for kc in range(8):
                    ins = e.matmul(bk_ss[:, m * 2:m * 2 + 2], lhsT=ring[:, slot, kc * 512 + mm * 128:kc * 512 + mm * 128 + 128],
                                   rhs=slb[:, kc, :], start=(kc == 0), stop=(kc == 7))
            return ins
        for pc in range(4):
            slot, rb = get("ada", pc)
            S.op("pe", (lambda e, pc=pc, slot=slot: mm_ss(e, pc, slot)), reads=[rb, slB] + ([bk_ssB] if pc else []), writes=[bk_ssB])
            release("ada", pc)
        S.op("dve", lambda e: e.tensor_tensor(out=ss[:], in0=bk_ss[:, 0:32], in1=cst[:, C_BADA:C_BADA + 32], op=ALU.add),
             reads=[bk_ssB, cstB], writes=[prmB])

        def mk_gsh(e):
            for b in range(2):
                e.scalar_tensor_tensor(out=prm[:, Q_G + 8 * b:Q_G + 8 * b + 8], in0=ss[:, 16 + b:32:2], scalar=1.0,
                                       in1=cst[:, C_GN:C_GN + 8], op0=ALU.add, op1=ALU.mult)
                ins = e.tensor_copy(out=prm[:, Q_SH + 8 * b:Q_SH + 8 * b + 8], in_=ss[:, b:16:2])
            return ins
        S.op("dve", mk_gsh, reads=[prmB, cstB], writes=[prmB])

    def gate_bc():
        for pc in range(2):
            slot, rb = get("w", P_GT0 + pc)
            for b in range(2):
                bk, bkB = nb()

                def mm_gate(e, slot=slot, b=b, bk=bk):
                    for kc in range(8):
                        ins = e.matmul(bk[:, :], lhsT=slrep(b, kc), rhs=ring[:, slot, kc * 512:(kc + 1) * 512],
                                       start=(kc == 0), stop=(kc == 7))
                    return ins
                S.op("pe", mm_gate, reads=[slB, rb] + mergedB, writes=[bkB])
                S.op("dve", (lambda e, bk=bk, b=b, pc=pc: e.tensor_tensor(out=gatebc[:, b, pc * 512:(pc + 1) * 512], in0=bk[:, :],
                                                                           in1=t1[0][:, pc * 512:(pc + 1) * 512], op=ALU.add)),
                     reads=[bkB, t1B[0]], writes=[gatebcB])
            release("w", P_GT0 + pc)
        S.op("dve", lambda e: e.tensor_scalar(out=gatebc[:].rearrange("p b n -> p (b n)"), in0=gatebc[:].rearrange("p b n -> p (b n)"),
                                              scalar1=0.5, scalar2=None, op0=ALU.mult), reads=[gatebcB], writes=[gatebcB, t1B[0]])

    xb_ctr = [0]
    st_ctr = [0]

    def next_xb():
        k = xb_ctr[0] % NXB
        xb_ctr[0] += 1
        return xbuf[k], xbufB[k]

    def next_stat():
        k = st_ctr[0] % 16
        st_ctr[0] += 1
        return k, statB[k]

    def rstd_from(col, colB):
        S.op("dve", lambda e: e.tensor_scalar(out=stat[:, col:col + 1], in0=stat[:, col:col + 1], scalar1=1.0 / D, scalar2=EPS,
                                              op0=ALU.mult, op1=ALU.add), reads=[colB], writes=[colB])
        S.op("pool", lambda e: e.tensor_tensor(out=stat[:, col:col + 1], in0=stat[:, col:col + 1], in1=mhalf[:], op=ALU.pow),
             reads=[colB, constB], writes=[colB])

    xb_gen = [0] * NXB
    pending_norm = []

    def flush_norm():
        while pending_norm:
            xb, xbB, k, gen, col, colB, s_, on_act = pending_norm.pop(0)
            assert xb_gen[k] == gen, "xbuf slot reused before deferred normalize"
            if on_act:
                S.op("act", (lambda e, xb=xb, col=col, s_=s_: e.activation(out=xn[:, s_, :], in_=xb[:], func=AF.Copy,
                                                                             scale=stat[:, col:col + 1])),
                     reads=[xbB, colB], writes=[xnB[s_]])
                continue
            S.op("dve", (lambda e, xb=xb, col=col, s_=s_: e.tensor_scalar(out=xn[:, s_, :], in0=xb[:], scalar1=stat[:, col:col + 1],
                                                                            scalar2=None, op0=ALU.mult)),
                 reads=[xbB, colB], writes=[xnB[s_]])

    def sumsq(src, srcB, col, colB, jk, jkB, on_act=False):
        S.op("dve", lambda e: e.memset(stat[:, col:col + 1], 0.0), writes=[colB])
        if on_act:
            S.op("act", lambda e: e.activation(out=jk, in_=src[:], func=AF.Square, accum_out=stat[:, col:col + 1]),
                 reads=[srcB, colB], writes=[jkB, colB])
            return
        S.op("dve", lambda e: e.scalar_tensor_tensor(out=jk, in0=src[:], scalar=1.0, in1=src[:], op0=ALU.mult, op1=ALU.mult,
                                                     accum_out=stat[:, col:col + 1]), reads=[srcB, colB], writes=[jkB, colB])

    pl_state = {}

    def P_load(i, s):
        k = xb_ctr[0] % NXB
        xb, xbB = next_xb()
        xb_gen[k] += 1
        r0 = i * TT + s * 128
        S.dma("sp", lambda e: e.dma_start(out=xb[:], in_=x_d[r0:r0 + 128, :]), writes=[xbB])
        pl_state[(i, s)] = (xb, xbB, k, xb_gen[k])

    def P_rest(i, s):
        xb, xbB, k, gen = pl_state.pop((i, s))
        assert xb_gen[k] == gen
        col, colB = next_stat()
        sumsq(xb, xbB, col, colB, xn[:, s, :], xnB[s], on_act=(i > 0))
        rstd_from(col, colB)
        flush_norm()
        pending_norm.append((xb, xbB, k, gen, col, colB, s, i > 0))

    def prep_norm(i, s):
        P_load(i, s)
        P_rest(i, s)

    def prep_tr(i, q):
        b = i // 4
        h = i % 2

        def tr(e):
            for kl in range(2):
                kc = 2 * q + kl
                for s in range(4):
                    ins = e.transpose(out=pst[:, kl, s * 128:(s + 1) * 128], in_=xn[:, s, kc * 128:(kc + 1) * 128], identity=ident[:])
            return ins
        S.op("pe", tr, reads=xnB + [identB], writes=[pstB])
        for kl in range(2):
            kc = 2 * q + kl
            S.op("act", (lambda e, kl=kl, kc=kc: e.activation(out=hT[h][:, kc, :], in_=pst[:, kl, :], func=AF.Identity,
                                                             scale=prm[:, Q_G + 8 * b + kc:Q_G + 8 * b + kc + 1],
                                                             bias=prm[:, Q_SH + 8 * b + kc:Q_SH + 8 * b + kc + 1])),
                 reads=[pstB, prmB], writes=[hTB[h]])

    def zmm(i, pid, ci, bk, bkB):
        slot, rb = get("w", pid)
        h = i % 2

        def f(e):
            for kc in range(8):
                ins = e.matmul(bk[:, :], lhsT=ring[:, slot, kc * 512 + ci * 128:kc * 512 + ci * 128 + 128], rhs=hT[h][:, kc, :],
                               start=(kc == 0), stop=(kc == 7))
            return ins
        S.op("pe", f, reads=[rb, hTB[h]], writes=[bkB])

    a_ctr = [0]

    def A_gen(i, j):
        pid = P_A0 + j
        k = a_ctr[0] % NA
        a_ctr[0] += 1
        pb, pbB, ub, ubB, gb, gbB = pbuf[k], pbufB[k], ubuf[k], ubufB[k], gbuf[k], gbufB[k]
        w0 = prm[:, Q_SCW + 3 * j + 0:Q_SCW + 3 * j + 1]
        w1 = prm[:, Q_SCW + 3 * j + 1:Q_SCW + 3 * j + 2]
        w2 = prm[:, Q_SCW + 3 * j + 2:Q_SCW + 3 * j + 3]
        bia = prm[:, Q_SCB + j:Q_SCB + j + 1]
        bc, bcB = nb(); zmm(i, pid, 0, bc, bcB)
        bv, bvB = nb(); zmm(i, pid, 1, bv, bvB)
        bg, bgB = nb(); zmm(i, pid, 2, bg, bgB)
        yield
        S.op("act", lambda e: e.activation(out=pb[:, 2:2 + TT], in_=bc[:, :], func=AF.Copy), reads=[bcB], writes=[pbB])
        S.op("dve", lambda e: e.tensor_copy(out=pb[:, 0:2], in_=haloA[:, j, :]), reads=[haloAB[j]], writes=[pbB])
        S.op("dve", lambda e: e.tensor_tensor(out=pb[:, 2:2 + TT], in0=bv[:, :], in1=pb[:, 2:2 + TT], op=ALU.mult),
             reads=[bvB, pbB], writes=[pbB])
        S.op("dve", lambda e: e.tensor_copy(out=haloA[:, j, :], in_=pb[:, TT:TT + 2]), reads=[pbB], writes=[haloAB[j]])
        S.op("act", lambda e: e.activation(out=ub[:], in_=pb[:, 2:2 + TT], func=AF.Identity, scale=w2, bias=bia),
             reads=[pbB, prmB], writes=[ubB])
        yield
        bb, bbB = nb(); zmm(i, pid, 3, bb, bbB)
        release("w", pid)
        yield
        S.op("act", lambda e: e.activation(out=gb[:], in_=bg[:, :], func=AF.Tanh, scale=0.5), reads=[bgB], writes=[gbB])
        S.op("dve", lambda e: e.scalar_tensor_tensor(out=ub[:], in0=pb[:, 1:1 + TT], scalar=w1, in1=ub[:], op0=ALU.mult, op1=ALU.add),
             reads=[pbB, ubB, prmB], writes=[ubB])
        S.op("dve", lambda e: e.scalar_tensor_tensor(out=ub[:], in0=pb[:, 0:TT], scalar=w0, in1=ub[:], op0=ALU.mult, op1=ALU.add),
             reads=[pbB, ubB, prmB], writes=[ubB])
        S.op("dve", lambda e: e.scalar_tensor_tensor(out=gb[:], in0=gb[:], scalar=1.0, in1=bg[:, :], op0=ALU.add, op1=ALU.mult),
             reads=[gbB, bgB], writes=[gbB])
        S.op("dve", lambda e: e.tensor_tensor(out=ub[:], in0=bb[:, :], in1=ub[:], op=ALU.mult), reads=[bbB, ubB], writes=[ubB])
        S.op("dve", lambda e: e.tensor_tensor(out=pa[:, j, :], in0=ub[:], in1=gb[:], op=ALU.mult), reads=[ubB, gbB], writes=[paB[j]])
        yield

    r_ctr = [0]
    rstate = {}

    def R_gen(i, j):
        pid = P_R0 + j // 2
        ci = 2 * (j % 2)
        n = r_ctr[0]
        r_ctr[0] += 1
        rb_, rbB_ = rbuf[n % 2], rbufB[n % 2]
        v_, vB_ = vbuf[n % 4], vbufB[n % 4]
        vb_, vbB_ = vb[n % 4], vbB[n % 4]
        sg_, sgB_ = sgb[n % 4], sgbB[n % 4]
        brv, brvB = nb(); zmm(i, pid, ci, brv, brvB)
        brg, brgB = nb(); zmm(i, pid, ci + 1, brg, brgB)
        if j % 2 == 1:
            release("w", pid)
        yield
        rstate[j] = (v_, vB_, vb_, vbB_, sg_, sgB_)
        w0 = cst[:, C_RGW + 4 * j + 0:C_RGW + 4 * j + 1]
        w1 = cst[:, C_RGW + 4 * j + 1:C_RGW + 4 * j + 2]
        w2 = cst[:, C_RGW + 4 * j + 2:C_RGW + 4 * j + 3]
        w3 = cst[:, C_RGW + 4 * j + 3:C_RGW + 4 * j + 4]
        bia = cst[:, C_RGB + j:C_RGB + j + 1]
        S.op("act", lambda e: e.activation(out=rb_[:, 3:3 + TT], in_=brv[:, :], func=AF.Copy), reads=[brvB], writes=[rbB_])
        S.op("act", lambda e: e.activation(out=v_[:], in_=brv[:, :], func=AF.Identity, scale=w3, bias=bia),
             reads=[brvB, cstB], writes=[vB_])
        S.op("act", lambda e: e.activation(out=sg_[:], in_=brg[:, :], func=AF.Tanh, scale=0.5), reads=[brgB], writes=[sgB_])
        S.op("dve", lambda e: e.tensor_copy(out=rb_[:, 0:3], in_=haloR[:, j, :]), reads=[haloRB[j]], writes=[rbB_])
        S.op("dve", lambda e: e.scalar_tensor_tensor(out=v_[:], in0=rb_[:, 2:2 + TT], scalar=w2, in1=v_[:], op0=ALU.mult, op1=ALU.add),
             reads=[rbB_, vB_, cstB], writes=[vB_])
        S.op("dve", lambda e: e.scalar_tensor_tensor(out=v_[:], in0=rb_[:, 1:1 + TT], scalar=w1, in1=v_[:], op0=ALU.mult, op1=ALU.add),
             reads=[rbB_, vB_, cstB], writes=[vB_])
        S.op("dve", lambda e: e.scalar_tensor_tensor(out=v_[:], in0=rb_[:, 0:TT], scalar=w0, in1=v_[:], op0=ALU.mult, op1=ALU.add),
             reads=[rbB_, vB_, cstB], writes=[vB_])
        S.op("act", lambda e: e.activation(out=vb_[:], in_=v_[:], func=AF.Copy), reads=[vB_], writes=[vbB_])
        S.op("dve", lambda e: e.tensor_copy(out=haloR[:, j, :], in_=rb_[:, TT:TT + 3]), reads=[rbB_], writes=[haloRB[j]])
        S.op("dve", lambda e: e.scalar_tensor_tensor(out=sg_[:], in0=sg_[:], scalar=1.0, in1=brg[:, :], op0=ALU.add, op1=ALU.mult),
             reads=[sgB_, brgB], writes=[sgB_])
        yield

    g_ctr = [0]
    gstate = {}

    def G_gen(i, j):
        n = g_ctr[0]
        g_ctr[0] += 1
        tr_, trB_ = trb[n % 2], trbB[n % 2]
        ti_, tiB_ = tib[n % 2], tibB[n % 2]
        a_, aB_ = ab[n % 2], abB[n % 2]
        prs = [(gi, ii) for gi, (jj, ii) in enumerate(GPAIRS) if jj == j]
        reads = [wgB] + [rstate[ii][3] for _, ii in prs]
        rhs_l = [rstate[ii][2] for _, ii in prs]
        bks = []
        for g in range(2):
            bk, bkB = nb()
            bks.append((bk, bkB))

            def f(e, g=g, bk=bk):
                for t, (gi, ii) in enumerate(prs):
                    ins = e.matmul(bk[:, :], lhsT=wg[:, g, gi, :], rhs=rhs_l[t][:], start=(t == 0), stop=(t == len(prs) - 1))
                return ins
            S.op("pe", f, reads=reads, writes=[bkB])
        (br, brB), (bi, biB) = bks
        yield
        gstate[j] = (tr_, trB_, ti_, tiB_, a_, aB_)
        S.op("act", lambda e: e.activation(out=tr_[:], in_=br[:, :], func=AF.Tanh, scale=0.5, bias=prm[:, Q_RBA + j:Q_RBA + j + 1]),
             reads=[brB, prmB], writes=[trB_])
        S.op("act", lambda e: e.activation(out=ti_[:], in_=bi[:, :], func=AF.Tanh, scale=0.5, bias=prm[:, Q_RBX + j:Q_RBX + j + 1]),
             reads=[biB, prmB], writes=[tiB_])
        S.op("act", lambda e: e.activation(out=a_[:], in_=tr_[:], func=AF.Exp, scale=prm[:, Q_HCL + j:Q_HCL + j + 1],
                                           bias=prm[:, Q_HCL + j:Q_HCL + j + 1]), reads=[trB_, prmB], writes=[aB_])
        S.op("act", lambda e: e.activation(out=tr_[:], in_=tr_[:], func=AF.Exp, scale=prm[:, Q_CL + j:Q_CL + j + 1],
                                           bias=prm[:, Q_CL + j:Q_CL + j + 1]), reads=[trB_, prmB], writes=[trB_])
        yield
        S.op("act", lambda e: e.activation(out=tr_[:], in_=tr_[:], func=AF.Sqrt, scale=-1.0 / 16, bias=prm[:, Q_C16:Q_C16 + 1]),
             reads=[trB_, prmB], writes=[trB_])
        yield

    def G_dve(i, j):
        v_, vB_, vb_, vbB_, sg_, sgB_ = rstate[j]
        tr_, trB_, ti_, tiB_, a_, aB_ = gstate[j]
        S.op("dve", lambda e: e.scalar_tensor_tensor(out=ti_[:], in0=ti_[:], scalar=1.0, in1=v_[:], op0=ALU.add, op1=ALU.mult),
             reads=[tiB_, vB_], writes=[tiB_])
        S.op("dve", lambda e: e.tensor_tensor(out=ti_[:], in0=ti_[:], in1=tr_[:], op=ALU.mult), reads=[tiB_, trB_], writes=[tiB_])
        S.op("dve", lambda e: e.tensor_tensor_scan(out=tr_[:], data0=a_[:], data1=ti_[:], initial=hst[:, j:j + 1],
                                                   op0=ALU.mult, op1=ALU.add), reads=[aB_, tiB_, hstB[j], trB_], writes=[trB_])
        S.op("dve", lambda e: e.tensor_copy(out=hst[:, j:j + 1], in_=tr_[:, TT - 1:TT]), reads=[trB_], writes=[hstB[j]])
        S.op("dve", lambda e: e.tensor_tensor(out=prg[:, j, :], in0=tr_[:], in1=sg_[:], op=ALU.mult),
             reads=[trB_, sgB_], writes=[prgB[j]])

    m_ctr = [0]
    mstate = {}
    ga_ring = [(gab[0][:], gabB[0]), (gab[1][:], gabB[1]), (ubuf[1][:], ubufB[1]), (ubuf[0][:], ubufB[0]),
               (pbuf[0][:, 2:2 + TT], pbufB[0])]
    gb_ring = [(gbb[0][:], gbbB[0]), (gbb[1][:], gbbB[1]), (gbuf[1][:], gbufB[1]), (gbuf[0][:], gbufB[0]),
               (pbuf[1][:, 2:2 + TT], pbufB[1])]

    def M_step(i, j):
        pid = P_M0 + j // 2
        ci = 2 * (j % 2)
        n = m_ctr[0]
        m_ctr[0] += 1
        ga_, gaB_ = ga_ring[j % 5]
        gb_, gbB_ = gb_ring[j % 5]
        mstate[j] = (ga_, gaB_, gb_, gbB_)
        bma, bmaB = nb(); zmm(i, pid, ci, bma, bmaB)
        bmb, bmbB = nb(); zmm(i, pid, ci + 1, bmb, bmbB)
        if j % 2 == 1:
            release("w", pid)
        S.op("act", lambda e: e.activation(out=ga_, in_=bma[:, :], func=AF.Tanh, scale=0.5, bias=prm[:, Q_BM + j:Q_BM + j + 1]),
             reads=[bmaB, prmB], writes=[gaB_])
        S.op("act", lambda e: e.activation(out=gb_, in_=bmb[:, :], func=AF.Tanh, scale=0.5, bias=prm[:, Q_BM + 8 + j:Q_BM + 8 + j + 1]),
             reads=[bmbB, prmB], writes=[gbB_])
        sp_ = P_SC0 + j // 4
        slot, rb = get("w", sp_)
        off = (j % 4) * 128
        bya, byaB = nb()

        def f(e):
            for kc in range(8):
                ins = e.matmul(bya[:, :], lhsT=ring[:, slot, kc * 512 + off:kc * 512 + off + 128], rhs=pa[:, kc, :],
                               start=(kc == 0), stop=(kc == 7))
            return ins
        S.op("pe", f, reads=[rb] + paB, writes=[byaB])
        if j % 4 == 3:
            release("w", sp_)
        S.op("dve", lambda e: e.scalar_tensor_tensor(out=ga_, in0=ga_, scalar=1.0, in1=bya[:, :], op0=ALU.add, op1=ALU.mult),
             reads=[gaB_, byaB], writes=[gaB_])

    def YB_step(i, j):
        ga_, gaB_, gb_, gbB_ = mstate[j]
        rp = P_RG0 + j // 4
        slot, rb = get("w", rp)
        off = (j % 4) * 128
        byb, bybB = nb()

        def f(e):
            for kc in range(8):
                e.matmul(byb[:, :], lhsT=ring[:, slot, kc * 512 + off:kc * 512 + off + 128], rhs=prg[:, kc, :],
                         start=(kc == 0), stop=False)
            e.matmul(byb[:, :], lhsT=rgc[:, 0, j * 128:(j + 1) * 128], rhs=prg[:, 8, :], start=False, stop=False)
            return e.matmul(byb[:, :], lhsT=rgc[:, 1, j * 128:(j + 1) * 128], rhs=prg[:, 9, :], start=False, stop=True)
        S.op("pe", f, reads=[rb, rgcB] + prgB, writes=[bybB])
        if j % 4 == 3:
            release("w", rp)
        S.op("dve", lambda e: e.scalar_tensor_tensor(out=gb_, in0=gb_, scalar=1.0, in1=byb[:, :], op0=ALU.add, op1=ALU.mult),
             reads=[gbB_, bybB], writes=[gbB_])
        S.op("pool", lambda e: e.tensor_tensor(out=merged[:, j, :], in0=ga_, in1=gb_, op=ALU.add), reads=[gaB_, gbB_],
             writes=[mergedB[j]])

    o_ctr = [0]
    t1_gen = [0, 0]
    pending_fin = []
    out_tokens = []

    def flush_fin():
        while pending_fin:
            t1_, t1B_, k, gen, col, colB, r0 = pending_fin.pop(0)
            assert t1_gen[k] == gen, "t1 slot reused before deferred finalize"
            S.op("dve", (lambda e, t1_=t1_, col=col: e.scalar_tensor_tensor(out=t1_[:], in0=t1_[:], scalar=stat[:, col:col + 1],
                                                                             in1=gfin[:], op0=ALU.mult, op1=ALU.mult)),
                 reads=[t1B_, colB, gfinB], writes=[t1B_])
            tok = S.dma("sp", (lambda e, t1_=t1_, r0=r0: e.dma_start(out=out_d[r0:r0 + 128, :], in_=t1_[:])), reads=[t1B_])
            out_tokens.append(tok)

    ol_state = {}

    def O_load(i, s):
        xk = xb_ctr[0] % NXB
        xb, xbB = next_xb()
        xb_gen[xk] += 1
        r0 = i * TT + s * 128
        S.dma("sp", lambda e: e.dma_start(out=xb[:], in_=x_d[r0:r0 + 128, :]), writes=[xbB])
        ol_state[(i, s)] = (xb, xbB, xk, xb_gen[xk])

    def O_step(i, s):
        b = i // 4
        n = o_ctr[0]
        o_ctr[0] += 1
        k = n % 2
        t1_, t1B_ = t1[k], t1B[k]
        t1_gen[k] += 1
        r0 = i * TT + s * 128
        xb, xbB, xk, xgen = ol_state.pop((i, s))
        assert xb_gen[xk] == xgen
        halves = []
        for hh in range(2):
            bk, bkB = nb()
            halves.append((bk, bkB))

            def f(e, hh=hh, bk=bk):
                for kc in range(8):
                    ins = e.matmul(bk[:, :], lhsT=merged[:, kc, s * 128:(s + 1) * 128], rhs=wout[:, kc, hh * 512:(hh + 1) * 512],
                                   start=(kc == 0), stop=(kc == 7))
                return ins
            S.op("pe", f, reads=[woutB] + mergedB, writes=[bkB])
        for hh in range(2):
            bk, bkB = halves[hh]
            S.op("dve", (lambda e, hh=hh, bk=bk: e.tensor_tensor(out=t1_[:, hh * 512:(hh + 1) * 512], in0=bk[:, :],
                                                                  in1=gatebc[:, b, hh * 512:(hh + 1) * 512], op=ALU.mult)),
                 reads=[bkB, gatebcB], writes=[t1B_])
        S.op("dve", lambda e: e.tensor_tensor(out=t1_[:], in0=t1_[:], in1=xb[:], op=ALU.add), reads=[t1B_, xbB], writes=[t1B_])
        col, colB = next_stat()
        sumsq(t1_, t1B_, col, colB, xb[:], xbB)
        rstd_from(col, colB)
        flush_fin()
        pending_fin.append((t1_, t1B_, k, t1_gen[k], col, colB, r0))

    allh = haloAB + haloRB + hstB

    def seq_reset(e):
        e.memset(haloA[:].rearrange("p j k -> p (j k)"), 0.0)
        e.memset(haloR[:].rearrange("p j k -> p (j k)"), 0.0)
        return e.memset(hst[:], 0.0)

    def fin(g):
        for _ in g:
            pass

    pump()
    prologue_a()
    for s in range(4):
        prep_norm(0, s)
    flush_norm()
    prologue_b()
    for q in range(4):
        prep_tr(0, q)

    for i in range(NT):
        if i % 4 == 0:
            S.op("dve", seq_reset, writes=allh)
        for j in range(8):
            rg_ = R_gen(i, j); next(rg_)
            fin(rg_)
            gg = None
            if j >= 2:
                gg = G_gen(i, j - 2); next(gg)
            ag = A_gen(i, j); next(ag)
            if j >= 3:
                G_dve(i, j - 3)
            if gg is not None:
                next(gg)
            next(ag)
            next(ag)
            fin(ag)
            if gg is not None:
                fin(gg)
            if i > 0 and j < 4:
                O_step(i - 1, j)
                if j < 3:
                    O_load(i - 1, j + 1)
            if j == 4:
                flush_fin()
            if i + 1 < NT:
                if j >= 3 and j - 3 < 4:
                    P_load(i + 1, j - 3)
                if j >= 4:
                    P_rest(i + 1, j - 4)
        r8 = R_gen(i, 8); next(r8); fin(r8)
        g6 = G_gen(i, 6); next(g6)
        G_dve(i, 5)
        fin(g6)
        r9 = R_gen(i, 9); next(r9); fin(r9)
        g7 = G_gen(i, 7); next(g7)
        G_dve(i, 6)
        fin(g7)
        flush_norm()
        if i == 0:
            gate_bc()
        M_step(i, 0)
        g8 = G_gen(i, 8); next(g8)
        G_dve(i, 7)
        fin(g8)
        M_step(i, 1)
        g9 = G_gen(i, 9); next(g9)
        G_dve(i, 8)
        fin(g9)
        G_dve(i, 9)
        M_step(i, 2)
        M_step(i, 3)
        if i + 1 < NT:
            prep_tr(i + 1, 0)
        M_step(i, 4)
        if i + 1 < NT:
            prep_tr(i + 1, 1)
        for j in range(5, 8):
            YB_step(i, j - 5)
            M_step(i, j)
            if i + 1 < NT:
                if j == 5:
                    prep_tr(i + 1, 2)
                if j == 6:
                    prep_tr(i + 1, 3)
        for j in range(3, 8):
            if j == 6:
                O_load(i, 0)
            YB_step(i, j)
    for s in range(4):
        if s < 3:
            O_load(NT - 1, s + 1)
        O_step(NT - 1, s)
    flush_fin()
    S.wait_all("sp", out_tokens)
    S.emit()
    return nc


def _kmajor(w, ncols_piece=512):
    K, N = w.shape
    kc = K // 128
    a = w.reshape(kc, 128, N // ncols_piece, ncols_piece)
    return np.ascontiguousarray(a.transpose(2, 1, 0, 3)).reshape(N // ncols_piece, 128, kc * ncols_piece)


def _host_layout(inp):
    f = np.float32
    w_in = np.asarray(inp["w_in"][0], f)
    cols = np.concatenate([np.arange(c, c + 128) for pc in piece_columns() for c in pc])
    w_in_r = w_in[:, cols]
    wall = np.zeros((NPIECE, 128, 4096), f)
    wall[0:17] = _kmajor(w_in_r)
    wall[17:19] = _kmajor(np.asarray(inp["sc_w_out"][0], f))
    rgw = np.asarray(inp["rg_w_out"][0], f)
    wall[19:21] = _kmajor(rgw[0:1024])
    rgc = np.ascontiguousarray(rgw[1024:1280].reshape(2, 128, 1024).transpose(1, 0, 2)).reshape(128, 2048)
    wada = _kmajor(np.asarray(inp["w_ada"][0], f))
    wo = np.asarray(inp["w_out"][0], f)
    wout = np.ascontiguousarray(wo.reshape(8, 128, 1024).transpose(1, 0, 2)).reshape(128, 8192)
    wg = np.zeros((128, 2, NGP, 128), f)
    for g, key in enumerate(["rg_w_a", "rg_w_x"]):
        wh = np.asarray(inp[key][0], f)
        full = np.zeros((RGW, RGW), f)
        for h in range(16):
            full[HD * h:HD * h + HD, HD * h:HD * h + HD] = wh[h]
        for gi, (j, i) in enumerate(GPAIRS):
            wg[:, g, gi, :] = full[128 * i:128 * i + 128, 128 * j:128 * j + 128]
    wg = wg.reshape(128, 2 * NGP * 128)
    cst = np.zeros((128, NCST), f)
    cst[:, C_SCW:C_SCW + 24] = np.asarray(inp["sc_conv_w"][0], f).reshape(3, 8, 128).transpose(2, 1, 0).reshape(128, 24)
    cst[:, C_SCB:C_SCB + 8] = np.asarray(inp["sc_conv_b"][0], f).reshape(8, 128).T
    cst[:, C_RGW:C_RGW + 40] = np.asarray(inp["rg_conv_w"][0], f).reshape(4, 10, 128).transpose(2, 1, 0).reshape(128, 40)
    cst[:, C_RGB:C_RGB + 10] = np.asarray(inp["rg_conv_b"][0], f).reshape(10, 128).T
    cst[:, C_RBA:C_RBA + 10] = np.asarray(inp["rg_b_a"][0], f).reshape(10, 128).T
    cst[:, C_RBX:C_RBX + 10] = np.asarray(inp["rg_b_x"][0], f).reshape(10, 128).T
    cst[:, C_LAM:C_LAM + 10] = np.asarray(inp["rg_lambda"][0], f).reshape(10, 128).T
    cst[:, C_BM:C_BM + 16] = np.asarray(inp["b_merge"][0], f).reshape(2, 8, 128).transpose(2, 0, 1).reshape(128, 16)
    cst[:, C_GN:C_GN + 8] = np.asarray(inp["g_norm"][0], f).reshape(8, 128).T
    bada = np.asarray(inp["b_ada"][0], f)
    cst[:, C_BADA:C_BADA + 32] = np.repeat(bada[0:2048].reshape(16, 128).T[:, :, None], 2, axis=2).reshape(128, 32)
    rows = np.zeros((128, 2048), f)
    rows[:, 0:1024] = np.asarray(inp["g_final"], f)[None, :]
    rows[:, 1024:2048] = bada[None, 2048:3072]
    def padded(a, run):
        lead = a.shape[:-1]
        a = a.reshape(*lead, a.shape[-1] // run, run)
        out = np.zeros((*lead, a.shape[-2], run + PAD), f)
        out[..., :run] = a
        return out
    return dict(wall=padded(wall, 1024), wada=padded(wada, 1024), rgc=padded(rgc, 1024), wout=padded(wout, 1024),
                wg=padded(wg, 512), cst=cst, rows=rows)


_NC_CACHE = {}


def kernel(**inputs):
    x = np.asarray(inputs["x"], np.float32)
    c = np.asarray(inputs["c"], np.float32)
    shared = _host_layout(inputs)
    if "nc" not in _NC_CACHE:
        _NC_CACHE["nc"] = build_nc()
    nc = _NC_CACHE["nc"]
    in_maps = []
    for core in range(NCORES):
        xs = np.ascontiguousarray(x[2 * core:2 * core + 2].reshape(NT * TT, D))
        cc = c[2 * core:2 * core + 2]
        ct = np.ascontiguousarray(cc.T.reshape(8, 128, 2).transpose(1, 0, 2)).reshape(128, 16)
        m = dict(shared)
        m["x"] = xs
        m["ct"] = ct
        in_maps.append(m)
    res = run_bass_kernel_spmd(nc, in_maps, core_ids=list(range(NCORES)))
    out = np.stack([np.asarray(r["out"], np.float32).reshape(2, SEQ, D) for r in res.results], axis=0)
    return out.reshape(16, SEQ, D)
```

```python
import numpy as np
import concourse.bass as bass
import concourse.mybir as mybir
from concourse.bass_utils import run_bass_kernel_spmd

F32 = mybir.dt.float32
BF16 = mybir.dt.bfloat16
AF = mybir.ActivationFunctionType
ALU = mybir.AluOpType

ENGS = ("pe", "act", "dve", "pool", "sp")
NCORES = 8
D = 1024
SEQ = 2048
TT = 512
NT = 8
RGW = 1280
NRC = 10
HD = 80
EPS = 1e-6
NSLOT = 4
PAD = 16


class Buf:
    __slots__ = ("name", "w", "r", "dsem", "dcnt")

    def __init__(self, name):
        self.name = name
        self.w = None
        self.r = []
        self.dsem = None
        self.dcnt = 0


class Sched:
    def __init__(self, nc):
        self.nc = nc
        self.streams = {e: [] for e in ENGS}
        self.semobj = {}
        for e in ENGS:
            self.semobj["s_" + e] = nc.alloc_semaphore("s_" + e)
        self.tick = {e: 0 for e in ENGS}
        self.seen = {e: {} for e in ENGS}
        self.nbuf = 0

    def buf(self, name=None):
        self.nbuf += 1
        return Buf(f"{name or 'b'}_{self.nbuf}")

    def _need(self, e, tok, waits):
        if tok is None:
            return
        semkey, val, _ = tok
        if self.seen[e].get(semkey, 0) >= val:
            return
        if val > waits.get(semkey, 0):
            waits[semkey] = val

    def _deps(self, e, reads, writes):
        waits = {}
        for b in reads:
            if b.w is not None:
                self._need(e, b.w, waits)
        for b in writes:
            if b.w is not None and b.w[2] != e:
                self._need(e, b.w, waits)
            for t in b.r:
                if t[2] != e:
                    self._need(e, t, waits)
        for k, v in waits.items():
            self.seen[e][k] = v
            self.streams[e].append(("wait", k, v))

    def op(self, e, fn, reads=(), writes=()):
        for b in writes:
            if b.name.startswith("bank") and e == "pe" and b.w is not None and b.w[2] == "pe" and not b.r and b not in reads:
                raise AssertionError(f"PSUM {b.name} re-allocated before its consumers were emitted")
        self._deps(e, reads, writes)
        self.tick[e] += 1
        tok = ("s_" + e, self.tick[e], e)
        self.streams[e].append(("op", fn, "s_" + e, 1))
        for b in reads:
            b.r.append(tok)
        for b in writes:
            b.w = tok
            b.r = []
        return tok

    def dma(self, e, fn, reads=(), writes=(), track=None):
        self._deps(e, reads, writes)
        tb = track or (writes[0] if writes else reads[0])
        if tb.dsem is None:
            tb.dsem, tb.dcnt = {}, {}
        if e not in tb.dsem:
            tb.dsem[e] = f"d_{tb.name}_{e}"
            tb.dcnt[e] = 0
            self.semobj[tb.dsem[e]] = self.nc.alloc_semaphore(tb.dsem[e])
        tb.dcnt[e] += 16
        tok = (tb.dsem[e], tb.dcnt[e], "dma")
        self.streams[e].append(("op", fn, tb.dsem[e], 16))
        for b in reads:
            b.r.append(tok)
        for b in writes:
            b.w = tok
            b.r = []
        return tok

    def wait_all(self, e, toks):
        waits = {}
        for t in toks:
            self._need(e, t, waits)
        for k, v in waits.items():
            self.seen[e][k] = v
            self.streams[e].append(("wait", k, v))

    def emit(self):
        sched = self

        def run(e):
            def body(engine):
                for item in sched.streams[e]:
                    if item[0] == "wait":
                        engine.wait_ge(sched.semobj[item[1]], item[2])
                    else:
                        _, fn, semkey, inc = item
                        fn(engine).then_inc(sched.semobj[semkey], inc)
            return body

        with self.nc.Block() as block:
            block.tensor(run("pe"))
            block.scalar(run("act"))
            block.vector(run("dve"))
            block.gpsimd(run("pool"))
            block.sync(run("sp"))


OFF_B, OFF_C, OFF_V, OFF_G, OFF_RV, OFF_RG, OFF_MA, OFF_MB = 0, 1024, 2048, 3072, 4096, 5376, 6656, 7680


def piece_columns():
    pcs = []
    for j in range(8):
        pcs.append([OFF_C + 128 * j, OFF_V + 128 * j, OFF_G + 128 * j, OFF_B + 128 * j])
    for q in range(5):
        pcs.append([OFF_RV + 128 * (2 * q), OFF_RG + 128 * (2 * q),
                    OFF_RV + 128 * (2 * q + 1), OFF_RG + 128 * (2 * q + 1)])
    for q in range(4):
        pcs.append([OFF_MA + 128 * (2 * q), OFF_MB + 128 * (2 * q),
                    OFF_MA + 128 * (2 * q + 1), OFF_MB + 128 * (2 * q + 1)])
    return pcs


P_A0, P_R0, P_M0, P_SC0, P_RG0, NPIECE = 0, 8, 13, 17, 19, 21
P_GT0, NSCR = 21, 23


def gate_pairs():
    pairs = []
    for j in range(NRC):
        heads = set(range((128 * j) // HD, (128 * j + 127) // HD + 1))
        ins = set()
        for h in heads:
            for d in range(HD * h, HD * h + HD):
                ins.add(d // 128)
        for i in sorted(ins):
            pairs.append((j, i))
    return pairs


GPAIRS = gate_pairs()
NGP = len(GPAIRS)

C_SCW, C_SCB, C_RGW, C_RGB, C_RBA, C_RBX, C_LAM, C_BM, C_GN, C_BADA = 0, 24, 32, 72, 82, 92, 102, 112, 128, 136
NCST = 168


def tile_seq():
    s = []
    for j in range(8):
        s.append(P_A0 + j)
        if j % 2 == 0:
            s.append(P_R0 + j // 2)
    s.append(P_R0 + 4)
    s += [P_M0, P_SC0, P_M0 + 1, P_M0 + 2, P_SC0 + 1, P_RG0, P_M0 + 3, P_RG0 + 1]
    return s


def build_nc():
    nc = bass.Bass("TRN2", target_bir_lowering=False)
    x_d = nc.dram_tensor("x", [NT * TT, D], F32, kind="ExternalInput").ap()
    wall_d = nc.dram_tensor("wall", [NPIECE, 128, 4, 1024 + PAD], F32, kind="ExternalInput").ap()
    wada_d = nc.dram_tensor("wada", [6, 128, 4, 1024 + PAD], F32, kind="ExternalInput").ap()
    rgc_d = nc.dram_tensor("rgc", [128, 2, 1024 + PAD], F32, kind="ExternalInput").ap()
    wout_d = nc.dram_tensor("wout", [128, 8, 1024 + PAD], F32, kind="ExternalInput").ap()
    wg_d = nc.dram_tensor("wg", [128, NGP // 2, 512 + PAD], F32, kind="ExternalInput").ap()
    cst_d = nc.dram_tensor("cst", [128, NCST], F32, kind="ExternalInput").ap()
    rows_d = nc.dram_tensor("rows", [128, 2048], F32, kind="ExternalInput").ap()
    ct_d = nc.dram_tensor("ct", [128, 16], F32, kind="ExternalInput").ap()
    out_d = nc.dram_tensor("out", [NT * TT, D], F32, kind="ExternalOutput").ap()
    wsc_d = nc.dram_tensor("wsc", [NSCR, 128, 4096], BF16, kind="Internal").ap()

    S = Sched(nc)
    cnt = [0]

    def sb(shape, dtype, name):
        cnt[0] += 1
        return nc.alloc_sbuf_tensor(f"{name}_{cnt[0]}", list(shape), dtype)

    ring = sb([128, NSLOT, 4096], BF16, "ring")
    ringB = [S.buf(f"ring{k}") for k in range(NSLOT)]
    rgc = sb([128, 2, 1024], BF16, "rgc"); rgcB = S.buf("rgc")
    wout = sb([128, 8, 1024], BF16, "wout"); woutB = S.buf("wout")
    wg = sb([128, 2, NGP, 128], BF16, "wg"); wgB = S.buf("wg")
    hT = [sb([128, 8, TT], BF16, f"hT{k}") for k in range(2)]
    hTB = [S.buf(f"hT{k}") for k in range(2)]
    xn = sb([128, 4, D], BF16, "xn"); xnB = [S.buf(f"xn{s}") for s in range(4)]
    NXB = 3
    xbuf = [sb([128, D], F32, f"xb{k}") for k in range(NXB)]; xbufB = [S.buf(f"xb{k}") for k in range(NXB)]
    t1 = [sb([128, D], F32, f"t1_{k}") for k in range(2)]; t1B = [S.buf(f"t1_{k}") for k in range(2)]
    gatebc = sb([128, 2, D], F32, "gatebc"); gatebcB = S.buf("gatebc")
    gfin = sb([128, D], F32, "gfin"); gfinB = S.buf("gfin")
    NA = 2
    pbuf = [sb([128, 2 + TT], F32, f"pbuf{k}") for k in range(NA)]; pbufB = [S.buf(f"pbuf{k}") for k in range(NA)]
    ubuf = [sb([128, TT], F32, f"ubuf{k}") for k in range(NA)]; ubufB = [S.buf(f"ubuf{k}") for k in range(NA)]
    gbuf = [sb([128, TT], F32, f"gbuf{k}") for k in range(NA)]; gbufB = [S.buf(f"gbuf{k}") for k in range(NA)]
    pa = sb([128, 8, TT], BF16, "pa"); paB = [S.buf(f"pa{j}") for j in range(8)]
    prg = sb([128, NRC, TT], BF16, "prg"); prgB = [S.buf(f"prg{j}") for j in range(NRC)]
    rbuf = [sb([128, 3 + TT], F32, f"rbuf{k}") for k in range(2)]; rbufB = [S.buf(f"rbuf{k}") for k in range(2)]
    vbuf = [sb([128, TT], F32, f"vbuf{k}") for k in range(4)]; vbufB = [S.buf(f"vbuf{k}") for k in range(4)]
    vb = [sb([128, TT], BF16, f"vb{k}") for k in range(4)]; vbB = [S.buf(f"vb{k}") for k in range(4)]
    sgb = [sb([128, TT], F32, f"sgb{k}") for k in range(4)]; sgbB = [S.buf(f"sgb{k}") for k in range(4)]
    trb = [sb([128, TT], F32, f"trb{k}") for k in range(2)]; trbB = [S.buf(f"trb{k}") for k in range(2)]
    tib = [sb([128, TT], F32, f"tib{k}") for k in range(2)]; tibB = [S.buf(f"tib{k}") for k in range(2)]
    ab = [sb([128, TT], F32, f"ab{k}") for k in range(2)]; abB = [S.buf(f"ab{k}") for k in range(2)]
    gab = [sb([128, TT], F32, f"gab{k}") for k in range(2)]; gabB = [S.buf(f"gab{k}") for k in range(2)]
    gbb = [sb([128, TT], F32, f"gbb{k}") for k in range(2)]; gbbB = [S.buf(f"gbb{k}") for k in range(2)]
    merged = sb([128, 8, TT], BF16, "merged"); mergedB = [S.buf(f"mg{j}") for j in range(8)]
    ident = sb([128, 128], BF16, "ident"); identf = sb([128, 128], F32, "identf"); identB = S.buf("ident")
    ones = sb([128, 128], F32, "ones")
    mhalf = sb([128, 1], F32, "mhalf"); constB = S.buf("const")
    cst = sb([128, NCST], F32, "cst"); cstB = S.buf("cst")
    ctt = sb([128, 16], F32, "ctt"); cttB = S.buf("ctt")
    prm = sb([128, 128], F32, "prm"); prmB = S.buf("prm")
    Q_SCW, Q_SCB, Q_RBA, Q_RBX, Q_CL, Q_HCL, Q_BM, Q_G, Q_SH, Q_C16 = 0, 24, 32, 42, 52, 62, 72, 88, 104, 120
    slb = sb([128, 8, 2], BF16, "slb"); slB = S.buf("sl")

    def slrep(b, kc):
        return merged[:, b * 2 + kc // 4, (kc % 4) * 128:(kc % 4 + 1) * 128]
    ss = sb([128, 32], F32, "ss")
    haloA = sb([128, 8, 2], F32, "haloA"); haloAB = [S.buf(f"hA{j}") for j in range(8)]
    haloR = sb([128, NRC, 3], F32, "haloR"); haloRB = [S.buf(f"hR{j}") for j in range(NRC)]
    hst = sb([128, NRC], F32, "hst"); hstB = [S.buf(f"hs{j}") for j in range(NRC)]
    stat = sb([128, 32], F32, "stat"); statB = [S.buf(f"st{k}") for k in range(16)]

    NBK = 7
    banks = [nc.alloc_psum_tensor(f"bank{k}", [128, TT], F32) for k in range(NBK)]
    bankB = [S.buf(f"bank{k}") for k in range(NBK)]
    pst = nc.alloc_psum_tensor("pst", [128, 2, TT], BF16); pstB = S.buf("pst")
    bctr = [0]

    def nb():
        k = bctr[0] % NBK
        bctr[0] += 1
        return banks[k], bankB[k]

    wscB = [S.buf(f"wsc{p}") for p in range(NSCR)]

    seq = [("ada", k) for k in range(4)]
    def tile0_seq():
        ts = tile_seq()
        k = ts.index(P_M0)
        return ts[:k] + [P_GT0, P_GT0 + 1] + ts[k:]
    for i in range(NT):
        seq += [("w", p) for p in (tile0_seq() if i == 0 else tile_seq())]
    ring_state = {"next": 0, "released": set(), "where": {}}

    def pump():
        while ring_state["next"] < len(seq):
            n = ring_state["next"]
            if n >= NSLOT and (n - NSLOT) not in ring_state["released"]:
                break
            kind, p = seq[n]
            slot = n % NSLOT
            if kind == "ada":
                S.dma("pool", (lambda e, slot=slot, p=p: e.dma_start(out=ring[:, slot, :].rearrange("p (a n) -> p a n", a=4),
                                                                      in_=wada_d[p][:, :, 0:1024])),
                      writes=[ringB[slot]])
            else:
                S.dma("sp", (lambda e, slot=slot, p=p: e.dma_start(out=ring[:, slot, :], in_=wsc_d[p])),
                      reads=[wscB[p]], writes=[ringB[slot]])
            ring_state["where"][(kind, p)] = (slot, n)
            ring_state["next"] += 1

    def get(kind, p):
        key = (kind, p)
        assert key in ring_state["where"], f"piece {key} not loaded (ring order bug)"
        slot, n = ring_state["where"][key]
        return slot, ringB[slot]

    def release(kind, p):
        slot, n = ring_state["where"].pop((kind, p))
        ring_state["released"].add(n)
        pump()

    def prologue_a():
        S.dma("sp", lambda e: e.dma_start(out=cst[:], in_=cst_d), writes=[cstB])
        S.dma("sp", lambda e: e.dma_start(out=ctt[:], in_=ct_d), writes=[cttB])
        S.dma("sp", lambda e: e.dma_start(out=gfin[:], in_=rows_d[:, 0:1024]), writes=[gfinB])
        S.dma("sp", lambda e: e.dma_start(out=t1[0][:], in_=rows_d[:, 1024:2048]), writes=[t1B[0]])
        S.op("pool", lambda e: e.memset(identf[:], 0.0), writes=[identB])
        S.op("pool", lambda e: e.affine_select(out=identf[:], in_=identf[:], pattern=[[-1, 128]], compare_op=ALU.not_equal,
                                               fill=1.0, base=0, channel_multiplier=1), reads=[identB], writes=[identB])
        S.op("pool", lambda e: e.tensor_copy(out=ident[:], in_=identf[:]), reads=[identB], writes=[identB])

        def mk_const(e):
            e.memset(ones[:], 1.0)
            e.memset(mhalf[:], -0.5)
            return e.memset(stat[:], 0.0)
        S.op("pool", mk_const, writes=[constB] + statB)


    def prologue_b():
        order = tile0_seq()

        def cast_piece(p):
            src = wall_d[p] if p < NPIECE else wada_d[4 + p - P_GT0]
            S.dma("pool", (lambda e, p=p, src=src: e.dma_start(out=wsc_d[p].rearrange("p (a n) -> p a n", a=4), in_=src[:, :, 0:1024])),
                  writes=[wscB[p]])
        for k, p in enumerate(order):
            if k == 3:
                S.dma("pool", lambda e: e.dma_start(out=wg[:].rearrange("p a g e -> p (a g e)").rearrange("p (a n) -> p a n", n=512),
                                                    in_=wg_d[:, :, 0:512]), writes=[wgB])
            if p == P_RG0:
                S.dma("pool", lambda e: e.dma_start(out=rgc[:], in_=rgc_d[:, :, 0:1024]), writes=[rgcB])
            cast_piece(p)
        S.dma("pool", lambda e: e.dma_start(out=wout[:], in_=wout_d[:, :, 0:1024]), writes=[woutB])

        def mk_prm1(e):
            e.tensor_scalar(out=prm[:, Q_SCW:Q_SCW + 32], in0=cst[:, C_SCW:C_SCW + 32], scalar1=0.5, scalar2=None, op0=ALU.mult)
            e.tensor_scalar(out=prm[:, Q_RBA:Q_RBA + 20], in0=cst[:, C_RBA:C_RBA + 20], scalar1=0.5, scalar2=None, op0=ALU.mult)
            return e.tensor_scalar(out=prm[:, Q_BM:Q_BM + 16], in0=cst[:, C_BM:C_BM + 16], scalar1=0.5, scalar2=None, op0=ALU.mult)
        S.op("dve", mk_prm1, reads=[cstB], writes=[prmB])
        S.op("dve", lambda e: e.memset(prm[:, Q_C16:Q_C16 + 1], 1.0 / 16), writes=[prmB])
        tmpc = sb([128, 40], F32, "tmpc"); tmpcB = S.buf("tmpc")
        S.op("act", lambda e: e.activation(out=tmpc[:, 0:10], in_=cst[:, C_LAM:C_LAM + 10], func=AF.Abs),
             reads=[cstB], writes=[tmpcB])
        S.op("act", lambda e: e.activation(out=tmpc[:, 10:20], in_=tmpc[:, 0:10], func=AF.Exp, scale=-1.0),
             reads=[tmpcB], writes=[tmpcB])
        S.op("act", lambda e: e.activation(out=tmpc[:, 20:30], in_=tmpc[:, 10:20], func=AF.Ln, bias=1.0),
             reads=[tmpcB], writes=[tmpcB])

        S.op("dve", lambda e: e.tensor_scalar(out=tmpc[:, 30:40], in0=cst[:, C_LAM:C_LAM + 10], scalar1=-1.0, scalar2=0.0,
                                              op0=ALU.mult, op1=ALU.max), reads=[cstB, tmpcB], writes=[tmpcB])
        S.op("dve", lambda e: e.tensor_tensor(out=tmpc[:, 30:40], in0=tmpc[:, 30:40], in1=tmpc[:, 20:30], op=ALU.add),
             reads=[tmpcB], writes=[tmpcB])

        def mk_cl2(e):
            e.tensor_scalar(out=prm[:, Q_CL:Q_CL + 10], in0=tmpc[:, 30:40], scalar1=-8.0, scalar2=None, op0=ALU.mult)
            return e.tensor_scalar(out=prm[:, Q_HCL:Q_HCL + 10], in0=tmpc[:, 30:40], scalar1=-4.0, scalar2=None, op0=ALU.mult)
        S.op("dve", mk_cl2, reads=[tmpcB], writes=[prmB])

        slt = sb([128, 16], F32, "slt")
        S.op("act", lambda e: e.activation(out=slt[:], in_=ctt[:], func=AF.Tanh, scale=0.5), reads=[cttB], writes=[slB])
        S.op("dve", lambda e: e.scalar_tensor_tensor(out=slt[:], in0=slt[:], scalar=1.0, in1=ctt[:], op0=ALU.add, op1=ALU.mult),
             reads=[slB, cttB], writes=[slB])
        S.op("dve", lambda e: e.tensor_scalar(out=slt[:], in0=slt[:], scalar1=0.5, scalar2=None, op0=ALU.mult), reads=[slB], writes=[slB])
        S.op("dve", lambda e: e.tensor_copy(out=slb[:].rearrange("p k b -> p (k b)"), in_=slt[:]), reads=[slB], writes=[slB])

        def mk_slrep(e):
            for b in range(2):
                for kc in range(8):
                    ins = e.tensor_scalar(out=slrep(b, kc), in0=ones[:], scalar1=slt[:, kc * 2 + b:kc * 2 + b + 1],
                                          scalar2=None, op0=ALU.mult)
            return ins
        S.op("dve", mk_slrep, reads=[slB, constB], writes=[slB] + mergedB)

        bk_ss, bk_ssB = nb()

        def mm_ss(e, pc, slot):
            for mm in range(4):
                m = pc * 4 + mm
                for kc in range(8):
                    ins = e.matmul(bk_ss[:, m * 2:m * 2 + 2], lhsT=ring[:, slot, kc * 512 + mm * 128:kc * 512 + mm * 128 + 128],
                                   rhs=slb[:, kc, :], start=(kc == 0), stop=(kc == 7))
            return ins
        for pc in range(4):
            slot, rb = get("ada", pc)
            S.op("pe", (lambda e, pc=pc, slot=slot: mm_ss(e, pc, slot)), reads=[rb, slB] + ([bk_ssB] if pc else []), writes=[bk_ssB])
            release("ada", pc)
        S.op("dve", lambda e: e.tensor_tensor(out=ss[:], in0=bk_ss[:, 0:32], in1=cst[:, C_BADA:C_BADA + 32], op=ALU.add),
             reads=[bk_ssB, cstB], writes=[prmB])

        def mk_gsh(e):
            for b in range(2):
                e.scalar_tensor_tensor(out=prm[:, Q_G + 8 * b:Q_G + 8 * b + 8], in0=ss[:, 16 + b:32:2], scalar=1.0,
                                       in1=cst[:, C_GN:C_GN + 8], op0=ALU.add, op1=ALU.mult)
                ins = e.tensor_copy(out=prm[:, Q_SH + 8 * b:Q_SH + 8 * b + 8], in_=ss[:, b:16:2])
            return ins
        S.op("dve", mk_gsh, reads=[prmB, cstB], writes=[prmB])

    def gate_bc():
        for pc in range(2):
            slot, rb = get("w", P_GT0 + pc)
            for b in range(2):
                bk, bkB = nb()

                def mm_gate(e, slot=slot, b=b, bk=bk):
                    for kc in range(8):
                        ins = e.matmul(bk[:, :], lhsT=slrep(b, kc), rhs=ring[:, slot, kc * 512:(kc + 1) * 512],
                                       start=(kc == 0), stop=(kc == 7))
                    return ins
                S.op("pe", mm_gate, reads=[slB, rb] + mergedB, writes=[bkB])
                S.op("dve", (lambda e, bk=bk, b=b, pc=pc: e.tensor_tensor(out=gatebc[:, b, pc * 512:(pc + 1) * 512], in0=bk[:, :],
                                                                           in1=t1[0][:, pc * 512:(pc + 1) * 512], op=ALU.add)),
                     reads=[bkB, t1B[0]], writes=[gatebcB])
            release("w", P_GT0 + pc)
        S.op("dve", lambda e: e.tensor_scalar(out=gatebc[:].rearrange("p b n -> p (b n)"), in0=gatebc[:].rearrange("p b n -> p (b n)"),
                                              scalar1=0.5, scalar2=None, op0=ALU.mult), reads=[gatebcB], writes=[gatebcB, t1B[0]])

    xb_ctr = [0]
    st_ctr = [0]

    def next_xb():
        k = xb_ctr[0] % NXB
        xb_ctr[0] += 1
        return xbuf[k], xbufB[k]

    def next_stat():
        k = st_ctr[0] % 16
        st_ctr[0] += 1
        return k, statB[k]

    def rstd_from(col, colB):
        S.op("dve", lambda e: e.tensor_scalar(out=stat[:, col:col + 1], in0=stat[:, col:col + 1], scalar1=1.0 / D, scalar2=EPS,
                                              op0=ALU.mult, op1=ALU.add), reads=[colB], writes=[colB])
        S.op("pool", lambda e: e.tensor_tensor(out=stat[:, col:col + 1], in0=stat[:, col:col + 1], in1=mhalf[:], op=ALU.pow),
             reads=[colB, constB], writes=[colB])

    xb_gen = [0] * NXB
    pending_norm = []

    def flush_norm():
        while pending_norm:
            xb, xbB, k, gen, col, colB, s_ = pending_norm.pop(0)
            assert xb_gen[k] == gen, "xbuf slot reused before deferred normalize"
            S.op("dve", (lambda e, xb=xb, col=col, s_=s_: e.tensor_scalar(out=xn[:, s_, :], in0=xb[:], scalar1=stat[:, col:col + 1],
                                                                            scalar2=None, op0=ALU.mult)),
                 reads=[xbB, colB], writes=[xnB[s_]])

    def sumsq(src, srcB, col, colB, jk, jkB):
        S.op("dve", lambda e: e.memset(stat[:, col:col + 1], 0.0), writes=[colB])
        S.op("dve", lambda e: e.scalar_tensor_tensor(out=jk, in0=src[:], scalar=1.0, in1=src[:], op0=ALU.mult, op1=ALU.mult,
                                                     accum_out=stat[:, col:col + 1]), reads=[srcB, colB], writes=[jkB, colB])

    pl_state = {}

    def P_load(i, s):
        k = xb_ctr[0] % NXB
        xb, xbB = next_xb()
        xb_gen[k] += 1
        r0 = i * TT + s * 128
        S.dma("sp", lambda e: e.dma_start(out=xb[:], in_=x_d[r0:r0 + 128, :]), writes=[xbB])
        pl_state[(i, s)] = (xb, xbB, k, xb_gen[k])

    def P_rest(i, s):
        xb, xbB, k, gen = pl_state.pop((i, s))
        assert xb_gen[k] == gen
        col, colB = next_stat()
        sumsq(xb, xbB, col, colB, xn[:, s, :], xnB[s])
        rstd_from(col, colB)
        flush_norm()
        pending_norm.append((xb, xbB, k, gen, col, colB, s))

    def prep_norm(i, s):
        P_load(i, s)
        P_rest(i, s)

    def prep_tr(i, q):
        b = i // 4
        h = i % 2

        def tr(e):
            for kl in range(2):
                kc = 2 * q + kl
                for s in range(4):
                    ins = e.transpose(out=pst[:, kl, s * 128:(s + 1) * 128], in_=xn[:, s, kc * 128:(kc + 1) * 128], identity=ident[:])
            return ins
        S.op("pe", tr, reads=xnB + [identB], writes=[pstB])
        for kl in range(2):
            kc = 2 * q + kl
            S.op("act", (lambda e, kl=kl, kc=kc: e.activation(out=hT[h][:, kc, :], in_=pst[:, kl, :], func=AF.Identity,
                                                             scale=prm[:, Q_G + 8 * b + kc:Q_G + 8 * b + kc + 1],
                                                             bias=prm[:, Q_SH + 8 * b + kc:Q_SH + 8 * b + kc + 1])),
                 reads=[pstB, prmB], writes=[hTB[h]])

    def zmm(i, pid, ci, bk, bkB):
        slot, rb = get("w", pid)
        h = i % 2

        def f(e):
            for kc in range(8):
                ins = e.matmul(bk[:, :], lhsT=ring[:, slot, kc * 512 + ci * 128:kc * 512 + ci * 128 + 128], rhs=hT[h][:, kc, :],
                               start=(kc == 0), stop=(kc == 7))
            return ins
        S.op("pe", f, reads=[rb, hTB[h]], writes=[bkB])

    a_ctr = [0]

    def A_gen(i, j):
        pid = P_A0 + j
        k = a_ctr[0] % NA
        a_ctr[0] += 1
        pb, pbB, ub, ubB, gb, gbB = pbuf[k], pbufB[k], ubuf[k], ubufB[k], gbuf[k], gbufB[k]
        w0 = prm[:, Q_SCW + 3 * j + 0:Q_SCW + 3 * j + 1]
        w1 = prm[:, Q_SCW + 3 * j + 1:Q_SCW + 3 * j + 2]
        w2 = prm[:, Q_SCW + 3 * j + 2:Q_SCW + 3 * j + 3]
        bia = prm[:, Q_SCB + j:Q_SCB + j + 1]
        bc, bcB = nb(); zmm(i, pid, 0, bc, bcB)
        bv, bvB = nb(); zmm(i, pid, 1, bv, bvB)
        bg, bgB = nb(); zmm(i, pid, 2, bg, bgB)
        yield
        S.op("act", lambda e: e.activation(out=pb[:, 2:2 + TT], in_=bc[:, :], func=AF.Copy), reads=[bcB], writes=[pbB])
        S.op("dve", lambda e: e.tensor_copy(out=pb[:, 0:2], in_=haloA[:, j, :]), reads=[haloAB[j]], writes=[pbB])
        S.op("dve", lambda e: e.tensor_tensor(out=pb[:, 2:2 + TT], in0=bv[:, :], in1=pb[:, 2:2 + TT], op=ALU.mult),
             reads=[bvB, pbB], writes=[pbB])
        S.op("dve", lambda e: e.tensor_copy(out=haloA[:, j, :], in_=pb[:, TT:TT + 2]), reads=[pbB], writes=[haloAB[j]])
        S.op("act", lambda e: e.activation(out=ub[:], in_=pb[:, 2:2 + TT], func=AF.Identity, scale=w2, bias=bia),
             reads=[pbB, prmB], writes=[ubB])
        yield
        bb, bbB = nb(); zmm(i, pid, 3, bb, bbB)
        release("w", pid)
        yield
        S.op("act", lambda e: e.activation(out=gb[:], in_=bg[:, :], func=AF.Tanh, scale=0.5), reads=[bgB], writes=[gbB])
        S.op("dve", lambda e: e.scalar_tensor_tensor(out=ub[:], in0=pb[:, 1:1 + TT], scalar=w1, in1=ub[:], op0=ALU.mult, op1=ALU.add),
             reads=[pbB, ubB, prmB], writes=[ubB])
        S.op("dve", lambda e: e.scalar_tensor_tensor(out=ub[:], in0=pb[:, 0:TT], scalar=w0, in1=ub[:], op0=ALU.mult, op1=ALU.add),
             reads=[pbB, ubB, prmB], writes=[ubB])
        S.op("dve", lambda e: e.scalar_tensor_tensor(out=gb[:], in0=gb[:], scalar=1.0, in1=bg[:, :], op0=ALU.add, op1=ALU.mult),
             reads=[gbB, bgB], writes=[gbB])
        S.op("dve", lambda e: e.tensor_tensor(out=ub[:], in0=bb[:, :], in1=ub[:], op=ALU.mult), reads=[bbB, ubB], writes=[ubB])
        S.op("dve", lambda e: e.tensor_tensor(out=pa[:, j, :], in0=ub[:], in1=gb[:], op=ALU.mult), reads=[ubB, gbB], writes=[paB[j]])
        yield

    r_ctr = [0]
    rstate = {}

    def R_gen(i, j):
        pid = P_R0 + j // 2
        ci = 2 * (j % 2)
        n = r_ctr[0]
        r_ctr[0] += 1
        rb_, rbB_ = rbuf[n % 2], rbufB[n % 2]
        v_, vB_ = vbuf[n % 4], vbufB[n % 4]
        vb_, vbB_ = vb[n % 4], vbB[n % 4]
        sg_, sgB_ = sgb[n % 4], sgbB[n % 4]
        brv, brvB = nb(); zmm(i, pid, ci, brv, brvB)
        brg, brgB = nb(); zmm(i, pid, ci + 1, brg, brgB)
        if j % 2 == 1:
            release("w", pid)
        yield
        rstate[j] = (v_, vB_, vb_, vbB_, sg_, sgB_)
        w0 = cst[:, C_RGW + 4 * j + 0:C_RGW + 4 * j + 1]
        w1 = cst[:, C_RGW + 4 * j + 1:C_RGW + 4 * j + 2]
        w2 = cst[:, C_RGW + 4 * j + 2:C_RGW + 4 * j + 3]
        w3 = cst[:, C_RGW + 4 * j + 3:C_RGW + 4 * j + 4]
        bia = cst[:, C_RGB + j:C_RGB + j + 1]
        S.op("act", lambda e: e.activation(out=rb_[:, 3:3 + TT], in_=brv[:, :], func=AF.Copy), reads=[brvB], writes=[rbB_])
        S.op("act", lambda e: e.activation(out=v_[:], in_=brv[:, :], func=AF.Identity, scale=w3, bias=bia),
             reads=[brvB, cstB], writes=[vB_])
        S.op("act", lambda e: e.activation(out=sg_[:], in_=brg[:, :], func=AF.Tanh, scale=0.5), reads=[brgB], writes=[sgB_])
        S.op("dve", lambda e: e.tensor_copy(out=rb_[:, 0:3], in_=haloR[:, j, :]), reads=[haloRB[j]], writes=[rbB_])
        S.op("dve", lambda e: e.scalar_tensor_tensor(out=v_[:], in0=rb_[:, 2:2 + TT], scalar=w2, in1=v_[:], op0=ALU.mult, op1=ALU.add),
             reads=[rbB_, vB_, cstB], writes=[vB_])
        S.op("dve", lambda e: e.scalar_tensor_tensor(out=v_[:], in0=rb_[:, 1:1 + TT], scalar=w1, in1=v_[:], op0=ALU.mult, op1=ALU.add),
             reads=[rbB_, vB_, cstB], writes=[vB_])
        S.op("dve", lambda e: e.scalar_tensor_tensor(out=v_[:], in0=rb_[:, 0:TT], scalar=w0, in1=v_[:], op0=ALU.mult, op1=ALU.add),
             reads=[rbB_, vB_, cstB], writes=[vB_])
        S.op("act", lambda e: e.activation(out=vb_[:], in_=v_[:], func=AF.Copy), reads=[vB_], writes=[vbB_])
        S.op("dve", lambda e: e.tensor_copy(out=haloR[:, j, :], in_=rb_[:, TT:TT + 3]), reads=[rbB_], writes=[haloRB[j]])
        S.op("dve", lambda e: e.scalar_tensor_tensor(out=sg_[:], in0=sg_[:], scalar=1.0, in1=brg[:, :], op0=ALU.add, op1=ALU.mult),
             reads=[sgB_, brgB], writes=[sgB_])
        yield

    g_ctr = [0]
    gstate = {}

    def G_gen(i, j):
        n = g_ctr[0]
        g_ctr[0] += 1
        tr_, trB_ = trb[n % 2], trbB[n % 2]
        ti_, tiB_ = tib[n % 2], tibB[n % 2]
        a_, aB_ = ab[n % 2], abB[n % 2]
        prs = [(gi, ii) for gi, (jj, ii) in enumerate(GPAIRS) if jj == j]
        reads = [wgB] + [rstate[ii][3] for _, ii in prs]
        rhs_l = [rstate[ii][2] for _, ii in prs]
        bks = []
        for g in range(2):
            bk, bkB = nb()
            bks.append((bk, bkB))

            def f(e, g=g, bk=bk):
                for t, (gi, ii) in enumerate(prs):
                    ins = e.matmul(bk[:, :], lhsT=wg[:, g, gi, :], rhs=rhs_l[t][:], start=(t == 0), stop=(t == len(prs) - 1))
                return ins
            S.op("pe", f, reads=reads, writes=[bkB])
        (br, brB), (bi, biB) = bks
        yield
        gstate[j] = (tr_, trB_, ti_, tiB_, a_, aB_)
        S.op("act", lambda e: e.activation(out=tr_[:], in_=br[:, :], func=AF.Tanh, scale=0.5, bias=prm[:, Q_RBA + j:Q_RBA + j + 1]),
             reads=[brB, prmB], writes=[trB_])
        S.op("act", lambda e: e.activation(out=ti_[:], in_=bi[:, :], func=AF.Tanh, scale=0.5, bias=prm[:, Q_RBX + j:Q_RBX + j + 1]),
             reads=[biB, prmB], writes=[tiB_])
        S.op("act", lambda e: e.activation(out=a_[:], in_=tr_[:], func=AF.Exp, scale=prm[:, Q_HCL + j:Q_HCL + j + 1],
                                           bias=prm[:, Q_HCL + j:Q_HCL + j + 1]), reads=[trB_, prmB], writes=[aB_])
        S.op("act", lambda e: e.activation(out=tr_[:], in_=tr_[:], func=AF.Exp, scale=prm[:, Q_CL + j:Q_CL + j + 1],
                                           bias=prm[:, Q_CL + j:Q_CL + j + 1]), reads=[trB_, prmB], writes=[trB_])
        yield
        S.op("act", lambda e: e.activation(out=tr_[:], in_=tr_[:], func=AF.Sqrt, scale=-1.0 / 16, bias=prm[:, Q_C16:Q_C16 + 1]),
             reads=[trB_, prmB], writes=[trB_])
        yield

    def G_dve(i, j):
        v_, vB_, vb_, vbB_, sg_, sgB_ = rstate[j]
        tr_, trB_, ti_, tiB_, a_, aB_ = gstate[j]
        S.op("dve", lambda e: e.scalar_tensor_tensor(out=ti_[:], in0=ti_[:], scalar=1.0, in1=v_[:], op0=ALU.add, op1=ALU.mult),
             reads=[tiB_, vB_], writes=[tiB_])
        S.op("dve", lambda e: e.tensor_tensor(out=ti_[:], in0=ti_[:], in1=tr_[:], op=ALU.mult), reads=[tiB_, trB_], writes=[tiB_])
        S.op("dve", lambda e: e.tensor_tensor_scan(out=tr_[:], data0=a_[:], data1=ti_[:], initial=hst[:, j:j + 1],
                                                   op0=ALU.mult, op1=ALU.add), reads=[aB_, tiB_, hstB[j], trB_], writes=[trB_])
        S.op("dve", lambda e: e.tensor_copy(out=hst[:, j:j + 1], in_=tr_[:, TT - 1:TT]), reads=[trB_], writes=[hstB[j]])
        S.op("dve", lambda e: e.tensor_tensor(out=prg[:, j, :], in0=tr_[:], in1=sg_[:], op=ALU.mult),
             reads=[trB_, sgB_], writes=[prgB[j]])

    m_ctr = [0]
    mstate = {}
    ga_ring = [(gab[0][:], gabB[0]), (gab[1][:], gabB[1]), (ubuf[1][:], ubufB[1]), (ubuf[0][:], ubufB[0]),
               (pbuf[0][:, 2:2 + TT], pbufB[0])]
    gb_ring = [(gbb[0][:], gbbB[0]), (gbb[1][:], gbbB[1]), (gbuf[1][:], gbufB[1]), (gbuf[0][:], gbufB[0]),
               (pbuf[1][:, 2:2 + TT], pbufB[1])]

    def M_step(i, j):
        pid = P_M0 + j // 2
        ci = 2 * (j % 2)
        n = m_ctr[0]
        m_ctr[0] += 1
        ga_, gaB_ = ga_ring[j % 5]
        gb_, gbB_ = gb_ring[j % 5]
        mstate[j] = (ga_, gaB_, gb_, gbB_)
        bma, bmaB = nb(); zmm(i, pid, ci, bma, bmaB)
        bmb, bmbB = nb(); zmm(i, pid, ci + 1, bmb, bmbB)
        if j % 2 == 1:
            release("w", pid)
        S.op("act", lambda e: e.activation(out=ga_, in_=bma[:, :], func=AF.Tanh, scale=0.5, bias=prm[:, Q_BM + j:Q_BM + j + 1]),
             reads=[bmaB, prmB], writes=[gaB_])
        S.op("act", lambda e: e.activation(out=gb_, in_=bmb[:, :], func=AF.Tanh, scale=0.5, bias=prm[:, Q_BM + 8 + j:Q_BM + 8 + j + 1]),
             reads=[bmbB, prmB], writes=[gbB_])
        sp_ = P_SC0 + j // 4
        slot, rb = get("w", sp_)
        off = (j % 4) * 128
        bya, byaB = nb()

        def f(e):
            for kc in range(8):
                ins = e.matmul(bya[:, :], lhsT=ring[:, slot, kc * 512 + off:kc * 512 + off + 128], rhs=pa[:, kc, :],
                               start=(kc == 0), stop=(kc == 7))
            return ins
        S.op("pe", f, reads=[rb] + paB, writes=[byaB])
        if j % 4 == 3:
            release("w", sp_)
        S.op("dve", lambda e: e.scalar_tensor_tensor(out=ga_, in0=ga_, scalar=1.0, in1=bya[:, :], op0=ALU.add, op1=ALU.mult),
             reads=[gaB_, byaB], writes=[gaB_])

    def YB_step(i, j):
        ga_, gaB_, gb_, gbB_ = mstate[j]
        rp = P_RG0 + j // 4
        slot, rb = get("w", rp)
        off = (j % 4) * 128
        byb, bybB = nb()

        def f(e):
            for kc in range(8):
                e.matmul(byb[:, :], lhsT=ring[:, slot, kc * 512 + off:kc * 512 + off + 128], rhs=prg[:, kc, :],
                         start=(kc == 0), stop=False)
            e.matmul(byb[:, :], lhsT=rgc[:, 0, j * 128:(j + 1) * 128], rhs=prg[:, 8, :], start=False, stop=False)
            return e.matmul(byb[:, :], lhsT=rgc[:, 1, j * 128:(j + 1) * 128], rhs=prg[:, 9, :], start=False, stop=True)
        S.op("pe", f, reads=[rb, rgcB] + prgB, writes=[bybB])
        if j % 4 == 3:
            release("w", rp)
        S.op("dve", lambda e: e.scalar_tensor_tensor(out=gb_, in0=gb_, scalar=1.0, in1=byb[:, :], op0=ALU.add, op1=ALU.mult),
             reads=[gbB_, bybB], writes=[gbB_])
        S.op("pool", lambda e: e.tensor_tensor(out=merged[:, j, :], in0=ga_, in1=gb_, op=ALU.add), reads=[gaB_, gbB_],
             writes=[mergedB[j]])

    o_ctr = [0]
    t1_gen = [0, 0]
    pending_fin = []
    out_tokens = []

    def flush_fin():
        while pending_fin:
            t1_, t1B_, k, gen, col, colB, r0 = pending_fin.pop(0)
            assert t1_gen[k] == gen, "t1 slot reused before deferred finalize"
            S.op("dve", (lambda e, t1_=t1_, col=col: e.scalar_tensor_tensor(out=t1_[:], in0=t1_[:], scalar=stat[:, col:col + 1],
                                                                             in1=gfin[:], op0=ALU.mult, op1=ALU.mult)),
                 reads=[t1B_, colB, gfinB], writes=[t1B_])
            tok = S.dma("sp", (lambda e, t1_=t1_, r0=r0: e.dma_start(out=out_d[r0:r0 + 128, :], in_=t1_[:])), reads=[t1B_])
            out_tokens.append(tok)

    ol_state = {}

    def O_load(i, s):
        xk = xb_ctr[0] % NXB
        xb, xbB = next_xb()
        xb_gen[xk] += 1
        r0 = i * TT + s * 128
        S.dma("sp", lambda e: e.dma_start(out=xb[:], in_=x_d[r0:r0 + 128, :]), writes=[xbB])
        ol_state[(i, s)] = (xb, xbB, xk, xb_gen[xk])

    def O_step(i, s):
        b = i // 4
        n = o_ctr[0]
        o_ctr[0] += 1
        k = n % 2
        t1_, t1B_ = t1[k], t1B[k]
        t1_gen[k] += 1
        r0 = i * TT + s * 128
        xb, xbB, xk, xgen = ol_state.pop((i, s))
        assert xb_gen[xk] == xgen
        halves = []
        for hh in range(2):
            bk, bkB = nb()
            halves.append((bk, bkB))

            def f(e, hh=hh, bk=bk):
                for kc in range(8):
                    ins = e.matmul(bk[:, :], lhsT=merged[:, kc, s * 128:(s + 1) * 128], rhs=wout[:, kc, hh * 512:(hh + 1) * 512],
                                   start=(kc == 0), stop=(kc == 7))
                return ins
            S.op("pe", f, reads=[woutB] + mergedB, writes=[bkB])
        for hh in range(2):
            bk, bkB = halves[hh]
            S.op("dve", (lambda e, hh=hh, bk=bk: e.tensor_tensor(out=t1_[:, hh * 512:(hh + 1) * 512], in0=bk[:, :],
                                                                  in1=gatebc[:, b, hh * 512:(hh + 1) * 512], op=ALU.mult)),
                 reads=[bkB, gatebcB], writes=[t1B_])
        S.op("dve", lambda e: e.tensor_tensor(out=t1_[:], in0=t1_[:], in1=xb[:], op=ALU.add), reads=[t1B_, xbB], writes=[t1B_])
        col, colB = next_stat()
        sumsq(t1_, t1B_, col, colB, xb[:], xbB)
        rstd_from(col, colB)
        flush_fin()
        pending_fin.append((t1_, t1B_, k, t1_gen[k], col, colB, r0))

    allh = haloAB + haloRB + hstB

    def seq_reset(e):
        e.memset(haloA[:].rearrange("p j k -> p (j k)"), 0.0)
        e.memset(haloR[:].rearrange("p j k -> p (j k)"), 0.0)
        return e.memset(hst[:], 0.0)

    def fin(g):
        for _ in g:
            pass

    pump()
    prologue_a()
    for s in range(4):
        prep_norm(0, s)
    flush_norm()
    prologue_b()
    for q in range(4):
        prep_tr(0, q)

    for i in range(NT):
        if i % 4 == 0:
            S.op("dve", seq_reset, writes=allh)
        for j in range(8):
            rg_ = R_gen(i, j); next(rg_)
            fin(rg_)
            gg = None
            if j >= 2:
                gg = G_gen(i, j - 2); next(gg)
            ag = A_gen(i, j); next(ag)
            if j >= 3:
                G_dve(i, j - 3)
            if gg is not None:
                next(gg)
            next(ag)
            next(ag)
            fin(ag)
            if gg is not None:
                fin(gg)
            if i > 0 and j < 4:
                O_step(i - 1, j)
                if j < 3:
                    O_load(i - 1, j + 1)
            if j == 4:
                flush_fin()
            if i + 1 < NT:
                if j >= 3 and j - 3 < 4:
                    P_load(i + 1, j - 3)
                if j >= 4:
                    P_rest(i + 1, j - 4)
        r8 = R_gen(i, 8); next(r8); fin(r8)
        g6 = G_gen(i, 6); next(g6)
        G_dve(i, 5)
        next(g6)
        r9 = R_gen(i, 9); next(r9); fin(r9)
        g7 = G_gen(i, 7); next(g7)
        next(g7)
        fin(g6)
        fin(g7)
        G_dve(i, 6)
        G_dve(i, 7)
        flush_norm()
        if i == 0:
            gate_bc()
        M_step(i, 0)
        g8 = G_gen(i, 8); next(g8)
        next(g8)
        M_step(i, 1)
        g9 = G_gen(i, 9); next(g9)
        next(g9)
        fin(g8)
        fin(g9)
        G_dve(i, 8)
        G_dve(i, 9)
        M_step(i, 2)
        M_step(i, 3)
        if i + 1 < NT:
            prep_tr(i + 1, 0)
        M_step(i, 4)
        if i + 1 < NT:
            prep_tr(i + 1, 1)
        for j in range(5, 8):
            YB_step(i, j - 5)
            M_step(i, j)
            if i + 1 < NT:
                if j == 5:
                    prep_tr(i + 1, 2)
                if j == 6:
                    prep_tr(i + 1, 3)
        for j in range(3, 8):
            if j == 6:
                O_load(i, 0)
            YB_step(i, j)
    for s in range(4):
        if s < 3:
            O_load(NT - 1, s + 1)
        O_step(NT - 1, s)
    flush_fin()
    S.wait_all("sp", out_tokens)
    S.emit()
    return nc


def _kmajor(w, ncols_piece=512):
    K, N = w.shape
    kc = K // 128
    a = w.reshape(kc, 128, N // ncols_piece, ncols_piece)
    return np.ascontiguousarray(a.transpose(2, 1, 0, 3)).reshape(N // ncols_piece, 128, kc * ncols_piece)


def _host_layout(inp):
    f = np.float32
    w_in = np.asarray(inp["w_in"][0], f)
    cols = np.concatenate([np.arange(c, c + 128) for pc in piece_columns() for c in pc])
    w_in_r = w_in[:, cols]
    wall = np.zeros((NPIECE, 128, 4096), f)
    wall[0:17] = _kmajor(w_in_r)
    wall[17:19] = _kmajor(np.asarray(inp["sc_w_out"][0], f))
    rgw = np.asarray(inp["rg_w_out"][0], f)
    wall[19:21] = _kmajor(rgw[0:1024])
    rgc = np.ascontiguousarray(rgw[1024:1280].reshape(2, 128, 1024).transpose(1, 0, 2)).reshape(128, 2048)
    wada = _kmajor(np.asarray(inp["w_ada"][0], f))
    wo = np.asarray(inp["w_out"][0], f)
    wout = np.ascontiguousarray(wo.reshape(8, 128, 1024).transpose(1, 0, 2)).reshape(128, 8192)
    wg = np.zeros((128, 2, NGP, 128), f)
    for g, key in enumerate(["rg_w_a", "rg_w_x"]):
        wh = np.asarray(inp[key][0], f)
        full = np.zeros((RGW, RGW), f)
        for h in range(16):
            full[HD * h:HD * h + HD, HD * h:HD * h + HD] = wh[h]
        for gi, (j, i) in enumerate(GPAIRS):
            wg[:, g, gi, :] = full[128 * i:128 * i + 128, 128 * j:128 * j + 128]
    wg = wg.reshape(128, 2 * NGP * 128)
    cst = np.zeros((128, NCST), f)
    cst[:, C_SCW:C_SCW + 24] = np.asarray(inp["sc_conv_w"][0], f).reshape(3, 8, 128).transpose(2, 1, 0).reshape(128, 24)
    cst[:, C_SCB:C_SCB + 8] = np.asarray(inp["sc_conv_b"][0], f).reshape(8, 128).T
    cst[:, C_RGW:C_RGW + 40] = np.asarray(inp["rg_conv_w"][0], f).reshape(4, 10, 128).transpose(2, 1, 0).reshape(128, 40)
    cst[:, C_RGB:C_RGB + 10] = np.asarray(inp["rg_conv_b"][0], f).reshape(10, 128).T
    cst[:, C_RBA:C_RBA + 10] = np.asarray(inp["rg_b_a"][0], f).reshape(10, 128).T
    cst[:, C_RBX:C_RBX + 10] = np.asarray(inp["rg_b_x"][0], f).reshape(10, 128).T
    cst[:, C_LAM:C_LAM + 10] = np.asarray(inp["rg_lambda"][0], f).reshape(10, 128).T
    cst[:, C_BM:C_BM + 16] = np.asarray(inp["b_merge"][0], f).reshape(2, 8, 128).transpose(2, 0, 1).reshape(128, 16)
    cst[:, C_GN:C_GN + 8] = np.asarray(inp["g_norm"][0], f).reshape(8, 128).T
    bada = np.asarray(inp["b_ada"][0], f)
    cst[:, C_BADA:C_BADA + 32] = np.repeat(bada[0:2048].reshape(16, 128).T[:, :, None], 2, axis=2).reshape(128, 32)
    rows = np.zeros((128, 2048), f)
    rows[:, 0:1024] = np.asarray(inp["g_final"], f)[None, :]
    rows[:, 1024:2048] = bada[None, 2048:3072]
    def padded(a, run):
        lead = a.shape[:-1]
        a = a.reshape(*lead, a.shape[-1] // run, run)
        out = np.zeros((*lead, a.shape[-2], run + PAD), f)
        out[..., :run] = a
        return out
    return dict(wall=padded(wall, 1024), wada=padded(wada, 1024), rgc=padded(rgc, 1024), wout=padded(wout, 1024),
                wg=padded(wg, 512), cst=cst, rows=rows)


_NC_CACHE = {}


def kernel(**inputs):
    x = np.asarray(inputs["x"], np.float32)
    c = np.asarray(inputs["c"], np.float32)
    shared = _host_layout(inputs)
    if "nc" not in _NC_CACHE:
        _NC_CACHE["nc"] = build_nc()
    nc = _NC_CACHE["nc"]
    in_maps = []
    for core in range(NCORES):
        xs = np.ascontiguousarray(x[2 * core:2 * core + 2].reshape(NT * TT, D))
        cc = c[2 * core:2 * core + 2]
        ct = np.ascontiguousarray(cc.T.reshape(8, 128, 2).transpose(1, 0, 2)).reshape(128, 16)
        m = dict(shared)
        m["x"] = xs
        m["ct"] = ct
        in_maps.append(m)
    res = run_bass_kernel_spmd(nc, in_maps, core_ids=list(range(NCORES)))
    out = np.stack([np.asarray(r["out"], np.float32).reshape(2, SEQ, D) for r in res.results], axis=0)
    return out.reshape(16, SEQ, D)
```

```python
import numpy as np
import concourse.bass as bass
import concourse.mybir as mybir
from concourse.bass_utils import run_bass_kernel_spmd

F32 = mybir.dt.float32
BF16 = mybir.dt.bfloat16
AF = mybir.ActivationFunctionType
ALU = mybir.AluOpType

ENGS = ("pe", "act", "dve", "pool", "sp")
NCORES = 8
D = 1024
SEQ = 2048
TT = 512
NT = 8
RGW = 1280
NRC = 10
HD = 80
EPS = 1e-6
NSLOT = 4
PAD = 16


class Buf:
    __slots__ = ("name", "w", "r", "dsem", "dcnt")

    def __init__(self, name):
        self.name = name
        self.w = None
        self.r = []
        self.dsem = None
        self.dcnt = 0


class Sched:
    def __init__(self, nc):
        self.nc = nc
        self.streams = {e: [] for e in ENGS}
        self.semobj = {}
        for e in ENGS:
            self.semobj["s_" + e] = nc.alloc_semaphore("s_" + e)
        self.tick = {e: 0 for e in ENGS}
        self.seen = {e: {} for e in ENGS}
        self.nbuf = 0

    def buf(self, name=None):
        self.nbuf += 1
        return Buf(f"{name or 'b'}_{self.nbuf}")

    def _need(self, e, tok, waits):
        if tok is None:
            return
        semkey, val, _ = tok
        if self.seen[e].get(semkey, 0) >= val:
            return
        if val > waits.get(semkey, 0):
            waits[semkey] = val

    def _deps(self, e, reads, writes):
        waits = {}
        for b in reads:
            if b.w is not None:
                self._need(e, b.w, waits)
        for b in writes:
            if b.w is not None and b.w[2] != e:
                self._need(e, b.w, waits)
            for t in b.r:
                if t[2] != e:
                    self._need(e, t, waits)
        for k, v in waits.items():
            self.seen[e][k] = v
            self.streams[e].append(("wait", k, v))

    def op(self, e, fn, reads=(), writes=()):
        for b in writes:
            if b.name.startswith("bank") and e == "pe" and b.w is not None and b.w[2] == "pe" and not b.r and b not in reads:
                raise AssertionError(f"PSUM {b.name} re-allocated before its consumers were emitted")
        self._deps(e, reads, writes)
        self.tick[e] += 1
        tok = ("s_" + e, self.tick[e], e)
        self.streams[e].append(("op", fn, "s_" + e, 1))
        for b in reads:
            b.r.append(tok)
        for b in writes:
            b.w = tok
            b.r = []
        return tok

    def dma(self, e, fn, reads=(), writes=(), track=None):
        self._deps(e, reads, writes)
        tb = track or (writes[0] if writes else reads[0])
        if tb.dsem is None:
            tb.dsem, tb.dcnt = {}, {}
        if e not in tb.dsem:
            tb.dsem[e] = f"d_{tb.name}_{e}"
            tb.dcnt[e] = 0
            self.semobj[tb.dsem[e]] = self.nc.alloc_semaphore(tb.dsem[e])
        tb.dcnt[e] += 16
        tok = (tb.dsem[e], tb.dcnt[e], "dma")
        self.streams[e].append(("op", fn, tb.dsem[e], 16))
        for b in reads:
            b.r.append(tok)
        for b in writes:
            b.w = tok
            b.r = []
        return tok

    def wait_all(self, e, toks):
        waits = {}
        for t in toks:
            self._need(e, t, waits)
        for k, v in waits.items():
            self.seen[e][k] = v
            self.streams[e].append(("wait", k, v))

    def emit(self):
        sched = self

        def run(e):
            def body(engine):
                for item in sched.streams[e]:
                    if item[0] == "wait":
                        engine.wait_ge(sched.semobj[item[1]], item[2])
                    else:
                        _, fn, semkey, inc = item
                        fn(engine).then_inc(sched.semobj[semkey], inc)
            return body

        with self.nc.Block() as block:
            block.tensor(run("pe"))
            block.scalar(run("act"))
            block.vector(run("dve"))
            block.gpsimd(run("pool"))
            block.sync(run("sp"))


OFF_B, OFF_C, OFF_V, OFF_G, OFF_RV, OFF_RG, OFF_MA, OFF_MB = 0, 1024, 2048, 3072, 4096, 5376, 6656, 7680


def piece_columns():
    pcs = []
    for j in range(8):
        pcs.append([OFF_C + 128 * j, OFF_V + 128 * j, OFF_G + 128 * j, OFF_B + 128 * j])
    for q in range(5):
        pcs.append([OFF_RV + 128 * (2 * q), OFF_RG + 128 * (2 * q),
                    OFF_RV + 128 * (2 * q + 1), OFF_RG + 128 * (2 * q + 1)])
    for q in range(4):
        pcs.append([OFF_MA + 128 * (2 * q), OFF_MB + 128 * (2 * q),
                    OFF_MA + 128 * (2 * q + 1), OFF_MB + 128 * (2 * q + 1)])
    return pcs


P_A0, P_R0, P_M0, P_SC0, P_RG0, NPIECE = 0, 8, 13, 17, 19, 21
P_GT0, NSCR = 21, 23


def gate_pairs():
    pairs = []
    for j in range(NRC):
        heads = set(range((128 * j) // HD, (128 * j + 127) // HD + 1))
        ins = set()
        for h in heads:
            for d in range(HD * h, HD * h + HD):
                ins.add(d // 128)
        for i in sorted(ins):
            pairs.append((j, i))
    return pairs


GPAIRS = gate_pairs()
NGP = len(GPAIRS)

C_SCW, C_SCB, C_RGW, C_RGB, C_RBA, C_RBX, C_LAM, C_BM, C_GN, C_BADA = 0, 24, 32, 72, 82, 92, 102, 112, 128, 136
NCST = 168


def tile_seq():
    s = []
    for j in range(8):
        s.append(P_A0 + j)
        if j % 2 == 0:
            s.append(P_R0 + j // 2)
    s.append(P_R0 + 4)
    s += [P_M0, P_SC0, P_M0 + 1, P_M0 + 2, P_SC0 + 1, P_RG0, P_M0 + 3, P_RG0 + 1]
    return s


def build_nc():
    nc = bass.Bass("TRN2", target_bir_lowering=False)
    x_d = nc.dram_tensor("x", [NT * TT, D], F32, kind="ExternalInput").ap()
    wall_d = nc.dram_tensor("wall", [NPIECE, 128, 4, 1024 + PAD], F32, kind="ExternalInput").ap()
    wada_d = nc.dram_tensor("wada", [6, 128, 4, 1024 + PAD], F32, kind="ExternalInput").ap()
    rgc_d = nc.dram_tensor("rgc", [128, 2, 1024 + PAD], F32, kind="ExternalInput").ap()
    wout_d = nc.dram_tensor("wout", [128, 8, 1024 + PAD], F32, kind="ExternalInput").ap()
    wg_d = nc.dram_tensor("wg", [128, NGP // 2, 512 + PAD], F32, kind="ExternalInput").ap()
    cst_d = nc.dram_tensor("cst", [128, NCST], F32, kind="ExternalInput").ap()
    rows_d = nc.dram_tensor("rows", [128, 2048], F32, kind="ExternalInput").ap()
    ct_d = nc.dram_tensor("ct", [128, 16], F32, kind="ExternalInput").ap()
    out_d = nc.dram_tensor("out", [NT * TT, D], F32, kind="ExternalOutput").ap()
    wsc_d = nc.dram_tensor("wsc", [NSCR, 128, 4096], BF16, kind="Internal").ap()

    S = Sched(nc)
    cnt = [0]

    def sb(shape, dtype, name):
        cnt[0] += 1
        return nc.alloc_sbuf_tensor(f"{name}_{cnt[0]}", list(shape), dtype)

    ring = sb([128, NSLOT, 4096], BF16, "ring")
    ringB = [S.buf(f"ring{k}") for k in range(NSLOT)]
    rgc = sb([128, 2, 1024], BF16, "rgc"); rgcB = S.buf("rgc")
    wout = sb([128, 8, 1024], BF16, "wout"); woutB = S.buf("wout")
    wg = sb([128, 2, NGP, 128], BF16, "wg"); wgB = S.buf("wg")
    hT = [sb([128, 8, TT], BF16, f"hT{k}") for k in range(2)]
    hTB = [S.buf(f"hT{k}") for k in range(2)]
    xn = sb([128, 4, D], BF16, "xn"); xnB = [S.buf(f"xn{s}") for s in range(4)]
    NXB = 3
    xbuf = [sb([128, D], F32, f"xb{k}") for k in range(NXB)]; xbufB = [S.buf(f"xb{k}") for k in range(NXB)]
    t1 = [sb([128, D], F32, f"t1_{k}") for k in range(2)]; t1B = [S.buf(f"t1_{k}") for k in range(2)]
    gatebc = sb([128, 2, D], F32, "gatebc"); gatebcB = S.buf("gatebc")
    gfin = sb([128, D], F32, "gfin"); gfinB = S.buf("gfin")
    NA = 2
    pbuf = [sb([128, 2 + TT], F32, f"pbuf{k}") for k in range(NA)]; pbufB = [S.buf(f"pbuf{k}") for k in range(NA)]
    ubuf = [sb([128, TT], F32, f"ubuf{k}") for k in range(NA)]; ubufB = [S.buf(f"ubuf{k}") for k in range(NA)]
    gbuf = [sb([128, TT], F32, f"gbuf{k}") for k in range(NA)]; gbufB = [S.buf(f"gbuf{k}") for k in range(NA)]
    pa = sb([128, 8, TT], BF16, "pa"); paB = [S.buf(f"pa{j}") for j in range(8)]
    prg = sb([128, NRC, TT], BF16, "prg"); prgB = [S.buf(f"prg{j}") for j in range(NRC)]
    rbuf = [sb([128, 3 + TT], F32, f"rbuf{k}") for k in range(2)]; rbufB = [S.buf(f"rbuf{k}") for k in range(2)]
    vbuf = [sb([128, TT], F32, f"vbuf{k}") for k in range(4)]; vbufB = [S.buf(f"vbuf{k}") for k in range(4)]
    vb = [sb([128, TT], BF16, f"vb{k}") for k in range(4)]; vbB = [S.buf(f"vb{k}") for k in range(4)]
    sgb = [sb([128, TT], F32, f"sgb{k}") for k in range(4)]; sgbB = [S.buf(f"sgb{k}") for k in range(4)]
    trb = [sb([128, TT], F32, f"trb{k}") for k in range(2)]; trbB = [S.buf(f"trb{k}") for k in range(2)]
    tib = [sb([128, TT], F32, f"tib{k}") for k in range(2)]; tibB = [S.buf(f"tib{k}") for k in range(2)]
    ab = [sb([128, TT], F32, f"ab{k}") for k in range(2)]; abB = [S.buf(f"ab{k}") for k in range(2)]
    gab = [sb([128, TT], F32, f"gab{k}") for k in range(2)]; gabB = [S.buf(f"gab{k}") for k in range(2)]
    gbb = [sb([128, TT], F32, f"gbb{k}") for k in range(2)]; gbbB = [S.buf(f"gbb{k}") for k in range(2)]
    merged = sb([128, 8, TT], BF16, "merged"); mergedB = [S.buf(f"mg{j}") for j in range(8)]
    ident = sb([128, 128], BF16, "ident"); identf = sb([128, 128], F32, "identf"); identB = S.buf("ident")
    ones = sb([128, 128], F32, "ones")
    mhalf = sb([128, 1], F32, "mhalf"); constB = S.buf("const")
    cst = sb([128, NCST], F32, "cst"); cstB = S.buf("cst")
    ctt = sb([128, 16], F32, "ctt"); cttB = S.buf("ctt")
    prm = sb([128, 128], F32, "prm"); prmB = S.buf("prm")
    Q_SCW, Q_SCB, Q_RBA, Q_RBX, Q_CL, Q_HCL, Q_BM, Q_G, Q_SH, Q_C16 = 0, 24, 32, 42, 52, 62, 72, 88, 104, 120
    slb = sb([128, 8, 2], BF16, "slb"); slB = S.buf("sl")

    def slrep(b, kc):
        return merged[:, b * 2 + kc // 4, (kc % 4) * 128:(kc % 4 + 1) * 128]
    ss = sb([128, 32], F32, "ss")
    haloA = sb([128, 8, 2], F32, "haloA"); haloAB = [S.buf(f"hA{j}") for j in range(8)]
    haloR = sb([128, NRC, 3], F32, "haloR"); haloRB = [S.buf(f"hR{j}") for j in range(NRC)]
    hst = sb([128, NRC], F32, "hst"); hstB = [S.buf(f"hs{j}") for j in range(NRC)]
    stat = sb([128, 32], F32, "stat"); statB = [S.buf(f"st{k}") for k in range(16)]

    NBK = 7
    banks = [nc.alloc_psum_tensor(f"bank{k}", [128, TT], F32) for k in range(NBK)]
    bankB = [S.buf(f"bank{k}") for k in range(NBK)]
    pst = nc.alloc_psum_tensor("pst", [128, 2, TT], BF16); pstB = S.buf("pst")
    bctr = [0]

    def nb():
        k = bctr[0] % NBK
        bctr[0] += 1
        return banks[k], bankB[k]

    wscB = [S.buf(f"wsc{p}") for p in range(NSCR)]

    seq = [("ada", k) for k in range(4)]
    def tile0_seq():
        ts = tile_seq()
        k = ts.index(P_M0)
        return ts[:k] + [P_GT0, P_GT0 + 1] + ts[k:]
    for i in range(NT):
        seq += [("w", p) for p in (tile0_seq() if i == 0 else tile_seq())]
    ring_state = {"next": 0, "released": set(), "where": {}}

    def pump():
        while ring_state["next"] < len(seq):
            n = ring_state["next"]
            if n >= NSLOT and (n - NSLOT) not in ring_state["released"]:
                break
            kind, p = seq[n]
            slot = n % NSLOT
            if kind == "ada":
                S.dma("pool", (lambda e, slot=slot, p=p: e.dma_start(out=ring[:, slot, :].rearrange("p (a n) -> p a n", a=4),
                                                                      in_=wada_d[p][:, :, 0:1024])),
                      writes=[ringB[slot]])
            else:
                S.dma("sp", (lambda e, slot=slot, p=p: e.dma_start(out=ring[:, slot, :], in_=wsc_d[p])),
                      reads=[wscB[p]], writes=[ringB[slot]])
            ring_state["where"][(kind, p)] = (slot, n)
            ring_state["next"] += 1

    def get(kind, p):
        key = (kind, p)
        assert key in ring_state["where"], f"piece {key} not loaded (ring order bug)"
        slot, n = ring_state["where"][key]
        return slot, ringB[slot]

    def release(kind, p):
        slot, n = ring_state["where"].pop((kind, p))
        ring_state["released"].add(n)
        pump()

    def prologue_a():
        S.dma("sp", lambda e: e.dma_start(out=cst[:], in_=cst_d), writes=[cstB])
        S.dma("sp", lambda e: e.dma_start(out=ctt[:], in_=ct_d), writes=[cttB])
        S.dma("sp", lambda e: e.dma_start(out=gfin[:], in_=rows_d[:, 0:1024]), writes=[gfinB])
        S.dma("sp", lambda e: e.dma_start(out=t1[0][:], in_=rows_d[:, 1024:2048]), writes=[t1B[0]])
        S.op("pool", lambda e: e.memset(identf[:], 0.0), writes=[identB])
        S.op("pool", lambda e: e.affine_select(out=identf[:], in_=identf[:], pattern=[[-1, 128]], compare_op=ALU.not_equal,
                                               fill=1.0, base=0, channel_multiplier=1), reads=[identB], writes=[identB])
        S.op("pool", lambda e: e.tensor_copy(out=ident[:], in_=identf[:]), reads=[identB], writes=[identB])

        def mk_const(e):
            e.memset(ones[:], 1.0)
            e.memset(mhalf[:], -0.5)
            return e.memset(stat[:], 0.0)
        S.op("pool", mk_const, writes=[constB] + statB)


    def prologue_b():
        order = tile0_seq()

        def cast_piece(p):
            src = wall_d[p] if p < NPIECE else wada_d[4 + p - P_GT0]
            S.dma("pool", (lambda e, p=p, src=src: e.dma_start(out=wsc_d[p].rearrange("p (a n) -> p a n", a=4), in_=src[:, :, 0:1024])),
                  writes=[wscB[p]])
        for k, p in enumerate(order):
            if k == 3:
                S.dma("pool", lambda e: e.dma_start(out=wg[:].rearrange("p a g e -> p (a g e)").rearrange("p (a n) -> p a n", n=512),
                                                    in_=wg_d[:, :, 0:512]), writes=[wgB])
            if p == P_RG0:
                S.dma("pool", lambda e: e.dma_start(out=rgc[:], in_=rgc_d[:, :, 0:1024]), writes=[rgcB])
            cast_piece(p)
        S.dma("pool", lambda e: e.dma_start(out=wout[:], in_=wout_d[:, :, 0:1024]), writes=[woutB])

        def mk_prm1(e):
            e.tensor_scalar(out=prm[:, Q_SCW:Q_SCW + 32], in0=cst[:, C_SCW:C_SCW + 32], scalar1=0.5, scalar2=None, op0=ALU.mult)
            e.tensor_scalar(out=prm[:, Q_RBA:Q_RBA + 20], in0=cst[:, C_RBA:C_RBA + 20], scalar1=0.5, scalar2=None, op0=ALU.mult)
            return e.tensor_scalar(out=prm[:, Q_BM:Q_BM + 16], in0=cst[:, C_BM:C_BM + 16], scalar1=0.5, scalar2=None, op0=ALU.mult)
        S.op("dve", mk_prm1, reads=[cstB], writes=[prmB])
        S.op("dve", lambda e: e.memset(prm[:, Q_C16:Q_C16 + 1], 1.0 / 16), writes=[prmB])
        tmpc = sb([128, 40], F32, "tmpc"); tmpcB = S.buf("tmpc")
        S.op("act", lambda e: e.activation(out=tmpc[:, 0:10], in_=cst[:, C_LAM:C_LAM + 10], func=AF.Abs),
             reads=[cstB], writes=[tmpcB])
        S.op("act", lambda e: e.activation(out=tmpc[:, 10:20], in_=tmpc[:, 0:10], func=AF.Exp, scale=-1.0),
             reads=[tmpcB], writes=[tmpcB])
        S.op("act", lambda e: e.activation(out=tmpc[:, 20:30], in_=tmpc[:, 10:20], func=AF.Ln, bias=1.0),
             reads=[tmpcB], writes=[tmpcB])

        S.op("dve", lambda e: e.tensor_scalar(out=tmpc[:, 30:40], in0=cst[:, C_LAM:C_LAM + 10], scalar1=-1.0, scalar2=0.0,
                                              op0=ALU.mult, op1=ALU.max), reads=[cstB, tmpcB], writes=[tmpcB])
        S.op("dve", lambda e: e.tensor_tensor(out=tmpc[:, 30:40], in0=tmpc[:, 30:40], in1=tmpc[:, 20:30], op=ALU.add),
             reads=[tmpcB], writes=[tmpcB])

        def mk_cl2(e):
            e.tensor_scalar(out=prm[:, Q_CL:Q_CL + 10], in0=tmpc[:, 30:40], scalar1=-8.0, scalar2=None, op0=ALU.mult)
            return e.tensor_scalar(out=prm[:, Q_HCL:Q_HCL + 10], in0=tmpc[:, 30:40], scalar1=-4.0, scalar2=None, op0=ALU.mult)
        S.op("dve", mk_cl2, reads=[tmpcB], writes=[prmB])

        slt = sb([128, 16], F32, "slt")
        S.op("act", lambda e: e.activation(out=slt[:], in_=ctt[:], func=AF.Tanh, scale=0.5), reads=[cttB], writes=[slB])
        S.op("dve", lambda e: e.scalar_tensor_tensor(out=slt[:], in0=slt[:], scalar=1.0, in1=ctt[:], op0=ALU.add, op1=ALU.mult),
             reads=[slB, cttB], writes=[slB])
        S.op("dve", lambda e: e.tensor_scalar(out=slt[:], in0=slt[:], scalar1=0.5, scalar2=None, op0=ALU.mult), reads=[slB], writes=[slB])
        S.op("dve", lambda e: e.tensor_copy(out=slb[:].rearrange("p k b -> p (k b)"), in_=slt[:]), reads=[slB], writes=[slB])

        def mk_slrep(e):
            for b in range(2):
                for kc in range(8):
                    ins = e.tensor_scalar(out=slrep(b, kc), in0=ones[:], scalar1=slt[:, kc * 2 + b:kc * 2 + b + 1],
                                          scalar2=None, op0=ALU.mult)
            return ins
        S.op("dve", mk_slrep, reads=[slB, constB], writes=[slB] + mergedB)

        bk_ss, bk_ssB = nb()

        def mm_ss(e, pc, slot):
            for mm in range(4):
                m = pc * 4 + mm
                for kc in range(8):
                    ins = e.matmul(bk_ss[:, m * 2:m * 2 + 2], lhsT=ring[:, slot, kc * 512 + mm * 128:kc * 512 + mm * 128 + 128],
                                   rhs=slb[:, kc, :], start=(kc == 0), stop=(kc == 7))
            return ins
        for pc in range(4):
            slot, rb = get("ada", pc)
            S.op("pe", (lambda e, pc=pc, slot=slot: mm_ss(e, pc, slot)), reads=[rb, slB] + ([bk_ssB] if pc else []), writes=[bk_ssB])
            release("ada", pc)
        S.op("dve", lambda e: e.tensor_tensor(out=ss[:], in0=bk_ss[:, 0:32], in1=cst[:, C_BADA:C_BADA + 32], op=ALU.add),
             reads=[bk_ssB, cstB], writes=[prmB])

        def mk_gsh(e):
            for b in range(2):
                e.scalar_tensor_tensor(out=prm[:, Q_G + 8 * b:Q_G + 8 * b + 8], in0=ss[:, 16 + b:32:2], scalar=1.0,
                                       in1=cst[:, C_GN:C_GN + 8], op0=ALU.add, op1=ALU.mult)
                ins = e.tensor_copy(out=prm[:, Q_SH + 8 * b:Q_SH + 8 * b + 8], in_=ss[:, b:16:2])
            return ins
        S.op("dve", mk_gsh, reads=[prmB, cstB], writes=[prmB])

    def gate_bc():
        for pc in range(2):
            slot, rb = get("w", P_GT0 + pc)
            for b in range(2):
                bk, bkB = nb()

                def mm_gate(e, slot=slot, b=b, bk=bk):
                    for kc in range(8):
                        ins = e.matmul(bk[:, :], lhsT=slrep(b, kc), rhs=ring[:, slot, kc * 512:(kc + 1) * 512],
                                       start=(kc == 0), stop=(kc == 7))
                    return ins
                S.op("pe", mm_gate, reads=[slB, rb] + mergedB, writes=[bkB])
                S.op("dve", (lambda e, bk=bk, b=b, pc=pc: e.tensor_tensor(out=gatebc[:, b, pc * 512:(pc + 1) * 512], in0=bk[:, :],
                                                                           in1=t1[0][:, pc * 512:(pc + 1) * 512], op=ALU.add)),
                     reads=[bkB, t1B[0]], writes=[gatebcB])
            release("w", P_GT0 + pc)
        S.op("dve", lambda e: e.tensor_scalar(out=gatebc[:].rearrange("p b n -> p (b n)"), in0=gatebc[:].rearrange("p b n -> p (b n)"),
                                              scalar1=0.5, scalar2=None, op0=ALU.mult), reads=[gatebcB], writes=[gatebcB, t1B[0]])

    xb_ctr = [0]
    st_ctr = [0]

    def next_xb():
        k = xb_ctr[0] % NXB
        xb_ctr[0] += 1
        return xbuf[k], xbufB[k]

    def next_stat():
        k = st_ctr[0] % 16
        st_ctr[0] += 1
        return k, statB[k]

    def rstd_from(col, colB):
        S.op("dve", lambda e: e.tensor_scalar(out=stat[:, col:col + 1], in0=stat[:, col:col + 1], scalar1=1.0 / D, scalar2=EPS,
                                              op0=ALU.mult, op1=ALU.add), reads=[colB], writes=[colB])
        S.op("pool", lambda e: e.tensor_tensor(out=stat[:, col:col + 1], in0=stat[:, col:col + 1], in1=mhalf[:], op=ALU.pow),
             reads=[colB, constB], writes=[colB])

    xb_gen = [0] * NXB
    pending_norm = []

    def flush_norm():
        while pending_norm:
            xb, xbB, k, gen, col, colB, s_ = pending_norm.pop(0)
            assert xb_gen[k] == gen, "xbuf slot reused before deferred normalize"
            S.op("dve", (lambda e, xb=xb, col=col, s_=s_: e.tensor_scalar(out=xn[:, s_, :], in0=xb[:], scalar1=stat[:, col:col + 1],
                                                                            scalar2=None, op0=ALU.mult)),
                 reads=[xbB, colB], writes=[xnB[s_]])

    def sumsq(src, srcB, col, colB, jk, jkB):
        S.op("dve", lambda e: e.memset(stat[:, col:col + 1], 0.0), writes=[colB])
        S.op("dve", lambda e: e.scalar_tensor_tensor(out=jk, in0=src[:], scalar=1.0, in1=src[:], op0=ALU.mult, op1=ALU.mult,
                                                     accum_out=stat[:, col:col + 1]), reads=[srcB, colB], writes=[jkB, colB])

    pl_state = {}

    def P_load(i, s):
        k = xb_ctr[0] % NXB
        xb, xbB = next_xb()
        xb_gen[k] += 1
        r0 = i * TT + s * 128
        S.dma("sp", lambda e: e.dma_start(out=xb[:], in_=x_d[r0:r0 + 128, :]), writes=[xbB])
        pl_state[(i, s)] = (xb, xbB, k, xb_gen[k])

    def P_rest(i, s):
        xb, xbB, k, gen = pl_state.pop((i, s))
        assert xb_gen[k] == gen
        col, colB = next_stat()
        sumsq(xb, xbB, col, colB, xn[:, s, :], xnB[s])
        rstd_from(col, colB)
        flush_norm()
        pending_norm.append((xb, xbB, k, gen, col, colB, s))

    def prep_norm(i, s):
        P_load(i, s)
        P_rest(i, s)

    def prep_tr(i, q):
        b = i // 4
        h = i % 2

        def tr(e):
            for kl in range(2):
                kc = 2 * q + kl
                for s in range(4):
                    ins = e.transpose(out=pst[:, kl, s * 128:(s + 1) * 128], in_=xn[:, s, kc * 128:(kc + 1) * 128], identity=ident[:])
            return ins
        S.op("pe", tr, reads=xnB + [identB], writes=[pstB])
        for kl in range(2):
            kc = 2 * q + kl
            S.op("act", (lambda e, kl=kl, kc=kc: e.activation(out=hT[h][:, kc, :], in_=pst[:, kl, :], func=AF.Identity,
                                                             scale=prm[:, Q_G + 8 * b + kc:Q_G + 8 * b + kc + 1],
                                                             bias=prm[:, Q_SH + 8 * b + kc:Q_SH + 8 * b + kc + 1])),
                 reads=[pstB, prmB], writes=[hTB[h]])

    def zmm(i, pid, ci, bk, bkB):
        slot, rb = get("w", pid)
        h = i % 2

        def f(e):
            for kc in range(8):
                ins = e.matmul(bk[:, :], lhsT=ring[:, slot, kc * 512 + ci * 128:kc * 512 + ci * 128 + 128], rhs=hT[h][:, kc, :],
                               start=(kc == 0), stop=(kc == 7))
            return ins
        S.op("pe", f, reads=[rb, hTB[h]], writes=[bkB])

    a_ctr = [0]

    def A_gen(i, j):
        pid = P_A0 + j
        k = a_ctr[0] % NA
        a_ctr[0] += 1
        pb, pbB, ub, ubB, gb, gbB = pbuf[k], pbufB[k], ubuf[k], ubufB[k], gbuf[k], gbufB[k]
        w0 = prm[:, Q_SCW + 3 * j + 0:Q_SCW + 3 * j + 1]
        w1 = prm[:, Q_SCW + 3 * j + 1:Q_SCW + 3 * j + 2]
        w2 = prm[:, Q_SCW + 3 * j + 2:Q_SCW + 3 * j + 3]
        bia = prm[:, Q_SCB + j:Q_SCB + j + 1]
        bc, bcB = nb(); zmm(i, pid, 0, bc, bcB)
        bv, bvB = nb(); zmm(i, pid, 1, bv, bvB)
        bg, bgB = nb(); zmm(i, pid, 2, bg, bgB)
        yield
        S.op("act", lambda e: e.activation(out=pb[:, 2:2 + TT], in_=bc[:, :], func=AF.Copy), reads=[bcB], writes=[pbB])
        S.op("dve", lambda e: e.tensor_copy(out=pb[:, 0:2], in_=haloA[:, j, :]), reads=[haloAB[j]], writes=[pbB])
        S.op("dve", lambda e: e.tensor_tensor(out=pb[:, 2:2 + TT], in0=bv[:, :], in1=pb[:, 2:2 + TT], op=ALU.mult),
             reads=[bvB, pbB], writes=[pbB])
        S.op("dve", lambda e: e.tensor_copy(out=haloA[:, j, :], in_=pb[:, TT:TT + 2]), reads=[pbB], writes=[haloAB[j]])
        S.op("act", lambda e: e.activation(out=ub[:], in_=pb[:, 2:2 + TT], func=AF.Identity, scale=w2, bias=bia),
             reads=[pbB, prmB], writes=[ubB])
        yield
        bb, bbB = nb(); zmm(i, pid, 3, bb, bbB)
        release("w", pid)
        yield
        S.op("act", lambda e: e.activation(out=gb[:], in_=bg[:, :], func=AF.Tanh, scale=0.5), reads=[bgB], writes=[gbB])
        S.op("dve", lambda e: e.scalar_tensor_tensor(out=ub[:], in0=pb[:, 1:1 + TT], scalar=w1, in1=ub[:], op0=ALU.mult, op1=ALU.add),
             reads=[pbB, ubB, prmB], writes=[ubB])
        S.op("dve", lambda e: e.scalar_tensor_tensor(out=ub[:], in0=pb[:, 0:TT], scalar=w0, in1=ub[:], op0=ALU.mult, op1=ALU.add),
             reads=[pbB, ubB, prmB], writes=[ubB])
        S.op("dve", lambda e: e.scalar_tensor_tensor(out=gb[:], in0=gb[:], scalar=1.0, in1=bg[:, :], op0=ALU.add, op1=ALU.mult),
             reads=[gbB, bgB], writes=[gbB])
        S.op("dve", lambda e: e.tensor_tensor(out=ub[:], in0=bb[:, :], in1=ub[:], op=ALU.mult), reads=[bbB, ubB], writes=[ubB])
        S.op("dve", lambda e: e.tensor_tensor(out=pa[:, j, :], in0=ub[:], in1=gb[:], op=ALU.mult), reads=[ubB, gbB], writes=[paB[j]])
        yield

    r_ctr = [0]
    rstate = {}

    def R_gen(i, j):
        pid = P_R0 + j // 2
        ci = 2 * (j % 2)
        n = r_ctr[0]
        r_ctr[0] += 1
        rb_, rbB_ = rbuf[n % 2], rbufB[n % 2]
        v_, vB_ = vbuf[n % 4], vbufB[n % 4]
        vb_, vbB_ = vb[n % 4], vbB[n % 4]
        sg_, sgB_ = sgb[n % 4], sgbB[n % 4]
        brv, brvB = nb(); zmm(i, pid, ci, brv, brvB)
        brg, brgB = nb(); zmm(i, pid, ci + 1, brg, brgB)
        if j % 2 == 1:
            release("w", pid)
        yield
        rstate[j] = (v_, vB_, vb_, vbB_, sg_, sgB_)
        w0 = cst[:, C_RGW + 4 * j + 0:C_RGW + 4 * j + 1]
        w1 = cst[:, C_RGW + 4 * j + 1:C_RGW + 4 * j + 2]
        w2 = cst[:, C_RGW + 4 * j + 2:C_RGW + 4 * j + 3]
        w3 = cst[:, C_RGW + 4 * j + 3:C_RGW + 4 * j + 4]
        bia = cst[:, C_RGB + j:C_RGB + j + 1]
        S.op("act", lambda e: e.activation(out=rb_[:, 3:3 + TT], in_=brv[:, :], func=AF.Copy), reads=[brvB], writes=[rbB_])
        S.op("act", lambda e: e.activation(out=v_[:], in_=brv[:, :], func=AF.Identity, scale=w3, bias=bia),
             reads=[brvB, cstB], writes=[vB_])
        S.op("act", lambda e: e.activation(out=sg_[:], in_=brg[:, :], func=AF.Tanh, scale=0.5), reads=[brgB], writes=[sgB_])
        S.op("dve", lambda e: e.tensor_copy(out=rb_[:, 0:3], in_=haloR[:, j, :]), reads=[haloRB[j]], writes=[rbB_])
        S.op("dve", lambda e: e.scalar_tensor_tensor(out=v_[:], in0=rb_[:, 2:2 + TT], scalar=w2, in1=v_[:], op0=ALU.mult, op1=ALU.add),
             reads=[rbB_, vB_, cstB], writes=[vB_])
        S.op("dve", lambda e: e.scalar_tensor_tensor(out=v_[:], in0=rb_[:, 1:1 + TT], scalar=w1, in1=v_[:], op0=ALU.mult, op1=ALU.add),
             reads=[rbB_, vB_, cstB], writes=[vB_])
        S.op("dve", lambda e: e.scalar_tensor_tensor(out=v_[:], in0=rb_[:, 0:TT], scalar=w0, in1=v_[:], op0=ALU.mult, op1=ALU.add),
             reads=[rbB_, vB_, cstB], writes=[vB_])
        S.op("act", lambda e: e.activation(out=vb_[:], in_=v_[:], func=AF.Copy), reads=[vB_], writes=[vbB_])
        S.op("dve", lambda e: e.tensor_copy(out=haloR[:, j, :], in_=rb_[:, TT:TT + 3]), reads=[rbB_], writes=[haloRB[j]])
        S.op("dve", lambda e: e.scalar_tensor_tensor(out=sg_[:], in0=sg_[:], scalar=1.0, in1=brg[:, :], op0=ALU.add, op1=ALU.mult),
             reads=[sgB_, brgB], writes=[sgB_])
        yield

    g_ctr = [0]
    gstate = {}

    def G_gen(i, j):
        n = g_ctr[0]
        g_ctr[0] += 1
        tr_, trB_ = trb[n % 2], trbB[n % 2]
        ti_, tiB_ = tib[n % 2], tibB[n % 2]
        a_, aB_ = ab[n % 2], abB[n % 2]
        prs = [(gi, ii) for gi, (jj, ii) in enumerate(GPAIRS) if jj == j]
        reads = [wgB] + [rstate[ii][3] for _, ii in prs]
        rhs_l = [rstate[ii][2] for _, ii in prs]
        bks = []
        for g in range(2):
            bk, bkB = nb()
            bks.append((bk, bkB))

            def f(e, g=g, bk=bk):
                for t, (gi, ii) in enumerate(prs):
                    ins = e.matmul(bk[:, :], lhsT=wg[:, g, gi, :], rhs=rhs_l[t][:], start=(t == 0), stop=(t == len(prs) - 1))
                return ins
            S.op("pe", f, reads=reads, writes=[bkB])
        (br, brB), (bi, biB) = bks
        yield
        gstate[j] = (tr_, trB_, ti_, tiB_, a_, aB_)
        S.op("act", lambda e: e.activation(out=tr_[:], in_=br[:, :], func=AF.Tanh, scale=0.5, bias=prm[:, Q_RBA + j:Q_RBA + j + 1]),
             reads=[brB, prmB], writes=[trB_])
        S.op("act", lambda e: e.activation(out=ti_[:], in_=bi[:, :], func=AF.Tanh, scale=0.5, bias=prm[:, Q_RBX + j:Q_RBX + j + 1]),
             reads=[biB, prmB], writes=[tiB_])
        S.op("act", lambda e: e.activation(out=a_[:], in_=tr_[:], func=AF.Exp, scale=prm[:, Q_HCL + j:Q_HCL + j + 1],
                                           bias=prm[:, Q_HCL + j:Q_HCL + j + 1]), reads=[trB_, prmB], writes=[aB_])
        S.op("act", lambda e: e.activation(out=tr_[:], in_=tr_[:], func=AF.Exp, scale=prm[:, Q_CL + j:Q_CL + j + 1],
                                           bias=prm[:, Q_CL + j:Q_CL + j + 1]), reads=[trB_, prmB], writes=[trB_])
        yield
        S.op("act", lambda e: e.activation(out=tr_[:], in_=tr_[:], func=AF.Sqrt, scale=-1.0 / 16, bias=prm[:, Q_C16:Q_C16 + 1]),
             reads=[trB_, prmB], writes=[trB_])
        yield

    def G_dve(i, j):
        v_, vB_, vb_, vbB_, sg_, sgB_ = rstate[j]
        tr_, trB_, ti_, tiB_, a_, aB_ = gstate[j]
        S.op("dve", lambda e: e.scalar_tensor_tensor(out=ti_[:], in0=ti_[:], scalar=1.0, in1=v_[:], op0=ALU.add, op1=ALU.mult),
             reads=[tiB_, vB_], writes=[tiB_])
        S.op("dve", lambda e: e.tensor_tensor(out=ti_[:], in0=ti_[:], in1=tr_[:], op=ALU.mult), reads=[tiB_, trB_], writes=[tiB_])
        S.op("dve", lambda e: e.tensor_tensor_scan(out=tr_[:], data0=a_[:], data1=ti_[:], initial=hst[:, j:j + 1],
                                                   op0=ALU.mult, op1=ALU.add), reads=[aB_, tiB_, hstB[j], trB_], writes=[trB_])
        S.op("dve", lambda e: e.tensor_copy(out=hst[:, j:j + 1], in_=tr_[:, TT - 1:TT]), reads=[trB_], writes=[hstB[j]])
        S.op("dve", lambda e: e.tensor_tensor(out=prg[:, j, :], in0=tr_[:], in1=sg_[:], op=ALU.mult),
             reads=[trB_, sgB_], writes=[prgB[j]])

    m_ctr = [0]
    mstate = {}
    ga_ring = [(gab[0][:], gabB[0]), (gab[1][:], gabB[1]), (ubuf[1][:], ubufB[1]), (ubuf[0][:], ubufB[0]),
               (pbuf[0][:, 2:2 + TT], pbufB[0])]
    gb_ring = [(gbb[0][:], gbbB[0]), (gbb[1][:], gbbB[1]), (gbuf[1][:], gbufB[1]), (gbuf[0][:], gbufB[0]),
               (pbuf[1][:, 2:2 + TT], pbufB[1])]

    def M_step(i, j):
        pid = P_M0 + j // 2
        ci = 2 * (j % 2)
        n = m_ctr[0]
        m_ctr[0] += 1
        ga_, gaB_ = ga_ring[j % 5]
        gb_, gbB_ = gb_ring[j % 5]
        mstate[j] = (ga_, gaB_, gb_, gbB_)
        bma, bmaB = nb(); zmm(i, pid, ci, bma, bmaB)
        bmb, bmbB = nb(); zmm(i, pid, ci + 1, bmb, bmbB)
        if j % 2 == 1:
            release("w", pid)
        S.op("act", lambda e: e.activation(out=ga_, in_=bma[:, :], func=AF.Tanh, scale=0.5, bias=prm[:, Q_BM + j:Q_BM + j + 1]),
             reads=[bmaB, prmB], writes=[gaB_])
        S.op("act", lambda e: e.activation(out=gb_, in_=bmb[:, :], func=AF.Tanh, scale=0.5, bias=prm[:, Q_BM + 8 + j:Q_BM + 8 + j + 1]),
             reads=[bmbB, prmB], writes=[gbB_])
        sp_ = P_SC0 + j // 4
        slot, rb = get("w", sp_)
        off = (j % 4) * 128
        bya, byaB = nb()

        def f(e):
            for kc in range(8):
                ins = e.matmul(bya[:, :], lhsT=ring[:, slot, kc * 512 + off:kc * 512 + off + 128], rhs=pa[:, kc, :],
                               start=(kc == 0), stop=(kc == 7))
            return ins
        S.op("pe", f, reads=[rb] + paB, writes=[byaB])
        if j % 4 == 3:
            release("w", sp_)
        S.op("dve", lambda e: e.scalar_tensor_tensor(out=ga_, in0=ga_, scalar=1.0, in1=bya[:, :], op0=ALU.add, op1=ALU.mult),
             reads=[gaB_, byaB], writes=[gaB_])

    def YB_step(i, j):
        ga_, gaB_, gb_, gbB_ = mstate[j]
        rp = P_RG0 + j // 4
        slot, rb = get("w", rp)
        off = (j % 4) * 128
        byb, bybB = nb()

        def f(e):
            for kc in range(8):
                e.matmul(byb[:, :], lhsT=ring[:, slot, kc * 512 + off:kc * 512 + off + 128], rhs=prg[:, kc, :],
                         start=(kc == 0), stop=False)
            e.matmul(byb[:, :], lhsT=rgc[:, 0, j * 128:(j + 1) * 128], rhs=prg[:, 8, :], start=False, stop=False)
            return e.matmul(byb[:, :], lhsT=rgc[:, 1, j * 128:(j + 1) * 128], rhs=prg[:, 9, :], start=False, stop=True)
        S.op("pe", f, reads=[rb, rgcB] + prgB, writes=[bybB])
        if j % 4 == 3:
            release("w", rp)
        S.op("dve", lambda e: e.scalar_tensor_tensor(out=gb_, in0=gb_, scalar=1.0, in1=byb[:, :], op0=ALU.add, op1=ALU.mult),
             reads=[gbB_, bybB], writes=[gbB_])
        S.op("pool", lambda e: e.tensor_tensor(out=merged[:, j, :], in0=ga_, in1=gb_, op=ALU.add), reads=[gaB_, gbB_],
             writes=[mergedB[j]])

    o_ctr = [0]
    t1_gen = [0, 0]
    pending_fin = []
    out_tokens = []

    def flush_fin():
        while pending_fin:
            t1_, t1B_, k, gen, col, colB, r0 = pending_fin.pop(0)
            assert t1_gen[k] == gen, "t1 slot reused before deferred finalize"
            S.op("dve", (lambda e, t1_=t1_, col=col: e.scalar_tensor_tensor(out=t1_[:], in0=t1_[:], scalar=stat[:, col:col + 1],
                                                                             in1=gfin[:], op0=ALU.mult, op1=ALU.mult)),
                 reads=[t1B_, colB, gfinB], writes=[t1B_])
            tok = S.dma("sp", (lambda e, t1_=t1_, r0=r0: e.dma_start(out=out_d[r0:r0 + 128, :], in_=t1_[:])), reads=[t1B_])
            out_tokens.append(tok)

    ol_state = {}

    def O_load(i, s):
        xk = xb_ctr[0] % NXB
        xb, xbB = next_xb()
        xb_gen[xk] += 1
        r0 = i * TT + s * 128
        S.dma("sp", lambda e: e.dma_start(out=xb[:], in_=x_d[r0:r0 + 128, :]), writes=[xbB])
        ol_state[(i, s)] = (xb, xbB, xk, xb_gen[xk])

    def O_step(i, s):
        b = i // 4
        n = o_ctr[0]
        o_ctr[0] += 1
        k = n % 2
        t1_, t1B_ = t1[k], t1B[k]
        t1_gen[k] += 1
        r0 = i * TT + s * 128
        xb, xbB, xk, xgen = ol_state.pop((i, s))
        assert xb_gen[xk] == xgen
        halves = []
        for hh in range(2):
            bk, bkB = nb()
            halves.append((bk, bkB))

            def f(e, hh=hh, bk=bk):
                for kc in range(8):
                    ins = e.matmul(bk[:, :], lhsT=merged[:, kc, s * 128:(s + 1) * 128], rhs=wout[:, kc, hh * 512:(hh + 1) * 512],
                                   start=(kc == 0), stop=(kc == 7))
                return ins
            S.op("pe", f, reads=[woutB] + mergedB, writes=[bkB])
        for hh in range(2):
            bk, bkB = halves[hh]
            S.op("dve", (lambda e, hh=hh, bk=bk: e.tensor_tensor(out=t1_[:, hh * 512:(hh + 1) * 512], in0=bk[:, :],
                                                                  in1=gatebc[:, b, hh * 512:(hh + 1) * 512], op=ALU.mult)),
                 reads=[bkB, gatebcB], writes=[t1B_])
        S.op("dve", lambda e: e.tensor_tensor(out=t1_[:], in0=t1_[:], in1=xb[:], op=ALU.add), reads=[t1B_, xbB], writes=[t1B_])
        col, colB = next_stat()
        sumsq(t1_, t1B_, col, colB, xb[:], xbB)
        rstd_from(col, colB)
        flush_fin()
        pending_fin.append((t1_, t1B_, k, t1_gen[k], col, colB, r0))

    allh = haloAB + haloRB + hstB

    def seq_reset(e):
        e.memset(haloA[:].rearrange("p j k -> p (j k)"), 0.0)
        e.memset(haloR[:].rearrange("p j k -> p (j k)"), 0.0)
        return e.memset(hst[:], 0.0)

    def fin(g):
        for _ in g:
            pass

    pump()
    prologue_a()
    for s in range(4):
        prep_norm(0, s)
    flush_norm()
    prologue_b()
    for q in range(4):
        prep_tr(0, q)

    for i in range(NT):
        if i % 4 == 0:
            S.op("dve", seq_reset, writes=allh)
        for j in range(8):
            rg_ = R_gen(i, j); next(rg_)
            fin(rg_)
            gg = None
            if j >= 2:
                gg = G_gen(i, j - 2); next(gg)
            ag = A_gen(i, j); next(ag)
            if j >= 3:
                G_dve(i, j - 3)
            if gg is not None:
                next(gg)
            next(ag)
            next(ag)
            fin(ag)
            if gg is not None:
                fin(gg)
            if i > 0 and j < 4:
                O_step(i - 1, j)
                if j < 3:
                    O_load(i - 1, j + 1)
            if j == 4:
                flush_fin()
            if i + 1 < NT and j >= 5:
                P_load(i + 1, j - 5)
        r8 = R_gen(i, 8); next(r8); fin(r8)
        g6 = G_gen(i, 6); next(g6)
        G_dve(i, 5)
        fin(g6)
        r9 = R_gen(i, 9); next(r9); fin(r9)
        g7 = G_gen(i, 7); next(g7)
        G_dve(i, 6)
        fin(g7)
        nxt = i + 1 < NT
        if nxt:
            P_rest(i + 1, 0)
        if i == 0:
            gate_bc()
        M_step(i, 0)
        if nxt:
            P_rest(i + 1, 1)
            P_load(i + 1, 3)
        g8 = G_gen(i, 8); next(g8)
        G_dve(i, 7)
        fin(g8)
        M_step(i, 1)
        if nxt:
            P_rest(i + 1, 2)
        g9 = G_gen(i, 9); next(g9)
        G_dve(i, 8)
        fin(g9)
        G_dve(i, 9)
        M_step(i, 2)
        if nxt:
            P_rest(i + 1, 3)
        M_step(i, 3)
        flush_norm()
        M_step(i, 4)
        if nxt:
            prep_tr(i + 1, 0)
        for j in range(5, 8):
            YB_step(i, j - 5)
            M_step(i, j)
            if nxt:
                prep_tr(i + 1, j - 4)
        for j in range(3, 8):
            if j == 6:
                O_load(i, 0)
            YB_step(i, j)
    for s in range(4):
        if s < 3:
            O_load(NT - 1, s + 1)
        O_step(NT - 1, s)
    flush_fin()
    S.wait_all("sp", out_tokens)
    S.emit()
    return nc


def _kmajor(w, ncols_piece=512):
    K, N = w.shape
    kc = K // 128
    a = w.reshape(kc, 128, N // ncols_piece, ncols_piece)
    return np.ascontiguousarray(a.transpose(2, 1, 0, 3)).reshape(N // ncols_piece, 128, kc * ncols_piece)


def _host_layout(inp):
    f = np.float32
    w_in = np.asarray(inp["w_in"][0], f)
    cols = np.concatenate([np.arange(c, c + 128) for pc in piece_columns() for c in pc])
    w_in_r = w_in[:, cols]
    wall = np.zeros((NPIECE, 128, 4096), f)
    wall[0:17] = _kmajor(w_in_r)
    wall[17:19] = _kmajor(np.asarray(inp["sc_w_out"][0], f))
    rgw = np.asarray(inp["rg_w_out"][0], f)
    wall[19:21] = _kmajor(rgw[0:1024])
    rgc = np.ascontiguousarray(rgw[1024:1280].reshape(2, 128, 1024).transpose(1, 0, 2)).reshape(128, 2048)
    wada = _kmajor(np.asarray(inp["w_ada"][0], f))
    wo = np.asarray(inp["w_out"][0], f)
    wout = np.ascontiguousarray(wo.reshape(8, 128, 1024).transpose(1, 0, 2)).reshape(128, 8192)
    wg = np.zeros((128, 2, NGP, 128), f)
    for g, key in enumerate(["rg_w_a", "rg_w_x"]):
        wh = np.asarray(inp[key][0], f)
        full = np.zeros((RGW, RGW), f)
        for h in range(16):
            full[HD * h:HD * h + HD, HD * h:HD * h + HD] = wh[h]
        for gi, (j, i) in enumerate(GPAIRS):
            wg[:, g, gi, :] = full[128 * i:128 * i + 128, 128 * j:128 * j + 128]
    wg = wg.reshape(128, 2 * NGP * 128)
    cst = np.zeros((128, NCST), f)
    cst[:, C_SCW:C_SCW + 24] = np.asarray(inp["sc_conv_w"][0], f).reshape(3, 8, 128).transpose(2, 1, 0).reshape(128, 24)
    cst[:, C_SCB:C_SCB + 8] = np.asarray(inp["sc_conv_b"][0], f).reshape(8, 128).T
    cst[:, C_RGW:C_RGW + 40] = np.asarray(inp["rg_conv_w"][0], f).reshape(4, 10, 128).transpose(2, 1, 0).reshape(128, 40)
    cst[:, C_RGB:C_RGB + 10] = np.asarray(inp["rg_conv_b"][0], f).reshape(10, 128).T
    cst[:, C_RBA:C_RBA + 10] = np.asarray(inp["rg_b_a"][0], f).reshape(10, 128).T
    cst[:, C_RBX:C_RBX + 10] = np.asarray(inp["rg_b_x"][0], f).reshape(10, 128).T
    cst[:, C_LAM:C_LAM + 10] = np.asarray(inp["rg_lambda"][0], f).reshape(10, 128).T
    cst[:, C_BM:C_BM + 16] = np.asarray(inp["b_merge"][0], f).reshape(2, 8, 128).transpose(2, 0, 1).reshape(128, 16)
    cst[:, C_GN:C_GN + 8] = np.asarray(inp["g_norm"][0], f).reshape(8, 128).T
    bada = np.asarray(inp["b_ada"][0], f)
    cst[:, C_BADA:C_BADA + 32] = np.repeat(bada[0:2048].reshape(16, 128).T[:, :, None], 2, axis=2).reshape(128, 32)
    rows = np.zeros((128, 2048), f)
    rows[:, 0:1024] = np.asarray(inp["g_final"], f)[None, :]
    rows[:, 1024:2048] = bada[None, 2048:3072]
    def padded(a, run):
        lead = a.shape[:-1]
        a = a.reshape(*lead, a.shape[-1] // run, run)
        out = np.zeros((*lead, a.shape[-2], run + PAD), f)
        out[..., :run] = a
        return out
    return dict(wall=padded(wall, 1024), wada=padded(wada, 1024), rgc=padded(rgc, 1024), wout=padded(wout, 1024),
                wg=padded(wg, 512), cst=cst, rows=rows)


_NC_CACHE = {}


def kernel(**inputs):
    x = np.asarray(inputs["x"], np.float32)
    c = np.asarray(inputs["c"], np.float32)
    shared = _host_layout(inputs)
    if "nc" not in _NC_CACHE:
        _NC_CACHE["nc"] = build_nc()
    nc = _NC_CACHE["nc"]
    in_maps = []
    for core in range(NCORES):
        xs = np.ascontiguousarray(x[2 * core:2 * core + 2].reshape(NT * TT, D))
        cc = c[2 * core:2 * core + 2]
        ct = np.ascontiguousarray(cc.T.reshape(8, 128, 2).transpose(1, 0, 2)).reshape(128, 16)
        m = dict(shared)
        m["x"] = xs
        m["ct"] = ct
        in_maps.append(m)
    res = run_bass_kernel_spmd(nc, in_maps, core_ids=list(range(NCORES)))
    out = np.stack([np.asarray(r["out"], np.float32).reshape(2, SEQ, D) for r in res.results], axis=0)
    return out.reshape(16, SEQ, D)
```

```python
import numpy as np
import concourse.bass as bass
import concourse.mybir as mybir
from concourse.bass_utils import run_bass_kernel_spmd

F32 = mybir.dt.float32
BF16 = mybir.dt.bfloat16
AF = mybir.ActivationFunctionType
ALU = mybir.AluOpType

ENGS = ("pe", "act", "dve", "pool", "sp")
NCORES = 8
D = 1024
SEQ = 2048
TT = 512
NT = 8
RGW = 1280
NRC = 10
HD = 80
EPS = 1e-6
NSLOT = 4
PAD = 16


class Buf:
    __slots__ = ("name", "w", "r", "dsem", "dcnt")

    def __init__(self, name):
        self.name = name
        self.w = None
        self.r = []
        self.dsem = None
        self.dcnt = 0


class Sched:
    def __init__(self, nc):
        self.nc = nc
        self.streams = {e: [] for e in ENGS}
        self.semobj = {}
        for e in ENGS:
            self.semobj["s_" + e] = nc.alloc_semaphore("s_" + e)
        self.tick = {e: 0 for e in ENGS}
        self.seen = {e: {} for e in ENGS}
        self.nbuf = 0

    def buf(self, name=None):
        self.nbuf += 1
        return Buf(f"{name or 'b'}_{self.nbuf}")

    def _need(self, e, tok, waits):
        if tok is None:
            return
        semkey, val, _ = tok
        if self.seen[e].get(semkey, 0) >= val:
            return
        if val > waits.get(semkey, 0):
            waits[semkey] = val

    def _deps(self, e, reads, writes):
        waits = {}
        for b in reads:
            if b.w is not None:
                self._need(e, b.w, waits)
        for b in writes:
            if b.w is not None and b.w[2] != e:
                self._need(e, b.w, waits)
            for t in b.r:
                if t[2] != e:
                    self._need(e, t, waits)
        for k, v in waits.items():
            self.seen[e][k] = v
            self.streams[e].append(("wait", k, v))

    def op(self, e, fn, reads=(), writes=()):
        for b in writes:
            if b.name.startswith("bank") and e == "pe" and b.w is not None and b.w[2] == "pe" and not b.r and b not in reads:
                raise AssertionError(f"PSUM {b.name} re-allocated before its consumers were emitted")
        self._deps(e, reads, writes)
        self.tick[e] += 1
        tok = ("s_" + e, self.tick[e], e)
        self.streams[e].append(("op", fn, "s_" + e, 1))
        for b in reads:
            b.r.append(tok)
        for b in writes:
            b.w = tok
            b.r = []
        return tok

    def dma(self, e, fn, reads=(), writes=(), track=None):
        self._deps(e, reads, writes)
        tb = track or (writes[0] if writes else reads[0])
        if tb.dsem is None:
            tb.dsem, tb.dcnt = {}, {}
        if e not in tb.dsem:
            tb.dsem[e] = f"d_{tb.name}_{e}"
            tb.dcnt[e] = 0
            self.semobj[tb.dsem[e]] = self.nc.alloc_semaphore(tb.dsem[e])
        tb.dcnt[e] += 16
        tok = (tb.dsem[e], tb.dcnt[e], "dma")
        self.streams[e].append(("op", fn, tb.dsem[e], 16))
        for b in reads:
            b.r.append(tok)
        for b in writes:
            b.w = tok
            b.r = []
        return tok

    def wait_all(self, e, toks):
        waits = {}
        for t in toks:
            self._need(e, t, waits)
        for k, v in waits.items():
            self.seen[e][k] = v
            self.streams[e].append(("wait", k, v))

    def emit(self):
        sched = self

        def run(e):
            def body(engine):
                for item in sched.streams[e]:
                    if item[0] == "wait":
                        engine.wait_ge(sched.semobj[item[1]], item[2])
                    else:
                        _, fn, semkey, inc = item
                        fn(engine).then_inc(sched.semobj[semkey], inc)
            return body

        with self.nc.Block() as block:
            block.tensor(run("pe"))
            block.scalar(run("act"))
            block.vector(run("dve"))
            block.gpsimd(run("pool"))
            block.sync(run("sp"))


OFF_B, OFF_C, OFF_V, OFF_G, OFF_RV, OFF_RG, OFF_MA, OFF_MB = 0, 1024, 2048, 3072, 4096, 5376, 6656, 7680


def piece_columns():
    pcs = []
    for j in range(8):
        pcs.append([OFF_C + 128 * j, OFF_V + 128 * j, OFF_G + 128 * j, OFF_B + 128 * j])
    for q in range(5):
        pcs.append([OFF_RV + 128 * (2 * q), OFF_RG + 128 * (2 * q),
                    OFF_RV + 128 * (2 * q + 1), OFF_RG + 128 * (2 * q + 1)])
    for q in range(4):
        pcs.append([OFF_MA + 128 * (2 * q), OFF_MB + 128 * (2 * q),
                    OFF_MA + 128 * (2 * q + 1), OFF_MB + 128 * (2 * q + 1)])
    return pcs


P_A0, P_R0, P_M0, P_SC0, P_RG0, NPIECE = 0, 8, 13, 17, 19, 21
P_GT0, NSCR = 21, 23


def gate_pairs():
    pairs = []
    for j in range(NRC):
        heads = set(range((128 * j) // HD, (128 * j + 127) // HD + 1))
        ins = set()
        for h in heads:
            for d in range(HD * h, HD * h + HD):
                ins.add(d // 128)
        for i in sorted(ins):
            pairs.append((j, i))
    return pairs


GPAIRS = gate_pairs()
NGP = len(GPAIRS)

C_SCW, C_SCB, C_RGW, C_RGB, C_RBA, C_RBX, C_LAM, C_BM, C_GN, C_BADA = 0, 24, 32, 72, 82, 92, 102, 112, 128, 136
NCST = 168


def tile_seq():
    s = []
    for j in range(8):
        s.append(P_A0 + j)
        if j % 2 == 0:
            s.append(P_R0 + j // 2)
    s.append(P_R0 + 4)
    s += [P_M0, P_SC0, P_M0 + 1, P_M0 + 2, P_SC0 + 1, P_RG0, P_M0 + 3, P_RG0 + 1]
    return s


def build_nc():
    nc = bass.Bass("TRN2", target_bir_lowering=False)
    x_d = nc.dram_tensor("x", [NT * TT, D], F32, kind="ExternalInput").ap()
    wall_d = nc.dram_tensor("wall", [NPIECE, 128, 4, 1024 + PAD], F32, kind="ExternalInput").ap()
    wada_d = nc.dram_tensor("wada", [6, 128, 4, 1024 + PAD], F32, kind="ExternalInput").ap()
    rgc_d = nc.dram_tensor("rgc", [128, 2, 1024 + PAD], F32, kind="ExternalInput").ap()
    wout_d = nc.dram_tensor("wout", [128, 8, 1024 + PAD], F32, kind="ExternalInput").ap()
    wg_d = nc.dram_tensor("wg", [128, NGP // 2, 512 + PAD], F32, kind="ExternalInput").ap()
    cst_d = nc.dram_tensor("cst", [128, NCST], F32, kind="ExternalInput").ap()
    rows_d = nc.dram_tensor("rows", [128, 2048], F32, kind="ExternalInput").ap()
    ct_d = nc.dram_tensor("ct", [128, 16], F32, kind="ExternalInput").ap()
    out_d = nc.dram_tensor("out", [NT * TT, D], F32, kind="ExternalOutput").ap()
    wsc_d = nc.dram_tensor("wsc", [NSCR, 128, 4096], BF16, kind="Internal").ap()

    S = Sched(nc)
    cnt = [0]

    def sb(shape, dtype, name):
        cnt[0] += 1
        return nc.alloc_sbuf_tensor(f"{name}_{cnt[0]}", list(shape), dtype)

    ring = sb([128, NSLOT, 4096], BF16, "ring")
    ringB = [S.buf(f"ring{k}") for k in range(NSLOT)]
    rgc = sb([128, 2, 1024], BF16, "rgc"); rgcB = S.buf("rgc")
    wout = sb([128, 8, 1024], BF16, "wout"); woutB = S.buf("wout")
    wg = sb([128, 2, NGP, 128], BF16, "wg"); wgB = S.buf("wg")
    hT = [sb([128, 8, TT], BF16, f"hT{k}") for k in range(2)]
    hTB = [S.buf(f"hT{k}") for k in range(2)]
    xn = sb([128, 4, D], BF16, "xn"); xnB = [S.buf(f"xn{s}") for s in range(4)]
    NXB = 3
    xbuf = [sb([128, D], F32, f"xb{k}") for k in range(NXB)]; xbufB = [S.buf(f"xb{k}") for k in range(NXB)]
    t1 = [sb([128, D], F32, f"t1_{k}") for k in range(2)]; t1B = [S.buf(f"t1_{k}") for k in range(2)]
    gatebc = sb([128, 2, D], F32, "gatebc"); gatebcB = S.buf("gatebc")
    gfin = sb([128, D], F32, "gfin"); gfinB = S.buf("gfin")
    NA = 2
    pbuf = [sb([128, 2 + TT], F32, f"pbuf{k}") for k in range(NA)]; pbufB = [S.buf(f"pbuf{k}") for k in range(NA)]
    ubuf = [sb([128, TT], F32, f"ubuf{k}") for k in range(NA)]; ubufB = [S.buf(f"ubuf{k}") for k in range(NA)]
    gbuf = [sb([128, TT], F32, f"gbuf{k}") for k in range(NA)]; gbufB = [S.buf(f"gbuf{k}") for k in range(NA)]
    pa = sb([128, 8, TT], BF16, "pa"); paB = [S.buf(f"pa{j}") for j in range(8)]
    prg = sb([128, NRC, TT], BF16, "prg"); prgB = [S.buf(f"prg{j}") for j in range(NRC)]
    rbuf = [sb([128, 3 + TT], F32, f"rbuf{k}") for k in range(2)]; rbufB = [S.buf(f"rbuf{k}") for k in range(2)]
    vbuf = [sb([128, TT], F32, f"vbuf{k}") for k in range(4)]; vbufB = [S.buf(f"vbuf{k}") for k in range(4)]
    vb = [sb([128, TT], BF16, f"vb{k}") for k in range(4)]; vbB = [S.buf(f"vb{k}") for k in range(4)]
    sgb = [sb([128, TT], F32, f"sgb{k}") for k in range(4)]; sgbB = [S.buf(f"sgb{k}") for k in range(4)]
    trb = [sb([128, TT], F32, f"trb{k}") for k in range(2)]; trbB = [S.buf(f"trb{k}") for k in range(2)]
    tib = [sb([128, TT], F32, f"tib{k}") for k in range(2)]; tibB = [S.buf(f"tib{k}") for k in range(2)]
    ab = [sb([128, TT], F32, f"ab{k}") for k in range(2)]; abB = [S.buf(f"ab{k}") for k in range(2)]
    gab = [sb([128, TT], F32, f"gab{k}") for k in range(2)]; gabB = [S.buf(f"gab{k}") for k in range(2)]
    gbb = [sb([128, TT], F32, f"gbb{k}") for k in range(2)]; gbbB = [S.buf(f"gbb{k}") for k in range(2)]
    merged = sb([128, 8, TT], BF16, "merged"); mergedB = [S.buf(f"mg{j}") for j in range(8)]
    ident = sb([128, 128], BF16, "ident"); identf = sb([128, 128], F32, "identf"); identB = S.buf("ident")
    ones = sb([128, 128], F32, "ones")
    mhalf = sb([128, 1], F32, "mhalf"); constB = S.buf("const")
    cst = sb([128, NCST], F32, "cst"); cstB = S.buf("cst")
    ctt = sb([128, 16], F32, "ctt"); cttB = S.buf("ctt")
    prm = sb([128, 128], F32, "prm"); prmB = S.buf("prm")
    Q_SCW, Q_SCB, Q_RBA, Q_RBX, Q_CL, Q_HCL, Q_BM, Q_G, Q_SH, Q_C16 = 0, 24, 32, 42, 52, 62, 72, 88, 104, 120
    slb = sb([128, 8, 2], BF16, "slb"); slB = S.buf("sl")

    def slrep(b, kc):
        return merged[:, b * 2 + kc // 4, (kc % 4) * 128:(kc % 4 + 1) * 128]
    ss = sb([128, 32], F32, "ss")
    haloA = sb([128, 8, 2], F32, "haloA"); haloAB = [S.buf(f"hA{j}") for j in range(8)]
    haloR = sb([128, NRC, 3], F32, "haloR"); haloRB = [S.buf(f"hR{j}") for j in range(NRC)]
    hst = sb([128, NRC], F32, "hst"); hstB = [S.buf(f"hs{j}") for j in range(NRC)]
    stat = sb([128, 32], F32, "stat"); statB = [S.buf(f"st{k}") for k in range(16)]

    NBK = 7
    banks = [nc.alloc_psum_tensor(f"bank{k}", [128, TT], F32) for k in range(NBK)]
    bankB = [S.buf(f"bank{k}") for k in range(NBK)]
    pst = nc.alloc_psum_tensor("pst", [128, 2, TT], BF16); pstB = S.buf("pst")
    bctr = [0]

    def nb():
        k = bctr[0] % NBK
        bctr[0] += 1
        return banks[k], bankB[k]

    wscB = [S.buf(f"wsc{p}") for p in range(NSCR)]

    seq = [("ada", k) for k in range(4)]
    def tile0_seq():
        ts = tile_seq()
        k = ts.index(P_M0)
        return ts[:k] + [P_GT0, P_GT0 + 1] + ts[k:]
    for i in range(NT):
        seq += [("w", p) for p in (tile0_seq() if i == 0 else tile_seq())]
    ring_state = {"next": 0, "released": set(), "where": {}}

    def pump():
        while ring_state["next"] < len(seq):
            n = ring_state["next"]
            if n >= NSLOT and (n - NSLOT) not in ring_state["released"]:
                break
            kind, p = seq[n]
            slot = n % NSLOT
            if kind == "ada":
                S.dma("pool", (lambda e, slot=slot, p=p: e.dma_start(out=ring[:, slot, :].rearrange("p (a n) -> p a n", a=4),
                                                                      in_=wada_d[p][:, :, 0:1024])),
                      writes=[ringB[slot]])
            else:
                S.dma("sp", (lambda e, slot=slot, p=p: e.dma_start(out=ring[:, slot, :], in_=wsc_d[p])),
                      reads=[wscB[p]], writes=[ringB[slot]])
            ring_state["where"][(kind, p)] = (slot, n)
            ring_state["next"] += 1

    def get(kind, p):
        key = (kind, p)
        assert key in ring_state["where"], f"piece {key} not loaded (ring order bug)"
        slot, n = ring_state["where"][key]
        return slot, ringB[slot]

    def release(kind, p):
        slot, n = ring_state["where"].pop((kind, p))
        ring_state["released"].add(n)
        pump()

    def prologue_a():
        S.dma("sp", lambda e: e.dma_start(out=cst[:], in_=cst_d), writes=[cstB])
        S.dma("sp", lambda e: e.dma_start(out=ctt[:], in_=ct_d), writes=[cttB])
        S.dma("sp", lambda e: e.dma_start(out=gfin[:], in_=rows_d[:, 0:1024]), writes=[gfinB])
        S.dma("sp", lambda e: e.dma_start(out=t1[0][:], in_=rows_d[:, 1024:2048]), writes=[t1B[0]])
        S.op("pool", lambda e: e.memset(identf[:], 0.0), writes=[identB])
        S.op("pool", lambda e: e.affine_select(out=identf[:], in_=identf[:], pattern=[[-1, 128]], compare_op=ALU.not_equal,
                                               fill=1.0, base=0, channel_multiplier=1), reads=[identB], writes=[identB])
        S.op("pool", lambda e: e.tensor_copy(out=ident[:], in_=identf[:]), reads=[identB], writes=[identB])

        def mk_const(e):
            e.memset(ones[:], 1.0)
            e.memset(mhalf[:], -0.5)
            return e.memset(stat[:], 0.0)
        S.op("pool", mk_const, writes=[constB] + statB)


    def prologue_b():
        order = tile0_seq()

        def cast_piece(p):
            src = wall_d[p] if p < NPIECE else wada_d[4 + p - P_GT0]
            S.dma("pool", (lambda e, p=p, src=src: e.dma_start(out=wsc_d[p].rearrange("p (a n) -> p a n", a=4), in_=src[:, :, 0:1024])),
                  writes=[wscB[p]])
        for k, p in enumerate(order):
            if k == 3:
                S.dma("pool", lambda e: e.dma_start(out=wg[:].rearrange("p a g e -> p (a g e)").rearrange("p (a n) -> p a n", n=512),
                                                    in_=wg_d[:, :, 0:512]), writes=[wgB])
            if p == P_RG0:
                S.dma("pool", lambda e: e.dma_start(out=rgc[:], in_=rgc_d[:, :, 0:1024]), writes=[rgcB])
            cast_piece(p)
        S.dma("pool", lambda e: e.dma_start(out=wout[:], in_=wout_d[:, :, 0:1024]), writes=[woutB])

        def mk_prm1(e):
            e.tensor_scalar(out=prm[:, Q_SCW:Q_SCW + 32], in0=cst[:, C_SCW:C_SCW + 32], scalar1=0.5, scalar2=None, op0=ALU.mult)
            e.tensor_scalar(out=prm[:, Q_RBA:Q_RBA + 20], in0=cst[:, C_RBA:C_RBA + 20], scalar1=0.5, scalar2=None, op0=ALU.mult)
            return e.tensor_scalar(out=prm[:, Q_BM:Q_BM + 16], in0=cst[:, C_BM:C_BM + 16], scalar1=0.5, scalar2=None, op0=ALU.mult)
        S.op("dve", mk_prm1, reads=[cstB], writes=[prmB])
        S.op("dve", lambda e: e.memset(prm[:, Q_C16:Q_C16 + 1], 1.0 / 16), writes=[prmB])
        tmpc = sb([128, 40], F32, "tmpc"); tmpcB = S.buf("tmpc")
        S.op("act", lambda e: e.activation(out=tmpc[:, 0:10], in_=cst[:, C_LAM:C_LAM + 10], func=AF.Abs),
             reads=[cstB], writes=[tmpcB])
        S.op("act", lambda e: e.activation(out=tmpc[:, 10:20], in_=tmpc[:, 0:10], func=AF.Exp, scale=-1.0),
             reads=[tmpcB], writes=[tmpcB])
        S.op("act", lambda e: e.activation(out=tmpc[:, 20:30], in_=tmpc[:, 10:20], func=AF.Ln, bias=1.0),
             reads=[tmpcB], writes=[tmpcB])

        S.op("dve", lambda e: e.tensor_scalar(out=tmpc[:, 30:40], in0=cst[:, C_LAM:C_LAM + 10], scalar1=-1.0, scalar2=0.0,
                                              op0=ALU.mult, op1=ALU.max), reads=[cstB, tmpcB], writes=[tmpcB])
        S.op("dve", lambda e: e.tensor_tensor(out=tmpc[:, 30:40], in0=tmpc[:, 30:40], in1=tmpc[:, 20:30], op=ALU.add),
             reads=[tmpcB], writes=[tmpcB])

        def mk_cl2(e):
            e.tensor_scalar(out=prm[:, Q_CL:Q_CL + 10], in0=tmpc[:, 30:40], scalar1=-8.0, scalar2=None, op0=ALU.mult)
            return e.tensor_scalar(out=prm[:, Q_HCL:Q_HCL + 10], in0=tmpc[:, 30:40], scalar1=-4.0, scalar2=None, op0=ALU.mult)
        S.op("dve", mk_cl2, reads=[tmpcB], writes=[prmB])

        slt = sb([128, 16], F32, "slt")
        S.op("act", lambda e: e.activation(out=slt[:], in_=ctt[:], func=AF.Tanh, scale=0.5), reads=[cttB], writes=[slB])
        S.op("dve", lambda e: e.scalar_tensor_tensor(out=slt[:], in0=slt[:], scalar=1.0, in1=ctt[:], op0=ALU.add, op1=ALU.mult),
             reads=[slB, cttB], writes=[slB])
        S.op("dve", lambda e: e.tensor_scalar(out=slt[:], in0=slt[:], scalar1=0.5, scalar2=None, op0=ALU.mult), reads=[slB], writes=[slB])
        S.op("dve", lambda e: e.tensor_copy(out=slb[:].rearrange("p k b -> p (k b)"), in_=slt[:]), reads=[slB], writes=[slB])

        def mk_slrep(e):
            for b in range(2):
                for kc in range(8):
                    ins = e.tensor_scalar(out=slrep(b, kc), in0=ones[:], scalar1=slt[:, kc * 2 + b:kc * 2 + b + 1],
                                          scalar2=None, op0=ALU.mult)
            return ins
        S.op("dve", mk_slrep, reads=[slB, constB], writes=[slB] + mergedB)

        bk_ss, bk_ssB = nb()

        def mm_ss(e, pc, slot):
            for mm in range(4):
                m = pc * 4 + mm
                for kc in range(8):
                    ins = e.matmul(bk_ss[:, m * 2:m * 2 + 2], lhsT=ring[:, slot, kc * 512 + mm * 128:kc * 512 + mm * 128 + 128],
                                   rhs=slb[:, kc, :], start=(kc == 0), stop=(kc == 7))
            return ins
        for pc in range(4):
            slot, rb = get("ada", pc)
            S.op("pe", (lambda e, pc=pc, slot=slot: mm_ss(e, pc, slot)), reads=[rb, slB] + ([bk_ssB] if pc else []), writes=[bk_ssB])
            release("ada", pc)
        S.op("dve", lambda e: e.tensor_tensor(out=ss[:], in0=bk_ss[:, 0:32], in1=cst[:, C_BADA:C_BADA + 32], op=ALU.add),
             reads=[bk_ssB, cstB], writes=[prmB])

        def mk_gsh(e):
            for b in range(2):
                e.scalar_tensor_tensor(out=prm[:, Q_G + 8 * b:Q_G + 8 * b + 8], in0=ss[:, 16 + b:32:2], scalar=1.0,
                                       in1=cst[:, C_GN:C_GN + 8], op0=ALU.add, op1=ALU.mult)
                ins = e.tensor_copy(out=prm[:, Q_SH + 8 * b:Q_SH + 8 * b + 8], in_=ss[:, b:16:2])
            return ins
        S.op("dve", mk_gsh, reads=[prmB, cstB], writes=[prmB])

    def gate_bc():
        for pc in range(2):
            slot, rb = get("w", P_GT0 + pc)
            for b in range(2):
                bk, bkB = nb()

                def mm_gate(e, slot=slot, b=b, bk=bk):
                    for kc in range(8):
                        ins = e.matmul(bk[:, :], lhsT=slrep(b, kc), rhs=ring[:, slot, kc * 512:(kc + 1) * 512],
                                       start=(kc == 0), stop=(kc == 7))
                    return ins
                S.op("pe", mm_gate, reads=[slB, rb] + mergedB, writes=[bkB])
                S.op("dve", (lambda e, bk=bk, b=b, pc=pc: e.tensor_tensor(out=gatebc[:, b, pc * 512:(pc + 1) * 512], in0=bk[:, :],
                                                                           in1=t1[0][:, pc * 512:(pc + 1) * 512], op=ALU.add)),
                     reads=[bkB, t1B[0]], writes=[gatebcB])
            release("w", P_GT0 + pc)
        S.op("dve", lambda e: e.tensor_scalar(out=gatebc[:].rearrange("p b n -> p (b n)"), in0=gatebc[:].rearrange("p b n -> p (b n)"),
                                              scalar1=0.5, scalar2=None, op0=ALU.mult), reads=[gatebcB], writes=[gatebcB, t1B[0]])

    xb_ctr = [0]
    st_ctr = [0]

    def next_xb():
        k = xb_ctr[0] % NXB
        xb_ctr[0] += 1
        return xbuf[k], xbufB[k]

    def next_stat():
        k = st_ctr[0] % 16
        st_ctr[0] += 1
        return k, statB[k]

    def rstd_from(col, colB):
        S.op("dve", lambda e: e.tensor_scalar(out=stat[:, col:col + 1], in0=stat[:, col:col + 1], scalar1=1.0 / D, scalar2=EPS,
                                              op0=ALU.mult, op1=ALU.add), reads=[colB], writes=[colB])
        S.op("pool", lambda e: e.tensor_tensor(out=stat[:, col:col + 1], in0=stat[:, col:col + 1], in1=mhalf[:], op=ALU.pow),
             reads=[colB, constB], writes=[colB])

    xb_gen = [0] * NXB
    pending_norm = []

    def flush_norm():
        while pending_norm:
            xb, xbB, k, gen, col, colB, s_ = pending_norm.pop(0)
            assert xb_gen[k] == gen, "xbuf slot reused before deferred normalize"
            S.op("dve", (lambda e, xb=xb, col=col, s_=s_: e.tensor_scalar(out=xn[:, s_, :], in0=xb[:], scalar1=stat[:, col:col + 1],
                                                                            scalar2=None, op0=ALU.mult)),
                 reads=[xbB, colB], writes=[xnB[s_]])

    def sumsq(src, srcB, col, colB, jk, jkB):
        S.op("dve", lambda e: e.memset(stat[:, col:col + 1], 0.0), writes=[colB])
        S.op("dve", lambda e: e.scalar_tensor_tensor(out=jk, in0=src[:], scalar=1.0, in1=src[:], op0=ALU.mult, op1=ALU.mult,
                                                     accum_out=stat[:, col:col + 1]), reads=[srcB, colB], writes=[jkB, colB])

    pl_state = {}

    def P_load(i, s):
        k = xb_ctr[0] % NXB
        xb, xbB = next_xb()
        xb_gen[k] += 1
        r0 = i * TT + s * 128
        S.dma("sp", lambda e: e.dma_start(out=xb[:], in_=x_d[r0:r0 + 128, :]), writes=[xbB])
        pl_state[(i, s)] = (xb, xbB, k, xb_gen[k])

    def P_rest(i, s):
        xb, xbB, k, gen = pl_state.pop((i, s))
        assert xb_gen[k] == gen
        col, colB = next_stat()
        sumsq(xb, xbB, col, colB, xn[:, s, :], xnB[s])
        rstd_from(col, colB)
        flush_norm()
        pending_norm.append((xb, xbB, k, gen, col, colB, s))

    def prep_norm(i, s):
        P_load(i, s)
        P_rest(i, s)

    def prep_tr(i, q):
        b = i // 4
        h = i % 2

        def tr(e):
            for kl in range(2):
                kc = 2 * q + kl
                for s in range(4):
                    ins = e.transpose(out=pst[:, kl, s * 128:(s + 1) * 128], in_=xn[:, s, kc * 128:(kc + 1) * 128], identity=ident[:])
            return ins
        S.op("pe", tr, reads=xnB + [identB], writes=[pstB])
        for kl in range(2):
            kc = 2 * q + kl
            S.op("act", (lambda e, kl=kl, kc=kc: e.activation(out=hT[h][:, kc, :], in_=pst[:, kl, :], func=AF.Identity,
                                                             scale=prm[:, Q_G + 8 * b + kc:Q_G + 8 * b + kc + 1],
                                                             bias=prm[:, Q_SH + 8 * b + kc:Q_SH + 8 * b + kc + 1])),
                 reads=[pstB, prmB], writes=[hTB[h]])

    def zmm(i, pid, ci, bk, bkB):
        slot, rb = get("w", pid)
        h = i % 2

        def f(e):
            for kc in range(8):
                ins = e.matmul(bk[:, :], lhsT=ring[:, slot, kc * 512 + ci * 128:kc * 512 + ci * 128 + 128], rhs=hT[h][:, kc, :],
                               start=(kc == 0), stop=(kc == 7))
            return ins
        S.op("pe", f, reads=[rb, hTB[h]], writes=[bkB])

    a_ctr = [0]

    def A_gen(i, j):
        pid = P_A0 + j
        k = a_ctr[0] % NA
        a_ctr[0] += 1
        pb, pbB, ub, ubB, gb, gbB = pbuf[k], pbufB[k], ubuf[k], ubufB[k], gbuf[k], gbufB[k]
        w0 = prm[:, Q_SCW + 3 * j + 0:Q_SCW + 3 * j + 1]
        w1 = prm[:, Q_SCW + 3 * j + 1:Q_SCW + 3 * j + 2]
        w2 = prm[:, Q_SCW + 3 * j + 2:Q_SCW + 3 * j + 3]
        bia = prm[:, Q_SCB + j:Q_SCB + j + 1]
        bc, bcB = nb(); zmm(i, pid, 0, bc, bcB)
        bv, bvB = nb(); zmm(i, pid, 1, bv, bvB)
        bg, bgB = nb(); zmm(i, pid, 2, bg, bgB)
        yield
        S.op("act", lambda e: e.activation(out=pb[:, 2:2 + TT], in_=bc[:, :], func=AF.Copy), reads=[bcB], writes=[pbB])
        S.op("dve", lambda e: e.tensor_copy(out=pb[:, 0:2], in_=haloA[:, j, :]), reads=[haloAB[j]], writes=[pbB])
        S.op("dve", lambda e: e.tensor_tensor(out=pb[:, 2:2 + TT], in0=bv[:, :], in1=pb[:, 2:2 + TT], op=ALU.mult),
             reads=[bvB, pbB], writes=[pbB])
        S.op("dve", lambda e: e.tensor_copy(out=haloA[:, j, :], in_=pb[:, TT:TT + 2]), reads=[pbB], writes=[haloAB[j]])
        S.op("act", lambda e: e.activation(out=ub[:], in_=pb[:, 2:2 + TT], func=AF.Identity, scale=w2, bias=bia),
             reads=[pbB, prmB], writes=[ubB])
        yield
        bb, bbB = nb(); zmm(i, pid, 3, bb, bbB)
        release("w", pid)
        yield
        S.op("act", lambda e: e.activation(out=gb[:], in_=bg[:, :], func=AF.Tanh, scale=0.5), reads=[bgB], writes=[gbB])
        S.op("dve", lambda e: e.scalar_tensor_tensor(out=ub[:], in0=pb[:, 1:1 + TT], scalar=w1, in1=ub[:], op0=ALU.mult, op1=ALU.add),
             reads=[pbB, ubB, prmB], writes=[ubB])
        S.op("dve", lambda e: e.scalar_tensor_tensor(out=ub[:], in0=pb[:, 0:TT], scalar=w0, in1=ub[:], op0=ALU.mult, op1=ALU.add),
             reads=[pbB, ubB, prmB], writes=[ubB])
        S.op("dve", lambda e: e.scalar_tensor_tensor(out=gb[:], in0=gb[:], scalar=1.0, in1=bg[:, :], op0=ALU.add, op1=ALU.mult),
             reads=[gbB, bgB], writes=[gbB])
        S.op("dve", lambda e: e.tensor_tensor(out=ub[:], in0=bb[:, :], in1=ub[:], op=ALU.mult), reads=[bbB, ubB], writes=[ubB])
        S.op("dve", lambda e: e.tensor_tensor(out=pa[:, j, :], in0=ub[:], in1=gb[:], op=ALU.mult), reads=[ubB, gbB], writes=[paB[j]])
        yield

    r_ctr = [0]
    rstate = {}

    def R_gen(i, j):
        pid = P_R0 + j // 2
        ci = 2 * (j % 2)
        n = r_ctr[0]
        r_ctr[0] += 1
        rb_, rbB_ = rbuf[n % 2], rbufB[n % 2]
        v_, vB_ = vbuf[n % 4], vbufB[n % 4]
        vb_, vbB_ = vb[n % 4], vbB[n % 4]
        sg_, sgB_ = sgb[n % 4], sgbB[n % 4]
        brv, brvB = nb(); zmm(i, pid, ci, brv, brvB)
        brg, brgB = nb(); zmm(i, pid, ci + 1, brg, brgB)
        if j % 2 == 1:
            release("w", pid)
        yield
        rstate[j] = (v_, vB_, vb_, vbB_, sg_, sgB_)
        w0 = cst[:, C_RGW + 4 * j + 0:C_RGW + 4 * j + 1]
        w1 = cst[:, C_RGW + 4 * j + 1:C_RGW + 4 * j + 2]
        w2 = cst[:, C_RGW + 4 * j + 2:C_RGW + 4 * j + 3]
        w3 = cst[:, C_RGW + 4 * j + 3:C_RGW + 4 * j + 4]
        bia = cst[:, C_RGB + j:C_RGB + j + 1]
        S.op("act", lambda e: e.activation(out=rb_[:, 3:3 + TT], in_=brv[:, :], func=AF.Copy), reads=[brvB], writes=[rbB_])
        S.op("act", lambda e: e.activation(out=v_[:], in_=brv[:, :], func=AF.Identity, scale=w3, bias=bia),
             reads=[brvB, cstB], writes=[vB_])
        S.op("act", lambda e: e.activation(out=sg_[:], in_=brg[:, :], func=AF.Tanh, scale=0.5), reads=[brgB], writes=[sgB_])
        S.op("dve", lambda e: e.tensor_copy(out=rb_[:, 0:3], in_=haloR[:, j, :]), reads=[haloRB[j]], writes=[rbB_])
        S.op("dve", lambda e: e.scalar_tensor_tensor(out=v_[:], in0=rb_[:, 2:2 + TT], scalar=w2, in1=v_[:], op0=ALU.mult, op1=ALU.add),
             reads=[rbB_, vB_, cstB], writes=[vB_])
        S.op("dve", lambda e: e.scalar_tensor_tensor(out=v_[:], in0=rb_[:, 1:1 + TT], scalar=w1, in1=v_[:], op0=ALU.mult, op1=ALU.add),
             reads=[rbB_, vB_, cstB], writes=[vB_])
        S.op("dve", lambda e: e.scalar_tensor_tensor(out=v_[:], in0=rb_[:, 0:TT], scalar=w0, in1=v_[:], op0=ALU.mult, op1=ALU.add),
             reads=[rbB_, vB_, cstB], writes=[vB_])
        S.op("act", lambda e: e.activation(out=vb_[:], in_=v_[:], func=AF.Copy), reads=[vB_], writes=[vbB_])
        S.op("dve", lambda e: e.tensor_copy(out=haloR[:, j, :], in_=rb_[:, TT:TT + 3]), reads=[rbB_], writes=[haloRB[j]])
        S.op("dve", lambda e: e.scalar_tensor_tensor(out=sg_[:], in0=sg_[:], scalar=1.0, in1=brg[:, :], op0=ALU.add, op1=ALU.mult),
             reads=[sgB_, brgB], writes=[sgB_])
        yield

    g_ctr = [0]
    gstate = {}

    def G_gen(i, j):
        n = g_ctr[0]
        g_ctr[0] += 1
        tr_, trB_ = trb[n % 2], trbB[n % 2]
        ti_, tiB_ = tib[n % 2], tibB[n % 2]
        a_, aB_ = ab[n % 2], abB[n % 2]
        prs = [(gi, ii) for gi, (jj, ii) in enumerate(GPAIRS) if jj == j]
        reads = [wgB] + [rstate[ii][3] for _, ii in prs]
        rhs_l = [rstate[ii][2] for _, ii in prs]
        bks = []
        for g in range(2):
            bk, bkB = nb()
            bks.append((bk, bkB))

            def f(e, g=g, bk=bk):
                for t, (gi, ii) in enumerate(prs):
                    ins = e.matmul(bk[:, :], lhsT=wg[:, g, gi, :], rhs=rhs_l[t][:], start=(t == 0), stop=(t == len(prs) - 1))
                return ins
            S.op("pe", f, reads=reads, writes=[bkB])
        (br, brB), (bi, biB) = bks
        yield
        gstate[j] = (tr_, trB_, ti_, tiB_, a_, aB_)
        S.op("act", lambda e: e.activation(out=tr_[:], in_=br[:, :], func=AF.Tanh, scale=0.5, bias=prm[:, Q_RBA + j:Q_RBA + j + 1]),
             reads=[brB, prmB], writes=[trB_])
        S.op("act", lambda e: e.activation(out=ti_[:], in_=bi[:, :], func=AF.Tanh, scale=0.5, bias=prm[:, Q_RBX + j:Q_RBX + j + 1]),
             reads=[biB, prmB], writes=[tiB_])
        S.op("act", lambda e: e.activation(out=a_[:], in_=tr_[:], func=AF.Exp, scale=prm[:, Q_HCL + j:Q_HCL + j + 1],
                                           bias=prm[:, Q_HCL + j:Q_HCL + j + 1]), reads=[trB_, prmB], writes=[aB_])
        S.op("act", lambda e: e.activation(out=tr_[:], in_=tr_[:], func=AF.Exp, scale=prm[:, Q_CL + j:Q_CL + j + 1],
                                           bias=prm[:, Q_CL + j:Q_CL + j + 1]), reads=[trB_, prmB], writes=[trB_])
        yield
        S.op("act", lambda e: e.activation(out=tr_[:], in_=tr_[:], func=AF.Sqrt, scale=-1.0 / 16, bias=prm[:, Q_C16:Q_C16 + 1]),
             reads=[trB_, prmB], writes=[trB_])
        yield

    def G_dve(i, j):
        v_, vB_, vb_, vbB_, sg_, sgB_ = rstate[j]
        tr_, trB_, ti_, tiB_, a_, aB_ = gstate[j]
        S.op("dve", lambda e: e.scalar_tensor_tensor(out=ti_[:], in0=ti_[:], scalar=1.0, in1=v_[:], op0=ALU.add, op1=ALU.mult),
             reads=[tiB_, vB_], writes=[tiB_])
        S.op("dve", lambda e: e.tensor_tensor(out=ti_[:], in0=ti_[:], in1=tr_[:], op=ALU.mult), reads=[tiB_, trB_], writes=[tiB_])
        S.op("dve", lambda e: e.tensor_tensor_scan(out=tr_[:], data0=a_[:], data1=ti_[:], initial=hst[:, j:j + 1],
                                                   op0=ALU.mult, op1=ALU.add), reads=[aB_, tiB_, hstB[j], trB_], writes=[trB_])
        S.op("dve", lambda e: e.tensor_copy(out=hst[:, j:j + 1], in_=tr_[:, TT - 1:TT]), reads=[trB_], writes=[hstB[j]])
        S.op("dve", lambda e: e.tensor_tensor(out=prg[:, j, :], in0=tr_[:], in1=sg_[:], op=ALU.mult),
             reads=[trB_, sgB_], writes=[prgB[j]])

    m_ctr = [0]
    mstate = {}
    ga_ring = [(gab[0][:], gabB[0]), (gab[1][:], gabB[1]), (ubuf[1][:], ubufB[1]), (ubuf[0][:], ubufB[0]),
               (pbuf[0][:, 2:2 + TT], pbufB[0])]
    gb_ring = [(gbb[0][:], gbbB[0]), (gbb[1][:], gbbB[1]), (gbuf[1][:], gbufB[1]), (gbuf[0][:], gbufB[0]),
               (pbuf[1][:, 2:2 + TT], pbufB[1])]

    def M_step(i, j):
        pid = P_M0 + j // 2
        ci = 2 * (j % 2)
        n = m_ctr[0]
        m_ctr[0] += 1
        ga_, gaB_ = ga_ring[j % 5]
        gb_, gbB_ = gb_ring[j % 5]
        mstate[j] = (ga_, gaB_, gb_, gbB_)
        bma, bmaB = nb(); zmm(i, pid, ci, bma, bmaB)
        bmb, bmbB = nb(); zmm(i, pid, ci + 1, bmb, bmbB)
        if j % 2 == 1:
            release("w", pid)
        S.op("act", lambda e: e.activation(out=ga_, in_=bma[:, :], func=AF.Tanh, scale=0.5, bias=prm[:, Q_BM + j:Q_BM + j + 1]),
             reads=[bmaB, prmB], writes=[gaB_])
        S.op("act", lambda e: e.activation(out=gb_, in_=bmb[:, :], func=AF.Tanh, scale=0.5, bias=prm[:, Q_BM + 8 + j:Q_BM + 8 + j + 1]),
             reads=[bmbB, prmB], writes=[gbB_])
        sp_ = P_SC0 + j // 4
        slot, rb = get("w", sp_)
        off = (j % 4) * 128
        bya, byaB = nb()

        def f(e):
            for kc in range(8):
                ins = e.matmul(bya[:, :], lhsT=ring[:, slot, kc * 512 + off:kc * 512 + off + 128], rhs=pa[:, kc, :],
                               start=(kc == 0), stop=(kc == 7))
            return ins
        S.op("pe", f, reads=[rb] + paB, writes=[byaB])
        if j % 4 == 3:
            release("w", sp_)
        S.op("dve", lambda e: e.scalar_tensor_tensor(out=ga_, in0=ga_, scalar=1.0, in1=bya[:, :], op0=ALU.add, op1=ALU.mult),
             reads=[gaB_, byaB], writes=[gaB_])

    def YB_step(i, j):
        ga_, gaB_, gb_, gbB_ = mstate[j]
        rp = P_RG0 + j // 4
        slot, rb = get("w", rp)
        off = (j % 4) * 128
        byb, bybB = nb()

        def f(e):
            for kc in range(8):
                e.matmul(byb[:, :], lhsT=ring[:, slot, kc * 512 + off:kc * 512 + off + 128], rhs=prg[:, kc, :],
                         start=(kc == 0), stop=False)
            e.matmul(byb[:, :], lhsT=rgc[:, 0, j * 128:(j + 1) * 128], rhs=prg[:, 8, :], start=False, stop=False)
            return e.matmul(byb[:, :], lhsT=rgc[:, 1, j * 128:(j + 1) * 128], rhs=prg[:, 9, :], start=False, stop=True)
        S.op("pe", f, reads=[rb, rgcB] + prgB, writes=[bybB])
        if j % 4 == 3:
            release("w", rp)
        S.op("dve", lambda e: e.scalar_tensor_tensor(out=gb_, in0=gb_, scalar=1.0, in1=byb[:, :], op0=ALU.add, op1=ALU.mult),
             reads=[gbB_, bybB], writes=[gbB_])
        S.op("pool", lambda e: e.tensor_tensor(out=merged[:, j, :], in0=ga_, in1=gb_, op=ALU.add), reads=[gaB_, gbB_],
             writes=[mergedB[j]])

    o_ctr = [0]
    t1_gen = [0, 0]
    pending_fin = []
    out_tokens = []

    def flush_fin():
        while pending_fin:
            t1_, t1B_, k, gen, col, colB, r0 = pending_fin.pop(0)
            assert t1_gen[k] == gen, "t1 slot reused before deferred finalize"
            S.op("dve", (lambda e, t1_=t1_, col=col: e.scalar_tensor_tensor(out=t1_[:], in0=t1_[:], scalar=stat[:, col:col + 1],
                                                                             in1=gfin[:], op0=ALU.mult, op1=ALU.mult)),
                 reads=[t1B_, colB, gfinB], writes=[t1B_])
            tok = S.dma("sp", (lambda e, t1_=t1_, r0=r0: e.dma_start(out=out_d[r0:r0 + 128, :], in_=t1_[:])), reads=[t1B_])
            out_tokens.append(tok)

    ol_state = {}

    def O_load(i, s):
        xk = xb_ctr[0] % NXB
        xb, xbB = next_xb()
        xb_gen[xk] += 1
        r0 = i * TT + s * 128
        S.dma("sp", lambda e: e.dma_start(out=xb[:], in_=x_d[r0:r0 + 128, :]), writes=[xbB])
        ol_state[(i, s)] = (xb, xbB, xk, xb_gen[xk])

    def O_step(i, s):
        b = i // 4
        n = o_ctr[0]
        o_ctr[0] += 1
        k = n % 2
        t1_, t1B_ = t1[k], t1B[k]
        t1_gen[k] += 1
        r0 = i * TT + s * 128
        xb, xbB, xk, xgen = ol_state.pop((i, s))
        assert xb_gen[xk] == xgen
        halves = []
        for hh in range(2):
            bk, bkB = nb()
            halves.append((bk, bkB))

            def f(e, hh=hh, bk=bk):
                for kc in range(8):
                    ins = e.matmul(bk[:, :], lhsT=merged[:, kc, s * 128:(s + 1) * 128], rhs=wout[:, kc, hh * 512:(hh + 1) * 512],
                                   start=(kc == 0), stop=(kc == 7))
                return ins
            S.op("pe", f, reads=[woutB] + mergedB, writes=[bkB])
        for hh in range(2):
            bk, bkB = halves[hh]
            S.op("dve", (lambda e, hh=hh, bk=bk: e.tensor_tensor(out=t1_[:, hh * 512:(hh + 1) * 512], in0=bk[:, :],
                                                                  in1=gatebc[:, b, hh * 512:(hh + 1) * 512], op=ALU.mult)),
                 reads=[bkB, gatebcB], writes=[t1B_])
        S.op("dve", lambda e: e.tensor_tensor(out=t1_[:], in0=t1_[:], in1=xb[:], op=ALU.add), reads=[t1B_, xbB], writes=[t1B_])
        col, colB = next_stat()
        sumsq(t1_, t1B_, col, colB, xb[:], xbB)
        rstd_from(col, colB)
        flush_fin()
        pending_fin.append((t1_, t1B_, k, t1_gen[k], col, colB, r0))

    allh = haloAB + haloRB + hstB

    def seq_reset(e):
        e.memset(haloA[:].rearrange("p j k -> p (j k)"), 0.0)
        e.memset(haloR[:].rearrange("p j k -> p (j k)"), 0.0)
        return e.memset(hst[:], 0.0)

    def fin(g):
        for _ in g:
            pass

    pump()
    prologue_a()
    for s in range(4):
        prep_norm(0, s)
    flush_norm()
    prologue_b()
    for q in range(4):
        prep_tr(0, q)

    for i in range(NT):
        if i % 4 == 0:
            S.op("dve", seq_reset, writes=allh)
        for j in range(8):
            rg_ = R_gen(i, j); next(rg_)
            fin(rg_)
            gg = None
            if j >= 2:
                gg = G_gen(i, j - 2); next(gg)
            ag = A_gen(i, j); next(ag)
            if j >= 3:
                G_dve(i, j - 3)
            if gg is not None:
                next(gg)
            next(ag)
            next(ag)
            fin(ag)
            if gg is not None:
                fin(gg)
            if i > 0 and j < 4:
                O_step(i - 1, j)
                if j < 3:
                    O_load(i - 1, j + 1)
            if j == 4:
                flush_fin()
            if i + 1 < NT and j >= 5:
                P_load(i + 1, j - 5)
        r8 = R_gen(i, 8); next(r8); fin(r8)
        g6 = G_gen(i, 6); next(g6)
        G_dve(i, 5)
        fin(g6)
        r9 = R_gen(i, 9); next(r9); fin(r9)
        g7 = G_gen(i, 7); next(g7)
        G_dve(i, 6)
        fin(g7)
        nxt = i + 1 < NT
        if nxt:
            P_rest(i + 1, 0)
        if i == 0:
            gate_bc()
        M_step(i, 0)
        if nxt:
            P_rest(i + 1, 1)
            P_load(i + 1, 3)
        g8 = G_gen(i, 8); next(g8)
        G_dve(i, 7)
        fin(g8)
        M_step(i, 1)
        if nxt:
            P_rest(i + 1, 2)
        g9 = G_gen(i, 9); next(g9)
        G_dve(i, 8)
        fin(g9)
        G_dve(i, 9)
        M_step(i, 2)
        if nxt:
            P_rest(i + 1, 3)
        M_step(i, 3)
        flush_norm()
        M_step(i, 4)
        for j in range(5, 8):
            YB_step(i, j - 5)
            M_step(i, j)
            if nxt:
                prep_tr(i + 1, j - 5)
        for j in range(3, 8):
            if j == 6:
                O_load(i, 0)
            YB_step(i, j)
            if nxt and j == 3:
                prep_tr(i + 1, 3)
    for s in range(4):
        if s < 3:
            O_load(NT - 1, s + 1)
        O_step(NT - 1, s)
    flush_fin()
    S.wait_all("sp", out_tokens)
    S.emit()
    return nc


def _kmajor(w, ncols_piece=512):
    K, N = w.shape
    kc = K // 128
    a = w.reshape(kc, 128, N // ncols_piece, ncols_piece)
    return np.ascontiguousarray(a.transpose(2, 1, 0, 3)).reshape(N // ncols_piece, 128, kc * ncols_piece)


def _host_layout(inp):
    f = np.float32
    w_in = np.asarray(inp["w_in"][0], f)
    cols = np.concatenate([np.arange(c, c + 128) for pc in piece_columns() for c in pc])
    w_in_r = w_in[:, cols]
    wall = np.zeros((NPIECE, 128, 4096), f)
    wall[0:17] = _kmajor(w_in_r)
    wall[17:19] = _kmajor(np.asarray(inp["sc_w_out"][0], f))
    rgw = np.asarray(inp["rg_w_out"][0], f)
    wall[19:21] = _kmajor(rgw[0:1024])
    rgc = np.ascontiguousarray(rgw[1024:1280].reshape(2, 128, 1024).transpose(1, 0, 2)).reshape(128, 2048)
    wada = _kmajor(np.asarray(inp["w_ada"][0], f))
    wo = np.asarray(inp["w_out"][0], f)
    wout = np.ascontiguousarray(wo.reshape(8, 128, 1024).transpose(1, 0, 2)).reshape(128, 8192)
    wg = np.zeros((128, 2, NGP, 128), f)
    for g, key in enumerate(["rg_w_a", "rg_w_x"]):
        wh = np.asarray(inp[key][0], f)
        full = np.zeros((RGW, RGW), f)
        for h in range(16):
            full[HD * h:HD * h + HD, HD * h:HD * h + HD] = wh[h]
        for gi, (j, i) in enumerate(GPAIRS):
            wg[:, g, gi, :] = full[128 * i:128 * i + 128, 128 * j:128 * j + 128]
    wg = wg.reshape(128, 2 * NGP * 128)
    cst = np.zeros((128, NCST), f)
    cst[:, C_SCW:C_SCW + 24] = np.asarray(inp["sc_conv_w"][0], f).reshape(3, 8, 128).transpose(2, 1, 0).reshape(128, 24)
    cst[:, C_SCB:C_SCB + 8] = np.asarray(inp["sc_conv_b"][0], f).reshape(8, 128).T
    cst[:, C_RGW:C_RGW + 40] = np.asarray(inp["rg_conv_w"][0], f).reshape(4, 10, 128).transpose(2, 1, 0).reshape(128, 40)
    cst[:, C_RGB:C_RGB + 10] = np.asarray(inp["rg_conv_b"][0], f).reshape(10, 128).T
    cst[:, C_RBA:C_RBA + 10] = np.asarray(inp["rg_b_a"][0], f).reshape(10, 128).T
    cst[:, C_RBX:C_RBX + 10] = np.asarray(inp["rg_b_x"][0], f).reshape(10, 128).T
    cst[:, C_LAM:C_LAM + 10] = np.asarray(inp["rg_lambda"][0], f).reshape(10, 128).T
    cst[:, C_BM:C_BM + 16] = np.asarray(inp["b_merge"][0], f).reshape(2, 8, 128).transpose(2, 0, 1).reshape(128, 16)
    cst[:, C_GN:C_GN + 8] = np.asarray(inp["g_norm"][0], f).reshape(8, 128).T
    bada = np.asarray(inp["b_ada"][0], f)
    cst[:, C_BADA:C_BADA + 32] = np.repeat(bada[0:2048].reshape(16, 128).T[:, :, None], 2, axis=2).reshape(128, 32)
    rows = np.zeros((128, 2048), f)
    rows[:, 0:1024] = np.asarray(inp["g_final"], f)[None, :]
    rows[:, 1024:2048] = bada[None, 2048:3072]
    def padded(a, run):
        lead = a.shape[:-1]
        a = a.reshape(*lead, a.shape[-1] // run, run)
        out = np.zeros((*lead, a.shape[-2], run + PAD), f)
        out[..., :run] = a
        return out
    return dict(wall=padded(wall, 1024), wada=padded(wada, 1024), rgc=padded(rgc, 1024), wout=padded(wout, 1024),
                wg=padded(wg, 512), cst=cst, rows=rows)


_NC_CACHE = {}


def kernel(**inputs):
    x = np.asarray(inputs["x"], np.float32)
    c = np.asarray(inputs["c"], np.float32)
    shared = _host_layout(inputs)
    if "nc" not in _NC_CACHE:
        _NC_CACHE["nc"] = build_nc()
    nc = _NC_CACHE["nc"]
    in_maps = []
    for core in range(NCORES):
        xs = np.ascontiguousarray(x[2 * core:2 * core + 2].reshape(NT * TT, D))
        cc = c[2 * core:2 * core + 2]
        ct = np.ascontiguousarray(cc.T.reshape(8, 128, 2).transpose(1, 0, 2)).reshape(128, 16)
        m = dict(shared)
        m["x"] = xs
        m["ct"] = ct
        in_maps.append(m)
    res = run_bass_kernel_spmd(nc, in_maps, core_ids=list(range(NCORES)))
    out = np.stack([np.asarray(r["out"], np.float32).reshape(2, SEQ, D) for r in res.results], axis=0)
    return out.reshape(16, SEQ, D)
```
